# Optimizing a Trainium2 kernel written in Bass

```python
import math
import jax
import jax.numpy as jnp
from jax import lax
import numpy as np


D_MODEL = 1024
BATCH = 8
SEQ = 2048
DEPTH = 2

CHUNK = 64
Q_BLOCK = 128
BRANCH_WIDTH = D_MODEL // 2
SB_HEADS = 8
SB_HEAD_DIM = BRANCH_WIDTH // SB_HEADS
POOL_WINDOWS = (2, 4, 8, 16)
POOL_GROUPS = len(POOL_WINDOWS)
POOL_GROUP_DIM = BRANCH_WIDTH // POOL_GROUPS
HGRN_EXPAND = 128
HGRN_HEADS = BRANCH_WIDTH // HGRN_EXPAND
HGRN_HEAD_DIM = BRANCH_WIDTH // HGRN_HEADS
N_BRANCH = 3
EPS = 1e-6
IN_SIZES = (BRANCH_WIDTH,) * 10 + (D_MODEL,) * N_BRANCH
IN_COLS = sum(IN_SIZES)

kernel_name = "hybrid_stickbreak_pool_hgrn2_block"


def _rmsnorm(x, g):
    xf = x.astype(jnp.float32)
    return xf * lax.rsqrt(jnp.mean(xf * xf, axis=-1, keepdims=True) + EPS) * g.astype(jnp.float32)


def _stick_breaking(q, k, v):
    seq = q.shape[1]
    scale = 1.0 / math.sqrt(SB_HEAD_DIM)
    outs = []
    for blk in range(seq // Q_BLOCK):
        qs, qe = blk * Q_BLOCK, (blk + 1) * Q_BLOCK
        z = jnp.einsum('bqhd,bkhd->bhqk', q[:, qs:qe], k[:, :qe]) * scale
        qpos = jnp.arange(qs, qe)[:, None]
        kpos = jnp.arange(qe)[None, :]
        mask = kpos < qpos
        log_1mb = jnp.where(mask, jax.nn.log_sigmoid(-z), 0.0)
        cum = jnp.cumsum(log_1mb, axis=-1)
        rem = cum[..., -1:] - cum
        a = jnp.where(mask, jnp.exp(jax.nn.log_sigmoid(z) + rem), 0.0)
        outs.append(jnp.einsum('bhqk,bkhd->bqhd', a, v[:, :qe]))
    return jnp.concatenate(outs, axis=1)


def _multiscale_pool(u, w_grp, scale):
    bsz, seq, _ = u.shape
    c = jnp.cumsum(u, axis=1)
    c = jnp.concatenate([jnp.zeros_like(c[:, :1]), c], axis=1)
    cg = c.reshape(bsz, seq + 1, POOL_GROUPS, POOL_GROUP_DIM)
    ug = u.reshape(bsz, seq, POOL_GROUPS, POOL_GROUP_DIM)
    pos = jnp.arange(seq)
    diffs = []
    for g, w in enumerate(POOL_WINDOWS):
        start = jnp.maximum(pos + 1 - w, 0)
        cnt = (pos + 1 - start).astype(jnp.float32)
        mean = (cg[:, 1:, g] - cg[:, start, g]) / cnt[None, :, None]
        diffs.append(mean - ug[:, :, g])
    d = jnp.stack(diffs, axis=2)
    y = jnp.einsum('bsgc,gcd->bsgd', d, w_grp.astype(jnp.float32))
    return y.reshape(bsz, seq, BRANCH_WIDTH) * scale.astype(jnp.float32)


def _hgrn2(q, f_logit, i, lb):
    bsz, seq, _ = q.shape
    n_chunks = seq // CHUNK
    f = lb + (1.0 - lb) * jax.nn.sigmoid(f_logit)
    log_f = jnp.log(f)
    k = 1.0 - f

    def to_chunks(t):
        return t.reshape(bsz, n_chunks, CHUNK, HGRN_HEADS, HGRN_HEAD_DIM).transpose(1, 0, 3, 2, 4)

    causal = jnp.tril(jnp.ones((CHUNK, CHUNK), dtype=bool))[None, None, :, :, None]

    def step(state, inp):
        qc, kc, vc, lfc = inp
        b = jnp.cumsum(lfc, axis=2)
        diff = b[:, :, :, None, :] - b[:, :, None, :, :]
        decay = jnp.exp(jnp.where(causal, diff, -jnp.inf))
        attn = jnp.einsum('bhrsd,bhsd->bhrs', qc[:, :, :, None, :] * decay, kc)
        o = (jnp.einsum('bhrs,bhsv->bhrv', attn, vc)
             + jnp.einsum('bhrd,bhdv->bhrv', qc * jnp.exp(b), state))
        b_last = b[:, :, -1:, :]
        new_state = (jnp.exp(b_last[:, :, 0, :])[..., None] * state
                     + jnp.einsum('bhsd,bhsv->bhdv', kc * jnp.exp(b_last - b), vc))
        return new_state, o

    state0 = jnp.zeros((bsz, HGRN_HEADS, HGRN_HEAD_DIM, HGRN_HEAD_DIM), jnp.float32)
    _, o = lax.scan(step, state0, (to_chunks(q), to_chunks(k), to_chunks(i), to_chunks(log_f)))
    return o.transpose(1, 0, 3, 2, 4).reshape(bsz, seq, HGRN_HEADS, HGRN_HEAD_DIM)


def setup_inputs(seed: int = 0) -> dict:
    key = jax.random.key(seed)
    ks = jax.random.split(key, 12)
    f32 = jnp.float32
    x = jax.random.normal(ks[0], (BATCH, SEQ, D_MODEL), f32)
    norm_g = 1.0 + 0.02 * jax.random.normal(ks[1], (DEPTH, D_MODEL), f32)
    w_in = jax.random.normal(ks[2], (DEPTH, D_MODEL, IN_COLS), f32) * D_MODEL ** -0.5
    pool_w = jax.random.normal(ks[3], (DEPTH, POOL_GROUPS, POOL_GROUP_DIM, POOL_GROUP_DIM), f32) * POOL_GROUP_DIM ** -0.5
    pool_scale = 1.0 + 0.02 * jax.random.normal(ks[4], (DEPTH, BRANCH_WIDTH), f32)
    hgrn_lb = 1.0 + 0.1 * jax.random.normal(ks[5], (DEPTH, BRANCH_WIDTH), f32)
    hgrn_norm_g = 1.0 + 0.02 * jax.random.normal(ks[6], (DEPTH, BRANCH_WIDTH), f32)
    w_branch = jax.random.normal(ks[7], (DEPTH, N_BRANCH, BRANCH_WIDTH, D_MODEL), f32) * BRANCH_WIDTH ** -0.5
    w_out = jax.random.normal(ks[8], (DEPTH, D_MODEL, D_MODEL), f32) * D_MODEL ** -0.5
    final_g = 1.0 + 0.02 * jax.random.normal(ks[9], (D_MODEL,), f32)
    return {"x": x, "norm_g": norm_g, "w_in": w_in, "pool_w": pool_w,
            "pool_scale": pool_scale, "hgrn_lb": hgrn_lb, "hgrn_norm_g": hgrn_norm_g,
            "w_branch": w_branch, "w_out": w_out, "final_g": final_g}


def reference(x, norm_g, w_in, pool_w, pool_scale, hgrn_lb, hgrn_norm_g, w_branch, w_out, final_g):
    bsz, seq, _ = x.shape
    h_res = x.astype(jnp.float32)
    lb_all = jnp.cumsum(jax.nn.softmax(hgrn_lb.astype(jnp.float32), axis=0), axis=0)
    lb_all = lb_all - lb_all[:1]
    split_idx = list(np.cumsum(IN_SIZES)[:-1])
    for layer in range(DEPTH):
        h = _rmsnorm(h_res, norm_g[layer])
        proj = h @ w_in[layer].astype(jnp.float32)
        (sb_q, sb_k, sb_v, sb_z, pool_u, pool_z, hg_q, hg_f, hg_i, hg_z,
         gate_a, gate_b, gate_c) = jnp.split(proj, split_idx, axis=-1)

        hs = (bsz, seq, SB_HEADS, SB_HEAD_DIM)
        o_a = _stick_breaking(sb_q.reshape(hs), sb_k.reshape(hs), sb_v.reshape(hs))
        o_a = o_a.reshape(bsz, seq, BRANCH_WIDTH) * jax.nn.silu(sb_z)

        o_b = _multiscale_pool(pool_u, pool_w[layer], pool_scale[layer]) * jax.nn.silu(pool_z)

        o_c = _hgrn2(hg_q, hg_f, hg_i, lb_all[layer])
        o_c = _rmsnorm(o_c, hgrn_norm_g[layer].reshape(HGRN_HEADS, HGRN_HEAD_DIM))
        o_c = o_c.reshape(bsz, seq, BRANCH_WIDTH) * jax.nn.silu(hg_z)

        wb = w_branch[layer].astype(jnp.float32)
        merged = (jax.nn.sigmoid(gate_a) * (o_a @ wb[0])
                  + jax.nn.sigmoid(gate_b) * (o_b @ wb[1])
                  + jax.nn.sigmoid(gate_c) * (o_c @ wb[2]))
        h_res = h_res + merged @ w_out[layer].astype(jnp.float32)
    return _rmsnorm(h_res, final_g).astype(x.dtype)
```

```python
import numpy as np
from contextlib import ExitStack
import concourse.bass as bass
import concourse.mybir as mybir
from concourse.bass_utils import run_bass_kernel_spmd

F32 = mybir.dt.float32
BF16 = mybir.dt.bfloat16
AF = mybir.ActivationFunctionType
ALU = mybir.AluOpType

SEQ = 2048
DM = 1024
DEPTH = 2
NCORES = 8
EPS = 1e-6
NGRP = 21
NSLOT = 5
CW = 4 * 128 + 512 + 16

ENGS = ("pe", "act", "dve", "pool", "sp")
SEM_CHUNK = 4000
DMA_CHUNK = 1000


class Res:
    __slots__ = ("name", "psum", "last_w", "readers", "strict")

    def __init__(self, name, psum=False, strict=False):
        self.name = name
        self.psum = psum
        self.last_w = None
        self.readers = []
        self.strict = strict


class Op:
    __slots__ = ("eng", "fn", "deps", "signal", "sig_idx", "chan", "chan_idx", "name")

    def __init__(self, eng, fn, chan=None, name=""):
        self.eng = eng
        self.fn = fn
        self.deps = []
        self.signal = False
        self.sig_idx = -1
        self.chan = chan
        self.chan_idx = -1
        self.name = name


class Chan:
    __slots__ = ("name", "n", "sems", "last", "serial")

    def __init__(self, name, serial=True):
        self.name = name
        self.n = 0
        self.sems = []
        self.last = None
        self.serial = serial


class Sched:
    def __init__(self, nc):
        self.nc = nc
        self.ops = {e: [] for e in ENGS}
        self.chans = []
        self.last = {e: None for e in ENGS}
        self.pending_barrier = {e: None for e in ENGS}

    def chan(self, name, serial=True):
        c = Chan(name, serial)
        self.chans.append(c)
        return c

    def add(self, eng, fn, reads=(), writes=(), chan=None, name=""):
        op = Op(eng, fn, chan, name)
        raw = set()
        other = set()
        strict = set()
        for r in reads:
            if r.last_w is not None:
                raw.add(r.last_w)
            if r.psum:
                for o in r.readers:
                    if o.eng != eng:
                        other.add(o)
            r.readers.append(op)
        for w in writes:
            if w.last_w is not None:
                other.add(w.last_w)
                if w.strict:
                    strict.add(w.last_w)
            for o in w.readers:
                if o is not op:
                    other.add(o)
                    if w.strict:
                        strict.add(o)
            w.last_w = op
            w.readers = []
        pb = self.pending_barrier[eng]
        if pb is not None:
            for o in pb:
                raw.add(o)
            self.pending_barrier[eng] = None
        if chan is not None and chan.serial and chan.last is not None:
            raw.add(chan.last)
        deps = []
        seen = set()
        for d in list(raw | other):
            if d is op:
                continue
            israw = d in raw
            if d.chan is not None and not d.chan.serial:
                d = d.chan.last
            if id(d) in seen:
                continue
            seen.add(id(d))
            if d.chan is None and d.eng == eng:
                if eng == "pe" or eng == "sp":
                    continue
                if not israw and d not in strict:
                    continue
            deps.append(d)
        op.deps = deps
        for d in deps:
            d.signal = True
        if chan is not None:
            op.chan_idx = chan.n
            chan.n += 1
            chan.last = op
        self.ops[eng].append(op)
        self.last[eng] = op
        return op

    def pe(self, fn, reads=(), writes=()):
        return self.add("pe", fn, reads, writes)

    def act(self, fn, reads=(), writes=()):
        return self.add("act", fn, reads, writes)

    def dve(self, fn, reads=(), writes=()):
        return self.add("dve", fn, reads, writes)

    def pool(self, fn, reads=(), writes=()):
        return self.add("pool", fn, reads, writes)

    def dma(self, eng, chan, fn, reads=(), writes=()):
        return self.add(eng, fn, reads, writes, chan=chan)

    def frontier(self):
        fr = [o for o in self.last.values() if o is not None]
        fr += [c.last for c in self.chans if c.last is not None]
        return fr

    def barrier(self, fr=None):
        if fr is None:
            fr = self.frontier()
        for e in ENGS:
            cur = self.pending_barrier[e]
            self.pending_barrier[e] = list(fr) + (cur if cur else [])

    def emit(self, final_chans=()):
        nc = self.nc
        with ExitStack() as es:
            eng_sems = {}
            for e in ENGS:
                j = 0
                for op in self.ops[e]:
                    if op.chan is None and op.signal:
                        op.sig_idx = j
                        j += 1
                nsem = (j + SEM_CHUNK - 1) // SEM_CHUNK
                eng_sems[e] = [es.enter_context(nc.semaphore(f"s_{e}_{i}")) for i in range(nsem)]
            for ci, c in enumerate(self.chans):
                nsem = (c.n + DMA_CHUNK - 1) // DMA_CHUNK
                c.sems = [es.enter_context(nc.semaphore(f"c_{ci}_{i}")) for i in range(nsem)]

            def sem_of(op):
                if op.chan is not None:
                    return (op.chan.sems[op.chan_idx // DMA_CHUNK],
                            16 * (op.chan_idx % DMA_CHUNK + 1))
                return (eng_sems[op.eng][op.sig_idx // SEM_CHUNK],
                        op.sig_idx % SEM_CHUNK + 1)

            block = es.enter_context(nc.Block())
            handles = {"pe": block.tensor, "act": block.scalar, "dve": block.vector,
                       "pool": block.gpsimd, "sp": block.sync}

            def make(e):
                def body(eng):
                    waited = {}
                    for op in self.ops[e]:
                        need = {}
                        for d in op.deps:
                            s, v = sem_of(d)
                            k = id(s)
                            if waited.get(k, 0) >= v:
                                continue
                            if k not in need or need[k][1] < v:
                                need[k] = (s, v)
                        for k, (s, v) in need.items():
                            eng.wait_ge(s, v)
                            waited[k] = v
                        ins = op.fn(eng)
                        if op.chan is not None:
                            s, _ = sem_of(op)
                            ins.then_inc(s, 16)
                        elif op.signal:
                            s, _ = sem_of(op)
                            ins.then_inc(s, 1)
                    if e == "sp":
                        for c in final_chans:
                            if c.last is not None:
                                s, v = sem_of(c.last)
                                eng.wait_ge(s, v)
                return body

            for e in ENGS:
                handles[e](make(e))


def ts(i, n):
    return slice(i * n, (i + 1) * n)


def build_program(layers=(0, 1), first=True, last=True, dbg=None, stop_after=None):
    nc = bass.Bass("TRN2", target_bir_lowering=False)
    x_in = nc.dram_tensor("x", [SEQ, DM], F32, kind="ExternalInput").ap()
    wgrp = nc.dram_tensor("wgrp", [DEPTH * NGRP, 128, 4096], F32, kind="ExternalInput").ap()
    poolw_d = nc.dram_tensor("poolw", [DEPTH, 128, 512], F32, kind="ExternalInput").ap()
    vecs_d = nc.dram_tensor("vecs", [128, DEPTH * 12], F32, kind="ExternalInput").ap()
    gb_d = nc.dram_tensor("gb", [DEPTH + 1, 128, DM], F32, kind="ExternalInput").ap()
    cst_d = nc.dram_tensor("cst", [128, CW], F32, kind="ExternalInput").ap()
    out_d = nc.dram_tensor("out", [SEQ, DM], F32, kind="ExternalOutput").ap()
    hscr = nc.dram_tensor("hscr", [SEQ, DM], F32, kind="Internal").ap()
    dbg_out = {}
    if dbg:
        for name, shape in dbg.items():
            dbg_out[name] = nc.dram_tensor("dbg_" + name, list(shape), F32, kind="ExternalOutput").ap()

    S = Sched(nc)
    es = ExitStack()

    def sb(name, shape, dt):
        return es.enter_context(nc.sbuf_tensor(name, shape, dt))

    cf = sb("cf", [128, CW], F32)
    cb = sb("cb", [128, 6, 128], BF16)
    mi4 = sb("mi4", [128, 512], BF16)
    vecs = sb("vecs_s", [128, DEPTH * 12], F32)
    lbv = sb("lbv", [128, DEPTH * 4], F32)
    l1m = sb("l1m", [128, DEPTH * 4], F32)
    lbt = sb("lbt", [128, 16], F32)
    hT = sb("hT", [128, 8, SEQ], BF16)
    mg = sb("mg", [128, 8, SEQ], BF16)
    obT = sb("obT", [128, 4, SEQ], BF16)
    wsl = [sb(f"wsl{i}", [128, 4096], BF16) for i in range(NSLOT)]
    poolw = sb("poolw_s", [128, 512], BF16)
    NF = 9760
    NB = 16896
    arf = sb("arf", [128, NF], F32)
    mt = sb("mt", [128, 2048], F32)
    arb = sb("arb", [128, NB], BF16)
    ps = es.enter_context(nc.psum_tensor("ps", [128, 8 * 512], F32))

    def bank(i):
        return ps[:, i * 512:(i + 1) * 512]

    PB = [Res(f"pb{i}", psum=True) for i in range(8)]
    R_cf = Res("cf"); R_cb = Res("cb"); R_vecs = Res("vecs"); R_lb = Res("lb")
    R_hT = [Res(f"hT{tg}") for tg in range(4)]
    R_mg = [[Res(f"mg{dc}_{tg}") for tg in range(4)] for dc in range(8)]
    R_ob = [[Res(f"ob{c}_{tg}") for tg in range(4)] for c in range(4)]
    R_ws = [Res(f"ws{i}") for i in range(NSLOT)]
    R_pw = Res("poolw")
    R_mt = [Res(f"mt{i}", strict=True) for i in range(4)]
    ch_ws = [S.chan(f"ws{i}") for i in range(NSLOT)]
    ch_misc = S.chan("misc")
    ch_pw = S.chan("pw")
    ch_out = S.chan("out", serial=False)
    ch_dbg = S.chan("dbg", serial=False)

    ident = cf[:, 0:128]
    TRI, ONESN8, MASKS, ONES, MASKI = 0, 1, 2, 3, 4
    rmask = cf[:, 512:1024]
    invcnt = cf[:, 1024:1040]

    class Arena:
        def __init__(self):
            self.f = 0
            self.b = 0

        def reset(self):
            self.f = 0
            self.b = 0

        def F(self, n, name):
            a = arf[:, self.f:self.f + n]
            self.f += n
            assert self.f <= NF, (name, self.f)
            return a, Res(name)

        def B(self, n, name):
            a = arb[:, self.b:self.b + n]
            self.b += n
            assert self.b <= NB, (name, self.b)
            return a, Res(name)

    AR = Arena()

    glist = [(l, g) for l in layers for g in range(NGRP)]
    gstate = {"next": 0}

    def issue_group(after=()):
        n = gstate["next"]
        if n >= len(glist):
            return
        l, g = glist[n]
        slot = n % NSLOT
        S.dma("pool", ch_ws[slot],
              lambda e, l=l, g=g, slot=slot: e.dma_start(out=wsl[slot][:], in_=wgrp[l * NGRP + g]),
              reads=list(after), writes=[R_ws[slot]])
        gstate["next"] = n + 1

    gpos = {"cur": 0}

    def take_group():
        n = gpos["cur"]
        gpos["cur"] = n + 1
        return n % NSLOT

    def release_group():
        issue_group()

    def w8(slot):
        return wsl[slot][:].rearrange("p (k c) -> p k c", k=8)

    def w4(slot):
        return wsl[slot][:].rearrange("p (k c) -> p k c", k=4)

    bank_rr = {"i": 0}

    def nb(lo=0, hi=8):
        i = bank_rr["i"]
        bank_rr["i"] = (i + 1) % (hi - lo)
        return lo + i % (hi - lo)

    def proj_fm(slot, c0, tg, b, nk=8):
        wv = w8(slot)
        for kc in range(nk):
            S.pe(lambda e, kc=kc: e.matmul(bank(b), lhsT=wv[:, kc, c0:c0 + 128], rhs=hT[:, kc, ts(tg, 512)],
                                           start=(kc == 0), stop=(kc == nk - 1)),
                 reads=[R_ws[slot], R_hT[tg]], writes=[PB[b]])

    def dump(name, ap, res):
        if dbg and name in dbg:
            S.dma("pool", ch_dbg, lambda e: e.dma_start(out=dbg_out[name], in_=ap), reads=res)

    S.dma("sp", ch_misc, lambda e: e.dma_start(out=cf[:], in_=cst_d), writes=[R_cf])
    S.dma("sp", ch_misc, lambda e: e.dma_start(out=vecs[:], in_=vecs_d), writes=[R_vecs])
    for _ in range(2):
        issue_group()
    for i, c0_ in ((0, 128), (2, 256), (4, 384)):
        S.dve(lambda e, i=i, c0_=c0_: e.tensor_copy(out=cb[:, i, :], in_=cf[:, c0_:c0_ + 128]), reads=[R_cf], writes=[R_cb])
    S.dve(lambda e: e.memset(cb[:, 1, :], -8.0), writes=[R_cb])
    S.dve(lambda e: e.memset(cb[:, 3, :], 1.0), writes=[R_cb])
    for i in range(4):
        S.dve(lambda e, i=i: e.tensor_copy(out=mi4[:, ts(i, 128)], in_=cf[:, 384:512]), reads=[R_cf], writes=[R_cb])
    S.dve(lambda e: e.tensor_copy(out=cb[:, 5, :], in_=cf[:, 0:128]), reads=[R_cf], writes=[R_cb])
    cbi = cb[:, 5, :]
    V_PS, V_LB, V_HG = 0, 4, 8
    S.dve(lambda e: e.memset(lbv[:], 0.0), writes=[R_lb])
    S.dve(lambda e: e.memset(l1m[:], 0.0), writes=[R_lb])
    S.dve(lambda e: e.tensor_tensor(out=lbt[:, 0:4], in0=vecs[:, V_LB:V_LB + 4], in1=vecs[:, 12 + V_LB:12 + V_LB + 4],
                                    op=ALU.subtract), reads=[R_vecs], writes=[R_lb])
    S.act(lambda e: e.activation(out=lbt[:, 4:8], in_=lbt[:, 0:4], func=AF.Exp), reads=[R_lb], writes=[R_lb])
    S.dve(lambda e: e.tensor_scalar(out=lbt[:, 8:12], in0=lbt[:, 4:8], scalar1=1.0, scalar2=None, op0=ALU.add), reads=[R_lb], writes=[R_lb])
    S.dve(lambda e: e.reciprocal(out=lbv[:, 4:8], in_=lbt[:, 8:12]), reads=[R_lb], writes=[R_lb])
    S.dve(lambda e: e.tensor_tensor(out=lbt[:, 12:16], in0=lbt[:, 4:8], in1=lbv[:, 4:8], op=ALU.mult),
          reads=[R_lb], writes=[R_lb])
    S.act(lambda e: e.activation(out=l1m[:, 4:8], in_=lbt[:, 12:16], func=AF.Ln), reads=[R_lb], writes=[R_lb])

    n_layers = len(layers)
    chx_box = {}

    def layer_body(li, L):
        is_first_layer = (li == 0)
        is_last_layer = (li == n_layers - 1)
        src = x_in if (is_first_layer and first) else hscr
        if is_first_layer and not first:
            src = x_in
        VO = L * 12

        mtmp = {"th": [(mt[:, 512 * i:512 * (i + 1)], R_mt[i]) for i in range(2)],
                "tm": [(mt[:, 1024 + 512 * i:1024 + 512 * (i + 1)], R_mt[2 + i]) for i in range(2)]}
        AR.reset()
        do_p0 = (li == 0)
        xt = [AR.F(1024, f"xt{i}") for i in range(2)]
        xn = [AR.F(1024, f"xn{i}") for i in range(2)]
        sq, R_sq = AR.F(1024, "sq")
        gbuf, R_gb = AR.F(1024, "gbuf")
        st, _ = AR.F(64, "st")
        R_stt = [Res(f"st{i}") for i in range(16)]
        if li == 0:
            chx_box["c"] = [S.chan("x0"), S.chan("x1"), S.chan("x2"), S.chan("x3")]
        ch_x = chx_box["c"]
        if do_p0:
            S.dma("sp", ch_misc, lambda e, L=L: e.dma_start(out=gbuf, in_=gb_d[L]), writes=[R_gb])
            S.dma("pool", ch_pw, lambda e, L=L: e.dma_start(out=poolw[:], in_=poolw_d[L]), writes=[R_pw])
        def p0_front(tt):
            k = tt % 2
            xa, xr = xt[k]
            na, nr = xn[k]
            S.dma("sp", ch_x[k], lambda e: e.dma_start(out=xa, in_=src[ts(tt, 128), :]), writes=[xr])
            S.act(lambda e: e.activation(out=sq, in_=xa, func=AF.Square, accum_out=st[:, tt:tt + 1]),
                  reads=[xr], writes=[R_sq, R_stt[tt]])
            S.act(lambda e: e.activation(out=st[:, 16 + tt:17 + tt], in_=st[:, tt:tt + 1], func=AF.Ln,
                                         scale=1.0 / DM, bias=EPS), reads=[R_stt[tt]], writes=[R_stt[tt]])
            S.act(lambda e: e.activation(out=st[:, 32 + tt:33 + tt], in_=st[:, 16 + tt:17 + tt], func=AF.Exp,
                                         scale=-0.5), reads=[R_stt[tt]], writes=[R_stt[tt]])
            S.dve(lambda e: e.scalar_tensor_tensor(out=na, in0=xa, scalar=st[:, 32 + tt:33 + tt],
                                                   in1=gbuf, op0=ALU.mult, op1=ALU.mult),
                  reads=[xr, R_stt[tt], R_gb], writes=[nr])

        def p0_back(tt):
            na, nr = xn[tt % 2]
            for half in range(2):
                b = nb(0, 4)
                for j in range(4):
                    dc = half * 4 + j
                    S.pe(lambda e, b=b, j=j, dc=dc: e.transpose(out=bank(b)[:, ts(j, 128)], in_=na[:, ts(dc, 128)],
                                                               identity=ident),
                         reads=[nr, R_cf], writes=[PB[b]])
                S.dve(lambda e, b=b, half=half: e.tensor_copy(
                    out=hT[:, half * 4:half * 4 + 4, ts(tt, 128)],
                    in_=bank(b).rearrange("p (j t) -> p j t", j=4)), reads=[PB[b]], writes=[R_hT[tt // 4]])

        for tt in range(17 if do_p0 else 0):
            if tt < 16:
                p0_front(tt)
            if tt >= 1:
                p0_back(tt - 1)
        if li == 0:
            for _ in range(NSLOT - 2):
                issue_group(after=[xt[1][1]] if do_p0 else ())
        dump("hT", hT[:], [r for r in R_hT])
        if stop_after == "p0":
            return True

        fr_pre = S.frontier()
        AR.reset()
        spb = [AR.B(1024, f"sp{i}") for i in range(3)]
        racc, R_racc = AR.B(1024, "racc")
        Ab = [AR.B(1024, f"A{i}") for i in range(3)]
        eb = [AR.F(1024, f"e{i}") for i in range(2)]
        mgflat = mg[:].rearrange("p a b -> p (a b)")
        R_mgall = [r for rr in R_mg for r in rr]
        bsets = []
        for si in range(2):
            d = {}
            for k_, nm in enumerate(("qT", "kT", "vS", "zs")):
                if si == 0:
                    d[nm], d["R_" + nm] = AR.B(2048, f"{nm}0")
                else:
                    d[nm] = mgflat[:, k_ * 2048:(k_ + 1) * 2048]
                    d["R_" + nm] = Res(f"{nm}1", strict=True)
            d["vS3"] = d["vS"].rearrange("p (t c) -> p t c", t=16)
            bsets.append(d)
        IPB = 7
        OB = 6

        def inproj_units(hp, bs, banks=(IPB,)):
            slot = take_group()
            wv = w8(slot)
            units = []
            bsel = {"i": 0}

            def nbk():
                bsel["i"] += 1
                return banks[bsel["i"] % len(banks)]

            def uq(tg, c0, dst, rdst):
                def f():
                    bk = nbk()
                    proj_fm(slot, c0, tg, bk)
                    S.dve(lambda e: e.tensor_copy(out=dst[:, ts(tg, 512)], in_=bank(bk)), reads=[PB[bk]], writes=[rdst])
                return f

            def uv(t4):
                def f():
                    bk = nbk()
                    for j in range(4):
                        tt = t4 * 4 + j
                        for kc in range(8):
                            S.pe(lambda e, j=j, tt=tt, kc=kc: e.matmul(
                                bank(bk)[:, ts(j, 128)], lhsT=hT[:, kc, ts(tt, 128)], rhs=wv[:, kc, 256:384],
                                start=(kc == 0), stop=(kc == 7)), reads=[R_ws[slot], R_hT[t4]], writes=[PB[bk]])
                    S.dve(lambda e: e.tensor_copy(out=bs["vS3"][:, t4 * 4:t4 * 4 + 4, :],
                                                  in_=bank(bk).rearrange("p (j c) -> p j c", j=4)),
                          reads=[PB[bk]], writes=[bs["R_vS"]])
                return f

            def uz():
                for tg in range(4):
                    bk = nbk()
                    proj_fm(slot, 384, tg, bk)
                    S.act(lambda e, tg=tg, bk=bk: e.activation(out=bs["zs"][:, ts(tg, 512)], in_=bank(bk), func=AF.Silu),
                          reads=[PB[bk]], writes=[bs["R_zs"]])
                release_group()

            for tg in range(4):
                units.append(uq(tg, 0, bs["qT"], bs["R_qT"]))
            for tg in range(4):
                units.append(uq(tg, 128, bs["kT"], bs["R_kT"]))
            for t4 in range(4):
                units.append(uv(t4))
            units.append(uz)
            return units

        def h3(a_):
            return a_.rearrange("p (h c) -> p h c", h=2)

        def pair(sl):
            return ps[:, (2 * sl) * 512:(2 * sl + 2) * 512].rearrange("p (h c) -> p h c", h=2)

        mask2 = cb[:, MASKS:MASKS + 1, :].broadcast_to([128, 2, 128])
        racc3 = h3(racc)
        items = [(QS, kb) for QS in range(4) for kb in range(4 * (QS + 1) - 1, -1, -1)]
        n_it = len(items)

        def attn_pair(hp, bs, filler):
            qT, kT, vS3, zs = bs["qT"], bs["kT"], bs["vS3"], bs["zs"]
            R_q, R_k, R_v, R_z = bs["R_qT"], bs["R_kT"], bs["R_vS"], bs["R_zs"]

            def meta(i):
                QS, kb = items[i]
                nkb = 4 * (QS + 1)
                j = kb - 4 * QS
                c0 = 128 * j if j >= 0 else 0
                return QS, kb, nkb, c0, (j >= 0), i % 3

            def stA(i):
                QS, kb, nkb, c0, diag, sl = meta(i)
                q0 = QS * 512
                for h in range(2):
                    hs = slice(64 * h, 64 * h + 64)
                    S.pe(lambda e, h=h, hs=hs: e.matmul(
                        bank(2 * sl + h)[:, c0:512], lhsT=kT[hs, ts(kb, 128)], rhs=qT[hs, q0 + c0:q0 + 512],
                        start=True, stop=True), reads=[R_k, R_q], writes=[PB[2 * sl + h]])

            def stB(i):
                QS, kb, nkb, c0, diag, sl = meta(i)
                ea, er = eb[i % 2]
                spa, spr = spb[i % 3]
                S.act(lambda e: e.activation(out=h3(ea)[:, :, c0:512], in_=pair(sl)[:, :, c0:512], func=AF.Exp, scale=0.125),
                      reads=[PB[2 * sl], PB[2 * sl + 1]], writes=[er])
                S.act(lambda e: e.activation(out=h3(spa)[:, :, c0:512], in_=h3(ea)[:, :, c0:512], func=AF.Ln, bias=1.0, scale=1.0),
                      reads=[er], writes=[spr])
                if diag:
                    S.dve(lambda e: e.tensor_tensor(out=h3(spa)[:, :, c0:c0 + 128], in0=h3(spa)[:, :, c0:c0 + 128],
                                                    in1=mask2, op=ALU.mult), reads=[spr, R_cb], writes=[spr])

            def stC(i):
                QS, kb, nkb, c0, diag, sl = meta(i)
                spa, spr = spb[i % 3]
                if kb == nkb - 1:
                    S.pool(lambda e: e.memset(racc, 0.0), writes=[R_racc])
                for h in range(2):
                    S.pe(lambda e, h=h: e.matmul(
                        bank(2 * sl + h)[:, c0:512], lhsT=cb[:, TRI, :], rhs=h3(spa)[:, h, c0:512], start=False, stop=True,
                        skip_group_check=True), reads=[R_cb, spr], writes=[PB[2 * sl + h]])
                    if kb < nkb - 1:
                        S.pe(lambda e, h=h: e.matmul(
                            bank(2 * sl + h)[:, c0:512], lhsT=cb[:, ONESN8, :], rhs=racc3[:, h, c0:512], start=False, stop=True,
                            skip_group_check=True), reads=[R_cb, R_racc], writes=[PB[2 * sl + h]])
                if kb > 0:
                    S.dve(lambda e: e.tensor_tensor(out=racc3[:, :, c0:512], in0=racc3[:, :, c0:512],
                                                    in1=h3(spa)[:, :, c0:512], op=ALU.add),
                          reads=[R_racc, spr], writes=[R_racc])

            def stD(i):
                QS, kb, nkb, c0, diag, sl = meta(i)
                Aa, Ar = Ab[i % 3]
                S.act(lambda e: e.activation(out=h3(Aa)[:, :, c0:512], in_=pair(sl)[:, :, c0:512], func=AF.Exp, scale=0.125),
                      reads=[PB[2 * sl], PB[2 * sl + 1]], writes=[Ar])
                if diag:
                    S.dve(lambda e: e.tensor_tensor(out=h3(Aa)[:, :, c0:c0 + 128], in0=h3(Aa)[:, :, c0:c0 + 128],
                                                    in1=mask2, op=ALU.mult), reads=[Ar, R_cb], writes=[Ar])

            def stE(i):
                QS, kb, nkb, c0, diag, sl = meta(i)
                Aa, Ar = Ab[i % 3]
                for h in range(2):
                    hs = slice(64 * h, 64 * h + 64)
                    S.pe(lambda e, h=h, hs=hs: e.matmul(
                        bank(OB)[hs, c0:512], lhsT=vS3[:, kb, hs], rhs=h3(Aa)[:, h, c0:512],
                        start=(kb == nkb - 1), stop=(kb == 0), skip_group_check=True),
                         reads=[R_v, Ar], writes=[PB[OB]])
                if kb == 0:
                    S.dve(lambda e: e.tensor_tensor(out=obT[:, hp, ts(QS, 512)], in0=bank(OB), in1=zs[:, ts(QS, 512)],
                                                    op=ALU.mult), reads=[PB[OB], R_z], writes=[R_ob[hp][QS]])

            fill = list(filler)
            for t in range(n_it + 3):
                if t < n_it:
                    stA(t)
                if 0 <= t - 1 < n_it:
                    stB(t - 1)
                if 0 <= t - 2 < n_it:
                    stC(t - 2)
                    stD(t - 2)
                if 0 <= t - 3 < n_it:
                    stE(t - 3)
                if fill and t % 2 == 1:
                    fill.pop(0)()
            while fill:
                fill.pop(0)()

        first_units = inproj_units(0, bsets[0], banks=tuple(range(8)))
        for u in first_units:
            u()
        if dbg:
            b0 = bsets[0]
            dump("qT0", b0["qT"], [b0["R_qT"]]); dump("kT0", b0["kT"], [b0["R_kT"]])
            dump("vS0", b0["vS"], [b0["R_vS"]]); dump("zs0", b0["zs"], [b0["R_zs"]])
        S.barrier(fr_pre)
        for hp in range(4):
            filler = inproj_units(hp + 1, bsets[(hp + 1) % 2]) if hp < 3 else []
            attn_pair(hp, bsets[hp % 2], filler)
        fr_A = S.frontier()
        dump("obA", obT[:], [r for rr in R_ob for r in rr])
        if stop_after == "A":
            return True

        def merge(br, barrier=True):
            if barrier:
                S.barrier()
            th, tm = mtmp["th"], mtmp["tm"]
            extra_w = [bsets[1][k_] for k_ in ("R_qT", "R_kT", "R_vS", "R_zs")] if br == 0 else []
            gslots = [take_group(), take_group()]
            wslot = take_group()
            wbv = w4(wslot)
            k = 0
            for dc in range(8):
                gs = gslots[dc // 4]
                for tg in range(4):
                    gbk = nb(0, 4)
                    proj_fm(gs, (dc % 4) * 128, tg, gbk)
                    pbk = 4 + nb(0, 4)
                    for kc in range(4):
                        S.pe(lambda e, pbk=pbk, kc=kc, dc=dc, tg=tg: e.matmul(
                            bank(pbk), lhsT=wbv[:, kc, ts(dc, 128)], rhs=obT[:, kc, ts(tg, 512)],
                            start=(kc == 0), stop=(kc == 3)), reads=[R_ws[wslot], R_ob[kc][tg]], writes=[PB[pbk]])
                    (tha, thr) = th[k % 2]
                    (tma, tmr) = tm[k % 2]
                    k += 1
                    S.act(lambda e, gbk=gbk, tha=tha: e.activation(out=tha, in_=bank(gbk), func=AF.Tanh, scale=0.5),
                          reads=[PB[gbk]], writes=[thr])
                    if br == 0:
                        S.dve(lambda e, pbk=pbk, tha=tha, dc=dc, tg=tg: e.scalar_tensor_tensor(
                            out=mg[:, dc, ts(tg, 512)], in0=tha, scalar=1.0, in1=bank(pbk), op0=ALU.add, op1=ALU.mult),
                              reads=[thr, PB[pbk]], writes=[R_mg[dc][tg]] + extra_w)
                    else:
                        S.dve(lambda e, pbk=pbk, tha=tha, tma=tma: e.scalar_tensor_tensor(
                            out=tma, in0=tha, scalar=1.0, in1=bank(pbk), op0=ALU.add, op1=ALU.mult),
                              reads=[thr, PB[pbk]], writes=[tmr])
                        S.pool(lambda e, tma=tma, dc=dc, tg=tg: e.tensor_tensor(
                            out=mg[:, dc, ts(tg, 512)], in0=mg[:, dc, ts(tg, 512)], in1=tma, op=ALU.add),
                               reads=[tmr, R_mg[dc][tg]], writes=[R_mg[dc][tg]])
                if dc == 3:
                    release_group()
            release_group()
            release_group()

        merge(0, barrier=False)
        dump("mgA", mg[:], [r for rr in R_mg for r in rr])
        if stop_after == "MA":
            return True

        S.barrier(fr_A)
        AR.reset()
        ubs = [AR.F(2064, f"ub{i}") for i in range(2)]
        pa, R_pa = AR.F(2064, "pa")
        pb_, R_pbb = AR.F(2064, "pb")
        t16, R_t16 = AR.F(16, "t16")
        dTs = [AR.B(2048, f"dT{i}") for i in range(2)]
        pzss = [AR.B(2048, f"pzs{i}") for i in range(2)]
        su = take_group()
        sz = take_group()
        for ub_, rub_ in ubs:
            S.dve(lambda e, ub_=ub_: e.memset(ub_[:, 0:16], 0.0), writes=[rub_])
        S.dve(lambda e: e.memset(pa[:, 0:16], 0.0), writes=[R_pa])
        S.dve(lambda e: e.memset(pb_[:, 0:16], 0.0), writes=[R_pbb])

        def b_inproj(g):
            ub, R_ub = ubs[g % 2]
            pzs, R_pzs = pzss[g % 2]
            for tg in range(4):
                b = nb(0, 4)
                proj_fm(su, g * 128, tg, b)
                S.act(lambda e, b=b, tg=tg: e.copy(out=ub[:, 16 + tg * 512:16 + (tg + 1) * 512], in_=bank(b)),
                      reads=[PB[b]], writes=[R_ub])
                b = nb(0, 4)
                proj_fm(sz, g * 128, tg, b)
                S.act(lambda e, b=b, tg=tg: e.activation(out=pzs[:, ts(tg, 512)], in_=bank(b), func=AF.Silu),
                      reads=[PB[b]], writes=[R_pzs])

        def b_chain(g):
            w = 2 << g
            ub, R_ub = ubs[g % 2]
            dT, R_dT = dTs[g % 2]
            cur, rcur = ub, R_ub
            bufs = [(pa, R_pa), (pb_, R_pbb)]
            sh = 1
            bi = 0
            while sh < w:
                dst, rdst = bufs[bi % 2]
                bi += 1
                S.dve(lambda e, cur=cur, dst=dst, sh=sh: e.tensor_tensor(out=dst[:, 16:2064], in0=cur[:, 16:2064],
                                                                        in1=cur[:, 16 - sh:2064 - sh], op=ALU.add),
                      reads=[rcur], writes=[rdst])
                cur, rcur = dst, rdst
                sh *= 2
            S.dve(lambda e, cur=cur: e.scalar_tensor_tensor(out=dT, in0=cur[:, 16:2064], scalar=1.0 / w,
                                                           in1=ub[:, 16:2064], op0=ALU.mult, op1=ALU.subtract),
                  reads=[rcur, R_ub], writes=[R_dT])
            wn = min(w, 16)
            S.dve(lambda e, cur=cur: e.tensor_tensor(out=t16[:, 0:wn], in0=cur[:, 16:16 + wn], in1=invcnt[:, 0:wn],
                                                    op=ALU.mult), reads=[rcur, R_cf], writes=[R_t16])
            S.dve(lambda e: e.tensor_tensor(out=dT[:, 0:wn], in0=t16[:, 0:wn], in1=ub[:, 16:16 + wn],
                                            op=ALU.subtract), reads=[R_t16, R_ub], writes=[R_dT])

        def b_out(g):
            dT, R_dT = dTs[g % 2]
            pzs, R_pzs = pzss[g % 2]
            for tg in range(4):
                b = 4 + nb(0, 4)
                S.pe(lambda e, b=b, tg=tg: e.matmul(bank(b), lhsT=poolw[:, ts(g, 128)], rhs=dT[:, ts(tg, 512)],
                                                    start=True, stop=True), reads=[R_pw, R_dT], writes=[PB[b]])
                S.dve(lambda e, b=b, tg=tg: e.scalar_tensor_tensor(
                    out=obT[:, g, ts(tg, 512)], in0=bank(b), scalar=vecs[:, VO + V_PS + g:VO + V_PS + g + 1],
                    in1=pzs[:, ts(tg, 512)], op0=ALU.mult, op1=ALU.mult),
                      reads=[PB[b], R_vecs, R_pzs], writes=[R_ob[g][tg]])

        b_inproj(0)
        for g in range(4):
            if g < 3:
                b_inproj(g + 1)
            b_chain(g)
            b_out(g)
        release_group()
        release_group()
        fr_B = S.frontier()
        dump("obB", obT[:], [r for rr in R_ob for r in rr])
        if stop_after == "B":
            return True
        merge(1, barrier=False)
        if stop_after == "MB":
            return True

        S.barrier(fr_B)
        AR.reset()
        Tsets = [[AR.F(512, f"T{k}_{i}") for i in range(6)] for k in range(2)]
        oTg, R_oTg = AR.F(2048, "oTg")
        rs = [AR.F(512, f"rs{i}") for i in range(2)]
        Sf, R_Sf = AR.F(512, "Sf")
        ebl, R_ebl = AR.F(16, "ebl")
        qt, R_qt = AR.B(2048, "qt")
        kt, R_kt = AR.B(2048, "kt")
        qp, R_qp = AR.B(2048, "qp")
        kpTs = [AR.B(512, f"kpT{i}") for i in range(2)]
        kp, R_kp = AR.B(2048, "kp")
        Vb, R_Vb = AR.B(2048, "Vb")
        q2, R_q2 = AR.B(1024, "q2")
        q2v = q2.rearrange("p (h c j) -> p h c j", h=4, c=4)
        v8 = lambda a: a.rearrange("p (c j) -> p c j", c=8)
        zsg, R_zsg = AR.B(2048, "zsg")
        sqbs = [AR.B(512, f"sqb{i}") for i in range(2)]
        Sb, R_Sb = AR.B(512, "Sb")
        AMs = [AR.B(512, f"AM{i}") for i in range(2)]
        maskI_u = mi4[:].bitcast(mybir.dt.uint16)
        si_, szc, sf_, sq_ = take_group(), take_group(), take_group(), take_group()
        Vb3 = Vb.rearrange("p (t c) -> p t c", t=4)
        kp4 = kp.rearrange("p (h c d) -> p h c d", h=4, c=4)
        S.dve(lambda e: e.memset(Sf, 0.0), writes=[R_Sf])
        S.dve(lambda e: e.memset(Sb, 0.0), writes=[R_Sb])
        v3 = lambda a: a.rearrange("p (c j) -> p c j", c=4)
        def vz_units(tg, Vb3, RVb, zsg, RZs):
            units = []

            def uv(j):
                def f():
                    tt = tg * 4 + j
                    b = 4 + nb(0, 4)
                    wv = w8(si_)
                    for kc in range(8):
                        S.pe(lambda e, kc=kc: e.matmul(bank(b), lhsT=hT[:, kc, ts(tt, 128)], rhs=wv[:, kc, :],
                                                       start=(kc == 0), stop=(kc == 7)),
                             reads=[R_ws[si_], R_hT[tg]], writes=[PB[b]])
                    S.dve(lambda e: e.tensor_copy(out=Vb3[:, j, :], in_=bank(b)), reads=[PB[b]], writes=list(RVb))
                return f

            def uz():
                for h in range(4):
                    bz = 4 + nb(0, 4)
                    proj_fm(szc, h * 128, tg, bz)
                    S.act(lambda e, bz=bz, h=h: e.activation(out=zsg[:, ts(h, 512)], in_=bank(bz), func=AF.Silu),
                          reads=[PB[bz]], writes=list(RZs))

            for j in range(4):
                units.append(uv(j))
            units.append(uz)
            return units

        def c_block(tg, Vb3, RVb, zsg, RZs, fill, prev_rms):
                def prep_head(h, tg):
                    hsl = ts(h, 512)
                    lb_ap = lbv[:, L * 4 + h:L * 4 + h + 1]
                    l1_ap = l1m[:, L * 4 + h:L * 4 + h + 1]
                    bx = nb(0, 4)
                    proj_fm(sf_, h * 128, tg, bx)
                    yield
                    bq = nb(0, 4)
                    proj_fm(sq_, h * 128, tg, bq)
                    yield
                    (t0, r0), (t1, r1), (t2, r2), (t3, r3), (t4, r4), (t5, r5) = Tsets[h % 2]
                    kpT, R_kpT = kpTs[h % 2]
                    S.act(lambda e, bx=bx: e.activation(out=t0, in_=bank(bx), func=AF.Exp, scale=-1.0), reads=[PB[bx]], writes=[r0])
                    yield
                    S.act(lambda e: e.activation(out=t1, in_=t0, func=AF.Ln, bias=1.0, scale=1.0), reads=[r0], writes=[r1])
                    yield
                    S.act(lambda e, lb_ap=lb_ap: e.activation(out=t2, in_=t0, func=AF.Ln, bias=1.0, scale=lb_ap),
                          reads=[r0, R_lb], writes=[r2])
                    yield
                    S.dve(lambda e, bx=bx: e.tensor_tensor(out=t3, in0=bank(bx), in1=t1, op=ALU.add), reads=[PB[bx], r1], writes=[r3])
                    yield
                    S.dve(lambda e: e.tensor_tensor(out=t2, in0=t2, in1=t1, op=ALU.subtract), reads=[r2, r1], writes=[r2])
                    yield
                    S.dve(lambda e: e.tensor_tensor_scan(out=t4, data0=rmask, data1=t2, initial=0.0, op0=ALU.mult, op1=ALU.add),
                          reads=[R_cf, r2], writes=[r4])
                    yield
                    S.dve(lambda e: e.tensor_tensor(out=v3(t5), in0=v3(t4), in1=v3(t4)[:, :, 63:64].broadcast_to([128, 4, 128]),
                                                     op=ALU.subtract), reads=[r4], writes=[r5])
                    yield
                    S.act(lambda e: e.activation(out=t0, in_=t5, func=AF.Exp), reads=[r5], writes=[r0])
                    yield
                    S.dve(lambda e, bq=bq, hsl=hsl: e.tensor_tensor(out=qt[:, hsl], in0=bank(bq), in1=t0, op=ALU.mult),
                          reads=[PB[bq], r0], writes=[R_qt])
                    yield
                    S.pool(lambda e: e.tensor_tensor(out=v3(t5)[:, :, 64:128], in0=v3(t4)[:, :, 64:128],
                                                     in1=v3(t4)[:, :, 127:128].broadcast_to([128, 4, 64]), op=ALU.subtract),
                           reads=[r4, r5], writes=[r5])
                    yield
                    S.act(lambda e: e.activation(out=v3(t0)[:, :, 64:128], in_=v3(t5)[:, :, 64:128], func=AF.Exp),
                          reads=[r5, r0], writes=[r0])
                    yield
                    S.dve(lambda e, bq=bq, h=h: e.tensor_tensor(out=q2v[:, h, :, :], in0=bank(bq).rearrange("p (c j) -> p c j", c=4)[:, :, 64:128],
                                                                in1=v3(t0)[:, :, 64:128], op=ALU.mult),
                          reads=[PB[bq], r0], writes=[R_q2])
                    yield
                    S.pool(lambda e: e.tensor_tensor(out=t3, in0=t4, in1=t3, op=ALU.add), reads=[r4, r3], writes=[r3])
                    yield
                    S.pool(lambda e: e.tensor_tensor(out=v8(t1), in0=v8(t4)[:, :, 63:64].broadcast_to([128, 8, 64]), in1=v8(t3),
                                                     op=ALU.subtract), reads=[r4, r3], writes=[r1])
                    yield
                    S.act(lambda e, hsl=hsl, l1_ap=l1_ap: e.activation(out=kt[:, hsl], in_=t1, func=AF.Exp, bias=l1_ap, scale=1.0),
                          reads=[r1, R_lb], writes=[R_kt])
                    yield
                    S.act(lambda e: e.activation(out=t0, in_=t4, func=AF.Exp), reads=[r4], writes=[r0])
                    yield
                    S.dve(lambda e, bq=bq, hsl=hsl: e.tensor_tensor(out=qp[:, hsl], in0=bank(bq), in1=t0, op=ALU.mult),
                          reads=[PB[bq], r0], writes=[R_qp])
                    yield
                    S.dve(lambda e: e.tensor_tensor(out=v3(t2), in0=v3(t4)[:, :, 127:128].broadcast_to([128, 4, 128]), in1=v3(t3),
                                                     op=ALU.subtract), reads=[r4, r3], writes=[r2])
                    yield
                    S.act(lambda e, l1_ap=l1_ap: e.activation(out=kpT, in_=t2, func=AF.Exp, bias=l1_ap, scale=1.0),
                          reads=[r2, R_lb], writes=[R_kpT])
                    yield
                    S.act(lambda e, h=h: e.activation(out=ebl[:, h * 4:h * 4 + 4], in_=v3(t4)[:, :, 127], func=AF.Exp),
                          reads=[r4], writes=[R_ebl])
                    yield
                    bt = 4 + nb(0, 4)
                    btv = bank(bt).bitcast(BF16)
                    for c in range(4):
                        S.pe(lambda e, c=c, h=h, btv=btv: e.transpose(out=btv[:, ts(c, 128)], in_=kpT[:, ts(c, 128)],
                                                                      identity=cbi), reads=[R_kpT, R_cb], writes=[PB[bt]])
                    S.dve(lambda e, h=h, btv=btv: e.tensor_copy(out=kp[:, ts(h, 512)], in_=btv[:, 0:512]), reads=[PB[bt]], writes=[R_kp])
                    yield

                fill = list(fill)
                for hpair in ((0, 1), (2, 3)):
                    gens = [prep_head(h, tg) for h in hpair]
                    for _ in range(5):
                        next(gens[0])
                    rg = prev_rms
                    if prev_rms is not None:
                        gens.append(prev_rms)
                        prev_rms = None
                    step = 0
                    while gens:
                        for g_ in list(gens):
                            try:
                                next(g_)
                            except StopIteration:
                                gens.remove(g_)
                        step += 1
                        if fill and step % 4 == 0 and (rg is None or rg not in gens):
                            fill.pop(0)()
                while fill:
                    fill.pop(0)()
                for c in range(4):
                    ba = nb(0, 4)
                    for h in range(4):
                        o_ = h * 512 + c * 128
                        S.pe(lambda e, ba=ba, h=h, o_=o_: e.matmul(bank(ba)[0:64, ts(h, 128)], lhsT=kt[:, o_:o_ + 64],
                                                                   rhs=qt[:, o_:o_ + 128], start=True, stop=True),
                             reads=[R_kt, R_qt], writes=[PB[ba]])
                        S.pe(lambda e, ba=ba, h=h, c=c, o_=o_: e.matmul(bank(ba)[64:128, h * 128 + 64:h * 128 + 128], lhsT=kt[:, o_ + 64:o_ + 128],
                                                                        rhs=q2v[:, h, c, :], start=True, stop=True),
                             reads=[R_kt, R_q2], writes=[PB[ba]])
                    bs = nb(0, 4)
                    for h in range(4):
                        S.pe(lambda e, bs=bs, h=h, c=c: e.matmul(bank(bs)[:, ts(h, 128)], lhsT=kp4[:, h, c, :], rhs=Vb3[:, c, ts(h, 128)],
                                                                 start=True, stop=True), reads=[R_kp, *RVb], writes=[PB[bs]])
                    AM, R_AM = AMs[c % 2]
                    S.pool(lambda e, AM=AM: e.memset(AM, 0.0), writes=[R_AM])
                    S.dve(lambda e, ba=ba, AM=AM: e.copy_predicated(out=AM, mask=maskI_u, data=bank(ba)),
                          reads=[PB[ba], R_cb, R_AM], writes=[R_AM])
                    bo = 4 + nb(0, 4)
                    for h in range(4):
                        S.pe(lambda e, bo=bo, h=h, c=c, AM=AM: e.matmul(bank(bo)[:, ts(h, 128)], lhsT=Vb3[:, c, ts(h, 128)], rhs=AM[:, ts(h, 128)],
                                                                 start=True, stop=False), reads=[*RVb, R_AM], writes=[PB[bo]])
                        S.pe(lambda e, bo=bo, h=h, c=c: e.matmul(bank(bo)[:, ts(h, 128)], lhsT=Sb[:, ts(h, 128)],
                                                                 rhs=qp[:, h * 512 + c * 128:h * 512 + (c + 1) * 128],
                                                                 start=False, stop=True), reads=[R_Sb, R_qp], writes=[PB[bo]])
                    S.act(lambda e, bo=bo, c=c: e.copy(out=oTg.rearrange("p (h r) -> p h r", h=4)[:, :, ts(c, 128)],
                                                       in_=bank(bo).rearrange("p (h r) -> p h r", h=4)),
                          reads=[PB[bo]], writes=[R_oTg])
                    Sf3 = Sf.rearrange("p (h v) -> p h v", h=4)
                    S.dve(lambda e, c=c, Sf3=Sf3: e.tensor_tensor(
                        out=Sf3, in0=Sf3, in1=ebl.rearrange("p (h c) -> p h c", h=4)[:, :, c:c + 1].broadcast_to([128, 4, 128]),
                        op=ALU.mult), reads=[R_Sf, R_ebl], writes=[R_Sf])
                    S.dve(lambda e, bs=bs: e.tensor_tensor(out=Sf, in0=Sf, in1=bank(bs), op=ALU.add),
                          reads=[R_Sf, PB[bs]], writes=[R_Sf])
                    S.dve(lambda e: e.tensor_copy(out=Sb, in_=Sf), reads=[R_Sf], writes=[R_Sb])
                if dbg and tg == 0:
                    dump("oTg0", oTg, [R_oTg])
                def rms_gen():
                  for h in range(4):
                    sqb, R_sqb = sqbs[h % 2]
                    S.act(lambda e, h=h, sqb=sqb: e.activation(out=sqb, in_=oTg[:, ts(h, 512)], func=AF.Square), reads=[R_oTg], writes=[R_sqb])
                    yield
                    bss = 4 + nb(0, 4)
                    S.pe(lambda e, bss=bss, h=h, sqb=sqb: e.matmul(bank(bss), lhsT=cb[:, ONES, :], rhs=sqb, start=True, stop=True),
                         reads=[R_cb, R_sqb], writes=[PB[bss]])
                    yield
                    (ra, rr_) = rs[h % 2]
                    S.act(lambda e, bss=bss, ra=ra: e.activation(out=ra, in_=bank(bss), func=AF.Ln, scale=1.0 / 128, bias=EPS),
                          reads=[PB[bss]], writes=[rr_])
                    yield
                    S.act(lambda e, ra=ra: e.activation(out=ra, in_=ra, func=AF.Exp, scale=-0.5), reads=[rr_], writes=[rr_])
                    yield
                    S.dve(lambda e, ra=ra, h=h: e.tensor_tensor(out=ra, in0=ra, in1=oTg[:, ts(h, 512)], op=ALU.mult),
                          reads=[rr_, R_oTg], writes=[rr_])
                    yield
                    S.dve(lambda e, ra=ra, h=h, tg=tg, VO=VO: e.scalar_tensor_tensor(
                        out=obT[:, h, ts(tg, 512)], in0=ra, scalar=vecs[:, VO + V_HG + h:VO + V_HG + h + 1], in1=zsg[:, ts(h, 512)],
                        op0=ALU.mult, op1=ALU.mult), reads=[rr_, R_vecs, *RZs], writes=[R_ob[h][tg]])
                    yield
                return rms_gen()

        mtb = mt[:].bitcast(BF16)
        vsets = [(Vb3, [R_Vb], zsg, [R_zsg]),
                 (mtb[:, 0:2048].rearrange("p (t c) -> p t c", t=4), [R_mt[0], R_mt[1]], mtb[:, 2048:4096], [R_mt[2], R_mt[3]])]
        for u_ in vz_units(0, *vsets[0]):
            u_()
        rms_prev = None
        for tg in range(4):
            nxt = vz_units(tg + 1, *vsets[(tg + 1) % 2]) if tg < 3 else []
            rms_prev = c_block(tg, *vsets[tg % 2], nxt, rms_prev)
        for _ in rms_prev:
            pass
        for _ in range(4):
            release_group()
        fr_C = S.frontier()
        dump("obC", obT[:], [r for rr in R_ob for r in rr])
        if stop_after == "C":
            return True
        merge(2, barrier=False)
        dump("mg", mg[:], [r for rr in R_mg for r in rr])
        if stop_after == "MC":
            return True

        S.barrier(fr_C)
        AR.reset()
        xo = [AR.F(1024, f"xo{i}") for i in range(4)]
        yo = [AR.F(1024, f"yo{i}") for i in range(2)]
        sq2, R_sq2 = AR.F(1024, "sq2")
        gbuf2, R_gb2 = AR.F(1024, "gbuf2")
        st2, _ = AR.F(64, "st2")
        R_st2t = [Res(f"st2{i}") for i in range(16)]
        so0, so1 = take_group(), take_group()
        wo = [w8(so0), w8(so1)]
        rwo = [R_ws[so0], R_ws[so1]]
        final = is_last_layer and last
        fuse_next = not is_last_layer
        if final:
            S.dma("sp", ch_misc, lambda e: e.dma_start(out=gbuf2, in_=gb_d[DEPTH]), writes=[R_gb2])
        elif fuse_next:
            Lnext = layers[li + 1]
            S.dma("sp", ch_misc, lambda e: e.dma_start(out=gbuf2, in_=gb_d[Lnext]), writes=[R_gb2])
            S.dma("pool", ch_pw, lambda e: e.dma_start(out=poolw[:], in_=poolw_d[Lnext]), writes=[R_pw])

        def o_load(tt):
            xa, xr = xo[tt % 4]
            S.dma("sp", ch_x[tt % 4], lambda e: e.dma_start(out=xa, in_=src[ts(tt, 128), :]), writes=[xr])

        def o_front(tt):
            k = tt % 2
            xa, xr = xo[tt % 4]
            ya, yr = yo[k]
            if tt + 2 < 16:
                o_load(tt + 2)
            for half in range(2):
                b = nb(0, 6)
                for dc in range(8):
                    S.pe(lambda e, b=b, dc=dc, half=half: e.matmul(bank(b), lhsT=mg[:, dc, ts(tt, 128)], rhs=wo[half][:, dc, :],
                                                                  start=(dc == 0), stop=(dc == 7)),
                         reads=[R_mg[dc][tt // 4], rwo[half]], writes=[PB[b]])
                S.dve(lambda e, b=b, half=half: e.scalar_tensor_tensor(
                    out=xa[:, ts(half, 512)], in0=bank(b), scalar=0.5, in1=xa[:, ts(half, 512)], op0=ALU.mult, op1=ALU.add),
                      reads=[PB[b], xr], writes=[xr])
            if not final:
                dst = hscr if fuse_next else out_d
                S.dma("sp", ch_out, lambda e: e.dma_start(out=dst[ts(tt, 128), :], in_=xa), reads=[xr])
            if final or fuse_next:
                S.act(lambda e: e.activation(out=sq2, in_=xa, func=AF.Square, accum_out=st2[:, tt:tt + 1]),
                      reads=[xr], writes=[R_sq2, R_st2t[tt]])
                S.act(lambda e: e.activation(out=st2[:, 16 + tt:17 + tt], in_=st2[:, tt:tt + 1], func=AF.Ln,
                                             scale=1.0 / DM, bias=EPS), reads=[R_st2t[tt]], writes=[R_st2t[tt]])
                S.act(lambda e: e.activation(out=st2[:, 32 + tt:33 + tt], in_=st2[:, 16 + tt:17 + tt], func=AF.Exp,
                                             scale=-0.5), reads=[R_st2t[tt]], writes=[R_st2t[tt]])

        def o_mid(tt):
            k = tt % 2
            xa, xr = xo[tt % 4]
            ya, yr = yo[k]
            if final or fuse_next:
                S.dve(lambda e: e.scalar_tensor_tensor(out=ya, in0=xa, scalar=st2[:, 32 + tt:33 + tt],
                                                       in1=gbuf2, op0=ALU.mult, op1=ALU.mult),
                      reads=[xr, R_st2t[tt], R_gb2], writes=[yr])
            if final:
                S.dma("sp", ch_out, lambda e: e.dma_start(out=out_d[ts(tt, 128), :], in_=ya), reads=[yr])

        def o_back(tt):
            ya, yr = yo[tt % 2]
            for half in range(2):
                b = 6 + half
                for j in range(4):
                    dc = half * 4 + j
                    S.pe(lambda e, b=b, j=j, dc=dc: e.transpose(out=bank(b)[:, ts(j, 128)], in_=ya[:, ts(dc, 128)],
                                                               identity=ident),
                         reads=[yr, R_cf], writes=[PB[b]])
                S.dve(lambda e, b=b, half=half: e.tensor_copy(
                    out=hT[:, half * 4:half * 4 + 4, ts(tt, 128)],
                    in_=bank(b).rearrange("p (j t) -> p j t", j=4)), reads=[PB[b]], writes=[R_hT[tt // 4]])

        o_load(0)
        o_load(1)
        for tt in range(18):
            if tt < 16:
                o_front(tt)
            if 0 <= tt - 1 < 16:
                o_mid(tt - 1)
            if fuse_next and 0 <= tt - 2 < 16:
                o_back(tt - 2)
        release_group()
        release_group()

        return False

    for li, L in enumerate(layers):
        if layer_body(li, L):
            break

    S.emit(final_chans=[ch_out, ch_dbg])
    es.close()
    return nc


def _consts():
    c = np.zeros((128, CW), np.float32)
    j = np.arange(128)[:, None]
    q = np.arange(128)[None, :]
    c[:, 0:128] = np.eye(128, dtype=np.float32)
    c[:, 128:256] = np.where(j >= q, -8.0, 0.0)
    c[:, 256:384] = np.where(j < q, 1.0, 0.0)
    c[:, 384:512] = np.where(j <= q, 1.0, 0.0)
    rm = np.ones((512,), np.float32)
    rm[0::128] = 0.0
    c[:, 512:1024] = rm[None, :]
    c[:, 1024:1040] = (1.0 / np.arange(1, 17, dtype=np.float32))[None, :]
    return c


def _fm8(w):
    C = w.shape[1]
    return np.ascontiguousarray(w.reshape(8, 128, C).transpose(1, 0, 2)).reshape(128, 8 * C)


def _fm4(w):
    C = w.shape[1]
    return np.ascontiguousarray(w.reshape(4, 128, C).transpose(1, 0, 2)).reshape(128, 4 * C)


def _pack(norm_g, w_in, pool_w, pool_scale, hgrn_lb, hgrn_norm_g, w_branch, w_out, final_g):
    wg = np.empty((DEPTH * NGRP, 128, 4096), np.float32)
    for l in range(DEPTH):
        W = w_in[l]
        groups = []
        for hp in range(4):
            cols = np.concatenate([np.arange(hp * 128, hp * 128 + 128) + off for off in (0, 512, 1024, 1536)])
            groups.append(_fm8(W[:, cols]))
        groups.append(_fm8(W[:, 5120:5632])); groups.append(_fm8(W[:, 5632:6144]))
        groups.append(_fm4(w_branch[l, 0]))
        groups.append(_fm8(W[:, 2048:2560])); groups.append(_fm8(W[:, 2560:3072]))
        groups.append(_fm8(W[:, 6144:6656])); groups.append(_fm8(W[:, 6656:7168]))
        groups.append(_fm4(w_branch[l, 1]))
        groups.append(_fm8(W[:, 4096:4608])); groups.append(_fm8(W[:, 4608:5120]))
        groups.append(_fm8(W[:, 3584:4096])); groups.append(_fm8(W[:, 3072:3584]))
        groups.append(_fm8(W[:, 7168:7680])); groups.append(_fm8(W[:, 7680:8192]))
        groups.append(_fm4(w_branch[l, 2]))
        groups.append(_fm8(w_out[l][:, 0:512])); groups.append(_fm8(w_out[l][:, 512:1024]))
        assert len(groups) == NGRP
        for g, a in enumerate(groups):
            wg[l * NGRP + g] = a
    pw = np.ascontiguousarray(pool_w.transpose(0, 2, 1, 3)).reshape(DEPTH, 128, 512)
    vec = np.empty((128, DEPTH * 12), np.float32)
    for l in range(DEPTH):
        vec[:, l * 12 + 0:l * 12 + 4] = pool_scale[l].reshape(4, 128).T
        vec[:, l * 12 + 4:l * 12 + 8] = hgrn_lb[l].reshape(4, 128).T
        vec[:, l * 12 + 8:l * 12 + 12] = hgrn_norm_g[l].reshape(4, 128).T
    gb = np.empty((DEPTH + 1, 128, DM), np.float32)
    for l in range(DEPTH):
        gb[l] = np.broadcast_to(norm_g[l][None, :], (128, DM))
    gb[DEPTH] = np.broadcast_to(final_g[None, :], (128, DM))
    return {"wgrp": wg, "poolw": pw, "vecs": vec, "gb": gb, "cst": _consts()}


_NC_CACHE = {}


def kernel(x, norm_g, w_in, pool_w, pool_scale, hgrn_lb, hgrn_norm_g, w_branch, w_out, final_g):
    f = lambda a: np.ascontiguousarray(np.asarray(a, dtype=np.float32))
    x = f(x)
    shared = _pack(f(norm_g), f(w_in), f(pool_w), f(pool_scale), f(hgrn_lb), f(hgrn_norm_g), f(w_branch), f(w_out), f(final_g))
    if "nc" not in _NC_CACHE:
        _NC_CACHE["nc"] = build_program()
    nc = _NC_CACHE["nc"]
    in_maps = [dict(shared, x=x[b]) for b in range(NCORES)]
    res = run_bass_kernel_spmd(nc, in_maps, core_ids=list(range(NCORES)))
    return np.stack([np.asarray(r["out"], dtype=np.float32) for r in res.results], axis=0)
```

```python
import numpy as np
from contextlib import ExitStack
import concourse.bass as bass
import concourse.mybir as mybir
from concourse.bass_utils import run_bass_kernel_spmd

F32 = mybir.dt.float32
BF16 = mybir.dt.bfloat16
AF = mybir.ActivationFunctionType
ALU = mybir.AluOpType

SEQ = 2048
DM = 1024
DEPTH = 2
NCORES = 8
EPS = 1e-6
NGRP = 21
NSLOT = 5
CW = 4 * 128 + 512 + 16

ENGS = ("pe", "act", "dve", "pool", "sp")
SEM_CHUNK = 4000
DMA_CHUNK = 1000


class Res:
    __slots__ = ("name", "psum", "last_w", "readers", "strict")

    def __init__(self, name, psum=False, strict=False):
        self.name = name
        self.psum = psum
        self.last_w = None
        self.readers = []
        self.strict = strict


class Op:
    __slots__ = ("eng", "fn", "deps", "signal", "sig_idx", "chan", "chan_idx", "name")

    def __init__(self, eng, fn, chan=None, name=""):
        self.eng = eng
        self.fn = fn
        self.deps = []
        self.signal = False
        self.sig_idx = -1
        self.chan = chan
        self.chan_idx = -1
        self.name = name


class Chan:
    __slots__ = ("name", "n", "sems", "last", "serial")

    def __init__(self, name, serial=True):
        self.name = name
        self.n = 0
        self.sems = []
        self.last = None
        self.serial = serial


class Sched:
    def __init__(self, nc):
        self.nc = nc
        self.ops = {e: [] for e in ENGS}
        self.chans = []
        self.last = {e: None for e in ENGS}
        self.pending_barrier = {e: None for e in ENGS}

    def chan(self, name, serial=True):
        c = Chan(name, serial)
        self.chans.append(c)
        return c

    def add(self, eng, fn, reads=(), writes=(), chan=None, name=""):
        op = Op(eng, fn, chan, name)
        raw = set()
        other = set()
        strict = set()
        for r in reads:
            if r.last_w is not None:
                raw.add(r.last_w)
            if r.psum:
                for o in r.readers:
                    if o.eng != eng:
                        other.add(o)
            r.readers.append(op)
        for w in writes:
            if w.last_w is not None:
                other.add(w.last_w)
                if w.strict:
                    strict.add(w.last_w)
            for o in w.readers:
                if o is not op:
                    other.add(o)
                    if w.strict:
                        strict.add(o)
            w.last_w = op
            w.readers = []
        pb = self.pending_barrier[eng]
        if pb is not None:
            for o in pb:
                raw.add(o)
            self.pending_barrier[eng] = None
        if chan is not None and chan.serial and chan.last is not None:
            raw.add(chan.last)
        deps = []
        seen = set()
        for d in list(raw | other):
            if d is op:
                continue
            israw = d in raw
            if d.chan is not None and not d.chan.serial:
                d = d.chan.last
            if id(d) in seen:
                continue
            seen.add(id(d))
            if d.chan is None and d.eng == eng:
                if eng == "pe" or eng == "sp":
                    continue
                if not israw and d not in strict:
                    continue
            deps.append(d)
        op.deps = deps
        for d in deps:
            d.signal = True
        if chan is not None:
            op.chan_idx = chan.n
            chan.n += 1
            chan.last = op
        self.ops[eng].append(op)
        self.last[eng] = op
        return op

    def pe(self, fn, reads=(), writes=()):
        return self.add("pe", fn, reads, writes)

    def act(self, fn, reads=(), writes=()):
        return self.add("act", fn, reads, writes)

    def dve(self, fn, reads=(), writes=()):
        return self.add("dve", fn, reads, writes)

    def pool(self, fn, reads=(), writes=()):
        return self.add("pool", fn, reads, writes)

    def dma(self, eng, chan, fn, reads=(), writes=()):
        return self.add(eng, fn, reads, writes, chan=chan)

    def frontier(self):
        fr = [o for o in self.last.values() if o is not None]
        fr += [c.last for c in self.chans if c.last is not None]
        return fr

    def barrier(self, fr=None):
        if fr is None:
            fr = self.frontier()
        for e in ENGS:
            cur = self.pending_barrier[e]
            self.pending_barrier[e] = list(fr) + (cur if cur else [])

    def emit(self, final_chans=()):
        nc = self.nc
        with ExitStack() as es:
            eng_sems = {}
            for e in ENGS:
                j = 0
                for op in self.ops[e]:
                    if op.chan is None and op.signal:
                        op.sig_idx = j
                        j += 1
                nsem = (j + SEM_CHUNK - 1) // SEM_CHUNK
                eng_sems[e] = [es.enter_context(nc.semaphore(f"s_{e}_{i}")) for i in range(nsem)]
            for ci, c in enumerate(self.chans):
                nsem = (c.n + DMA_CHUNK - 1) // DMA_CHUNK
                c.sems = [es.enter_context(nc.semaphore(f"c_{ci}_{i}")) for i in range(nsem)]

            def sem_of(op):
                if op.chan is not None:
                    return (op.chan.sems[op.chan_idx // DMA_CHUNK],
                            16 * (op.chan_idx % DMA_CHUNK + 1))
                return (eng_sems[op.eng][op.sig_idx // SEM_CHUNK],
                        op.sig_idx % SEM_CHUNK + 1)

            block = es.enter_context(nc.Block())
            handles = {"pe": block.tensor, "act": block.scalar, "dve": block.vector,
                       "pool": block.gpsimd, "sp": block.sync}

            def make(e):
                def body(eng):
                    waited = {}
                    for op in self.ops[e]:
                        need = {}
                        for d in op.deps:
                            s, v = sem_of(d)
                            k = id(s)
                            if waited.get(k, 0) >= v:
                                continue
                            if k not in need or need[k][1] < v:
                                need[k] = (s, v)
                        for k, (s, v) in need.items():
                            eng.wait_ge(s, v)
                            waited[k] = v
                        ins = op.fn(eng)
                        if op.chan is not None:
                            s, _ = sem_of(op)
                            ins.then_inc(s, 16)
                        elif op.signal:
                            s, _ = sem_of(op)
                            ins.then_inc(s, 1)
                    if e == "sp":
                        for c in final_chans:
                            if c.last is not None:
                                s, v = sem_of(c.last)
                                eng.wait_ge(s, v)
                return body

            for e in ENGS:
                handles[e](make(e))


def ts(i, n):
    return slice(i * n, (i + 1) * n)


def build_program(layers=(0, 1), first=True, last=True, dbg=None, stop_after=None):
    nc = bass.Bass("TRN2", target_bir_lowering=False)
    x_in = nc.dram_tensor("x", [SEQ, DM], F32, kind="ExternalInput").ap()
    wgrp = nc.dram_tensor("wgrp", [DEPTH * NGRP, 128, 4096], F32, kind="ExternalInput").ap()
    poolw_d = nc.dram_tensor("poolw", [DEPTH, 128, 512], F32, kind="ExternalInput").ap()
    vecs_d = nc.dram_tensor("vecs", [128, DEPTH * 12], F32, kind="ExternalInput").ap()
    gb_d = nc.dram_tensor("gb", [DEPTH + 1, 128, DM], F32, kind="ExternalInput").ap()
    cst_d = nc.dram_tensor("cst", [128, CW], F32, kind="ExternalInput").ap()
    out_d = nc.dram_tensor("out", [SEQ, DM], F32, kind="ExternalOutput").ap()
    hscr = nc.dram_tensor("hscr", [SEQ, DM], F32, kind="Internal").ap()
    dbg_out = {}
    if dbg:
        for name, shape in dbg.items():
            dbg_out[name] = nc.dram_tensor("dbg_" + name, list(shape), F32, kind="ExternalOutput").ap()

    S = Sched(nc)
    es = ExitStack()

    def sb(name, shape, dt):
        return es.enter_context(nc.sbuf_tensor(name, shape, dt))

    cf = sb("cf", [128, CW], F32)
    cb = sb("cb", [128, 6, 128], BF16)
    mi4 = sb("mi4", [128, 512], BF16)
    vecs = sb("vecs_s", [128, DEPTH * 12], F32)
    lbv = sb("lbv", [128, DEPTH * 4], F32)
    l1m = sb("l1m", [128, DEPTH * 4], F32)
    lbt = sb("lbt", [128, 16], F32)
    hT = sb("hT", [128, 8, SEQ], BF16)
    mg = sb("mg", [128, 8, SEQ], BF16)
    obT = sb("obT", [128, 4, SEQ], BF16)
    wsl = [sb(f"wsl{i}", [128, 4096], BF16) for i in range(NSLOT)]
    poolw = sb("poolw_s", [128, 512], BF16)
    NF = 9760
    NB = 16896
    arf = sb("arf", [128, NF], F32)
    mt = sb("mt", [128, 2048], F32)
    arb = sb("arb", [128, NB], BF16)
    ps = es.enter_context(nc.psum_tensor("ps", [128, 8 * 512], F32))

    def bank(i):
        return ps[:, i * 512:(i + 1) * 512]

    PB = [Res(f"pb{i}", psum=True) for i in range(8)]
    R_cf = Res("cf"); R_cb = Res("cb"); R_vecs = Res("vecs"); R_lb = Res("lb")
    R_hT = [Res(f"hT{tg}") for tg in range(4)]
    R_mg = [[Res(f"mg{dc}_{tg}") for tg in range(4)] for dc in range(8)]
    R_ob = [[Res(f"ob{c}_{tg}") for tg in range(4)] for c in range(4)]
    R_ws = [Res(f"ws{i}") for i in range(NSLOT)]
    R_pw = Res("poolw")
    R_mt = [Res(f"mt{i}", strict=True) for i in range(4)]
    ch_ws = [S.chan(f"ws{i}") for i in range(NSLOT)]
    ch_misc = S.chan("misc")
    ch_pw = S.chan("pw")
    ch_out = S.chan("out", serial=False)
    ch_dbg = S.chan("dbg", serial=False)

    ident = cf[:, 0:128]
    TRI, ONESN8, MASKS, ONES, MASKI = 0, 1, 2, 3, 4
    rmask = cf[:, 512:1024]
    invcnt = cf[:, 1024:1040]

    class Arena:
        def __init__(self):
            self.f = 0
            self.b = 0

        def reset(self):
            self.f = 0
            self.b = 0

        def F(self, n, name):
            a = arf[:, self.f:self.f + n]
            self.f += n
            assert self.f <= NF, (name, self.f)
            return a, Res(name)

        def B(self, n, name):
            a = arb[:, self.b:self.b + n]
            self.b += n
            assert self.b <= NB, (name, self.b)
            return a, Res(name)

    AR = Arena()

    glist = [(l, g) for l in layers for g in range(NGRP)]
    gstate = {"next": 0}

    def issue_group(after=()):
        n = gstate["next"]
        if n >= len(glist):
            return
        l, g = glist[n]
        slot = n % NSLOT
        S.dma("pool", ch_ws[slot],
              lambda e, l=l, g=g, slot=slot: e.dma_start(out=wsl[slot][:], in_=wgrp[l * NGRP + g]),
              reads=list(after), writes=[R_ws[slot]])
        gstate["next"] = n + 1

    gpos = {"cur": 0}

    def take_group():
        n = gpos["cur"]
        gpos["cur"] = n + 1
        return n % NSLOT

    def release_group():
        issue_group()

    def w8(slot):
        return wsl[slot][:].rearrange("p (k c) -> p k c", k=8)

    def w4(slot):
        return wsl[slot][:].rearrange("p (k c) -> p k c", k=4)

    bank_rr = {"i": 0}

    def nb(lo=0, hi=8):
        i = bank_rr["i"]
        bank_rr["i"] = (i + 1) % (hi - lo)
        return lo + i % (hi - lo)

    def proj_fm(slot, c0, tg, b, nk=8):
        wv = w8(slot)
        for kc in range(nk):
            S.pe(lambda e, kc=kc: e.matmul(bank(b), lhsT=wv[:, kc, c0:c0 + 128], rhs=hT[:, kc, ts(tg, 512)],
                                           start=(kc == 0), stop=(kc == nk - 1)),
                 reads=[R_ws[slot], R_hT[tg]], writes=[PB[b]])

    def dump(name, ap, res):
        if dbg and name in dbg:
            S.dma("pool", ch_dbg, lambda e: e.dma_start(out=dbg_out[name], in_=ap), reads=res)

    S.dma("sp", ch_misc, lambda e: e.dma_start(out=cf[:], in_=cst_d), writes=[R_cf])
    S.dma("sp", ch_misc, lambda e: e.dma_start(out=vecs[:], in_=vecs_d), writes=[R_vecs])
    for _ in range(2):
        issue_group()
    for i, c0_ in ((0, 128), (2, 256), (4, 384)):
        S.dve(lambda e, i=i, c0_=c0_: e.tensor_copy(out=cb[:, i, :], in_=cf[:, c0_:c0_ + 128]), reads=[R_cf], writes=[R_cb])
    S.dve(lambda e: e.memset(cb[:, 1, :], -8.0), writes=[R_cb])
    S.dve(lambda e: e.memset(cb[:, 3, :], 1.0), writes=[R_cb])
    for i in range(4):
        S.dve(lambda e, i=i: e.tensor_copy(out=mi4[:, ts(i, 128)], in_=cf[:, 384:512]), reads=[R_cf], writes=[R_cb])
    S.dve(lambda e: e.tensor_copy(out=cb[:, 5, :], in_=cf[:, 0:128]), reads=[R_cf], writes=[R_cb])
    cbi = cb[:, 5, :]
    V_PS, V_LB, V_HG = 0, 4, 8
    S.dve(lambda e: e.memset(lbv[:], 0.0), writes=[R_lb])
    S.dve(lambda e: e.memset(l1m[:], 0.0), writes=[R_lb])
    S.dve(lambda e: e.tensor_tensor(out=lbt[:, 0:4], in0=vecs[:, V_LB:V_LB + 4], in1=vecs[:, 12 + V_LB:12 + V_LB + 4],
                                    op=ALU.subtract), reads=[R_vecs], writes=[R_lb])
    S.act(lambda e: e.activation(out=lbt[:, 4:8], in_=lbt[:, 0:4], func=AF.Exp), reads=[R_lb], writes=[R_lb])
    S.dve(lambda e: e.tensor_scalar(out=lbt[:, 8:12], in0=lbt[:, 4:8], scalar1=1.0, scalar2=None, op0=ALU.add), reads=[R_lb], writes=[R_lb])
    S.dve(lambda e: e.reciprocal(out=lbv[:, 4:8], in_=lbt[:, 8:12]), reads=[R_lb], writes=[R_lb])
    S.dve(lambda e: e.tensor_tensor(out=lbt[:, 12:16], in0=lbt[:, 4:8], in1=lbv[:, 4:8], op=ALU.mult),
          reads=[R_lb], writes=[R_lb])
    S.act(lambda e: e.activation(out=l1m[:, 4:8], in_=lbt[:, 12:16], func=AF.Ln), reads=[R_lb], writes=[R_lb])

    n_layers = len(layers)
    chx_box = {}

    def layer_body(li, L):
        is_first_layer = (li == 0)
        is_last_layer = (li == n_layers - 1)
        src = x_in if (is_first_layer and first) else hscr
        if is_first_layer and not first:
            src = x_in
        VO = L * 12

        mtmp = {"th": [(mt[:, 512 * i:512 * (i + 1)], R_mt[i]) for i in range(2)],
                "tm": [(mt[:, 1024 + 512 * i:1024 + 512 * (i + 1)], R_mt[2 + i]) for i in range(2)]}
        AR.reset()
        do_p0 = (li == 0)
        xt = [AR.F(1024, f"xt{i}") for i in range(2)]
        xn = [AR.F(1024, f"xn{i}") for i in range(2)]
        sq, R_sq = AR.F(1024, "sq")
        gbuf, R_gb = AR.F(1024, "gbuf")
        st, _ = AR.F(64, "st")
        R_stt = [Res(f"st{i}") for i in range(16)]
        if li == 0:
            chx_box["c"] = [S.chan("x0"), S.chan("x1"), S.chan("x2"), S.chan("x3")]
        ch_x = chx_box["c"]
        if do_p0:
            S.dma("sp", ch_misc, lambda e, L=L: e.dma_start(out=gbuf, in_=gb_d[L]), writes=[R_gb])
            S.dma("pool", ch_pw, lambda e, L=L: e.dma_start(out=poolw[:], in_=poolw_d[L]), writes=[R_pw])
        def p0_front(tt):
            k = tt % 2
            xa, xr = xt[k]
            na, nr = xn[k]
            S.dma("sp", ch_x[k], lambda e: e.dma_start(out=xa, in_=src[ts(tt, 128), :]), writes=[xr])
            S.act(lambda e: e.activation(out=sq, in_=xa, func=AF.Square, accum_out=st[:, tt:tt + 1]),
                  reads=[xr], writes=[R_sq, R_stt[tt]])
            S.act(lambda e: e.activation(out=st[:, 16 + tt:17 + tt], in_=st[:, tt:tt + 1], func=AF.Ln,
                                         scale=1.0 / DM, bias=EPS), reads=[R_stt[tt]], writes=[R_stt[tt]])
            S.act(lambda e: e.activation(out=st[:, 32 + tt:33 + tt], in_=st[:, 16 + tt:17 + tt], func=AF.Exp,
                                         scale=-0.5), reads=[R_stt[tt]], writes=[R_stt[tt]])
            S.dve(lambda e: e.scalar_tensor_tensor(out=na, in0=xa, scalar=st[:, 32 + tt:33 + tt],
                                                   in1=gbuf, op0=ALU.mult, op1=ALU.mult),
                  reads=[xr, R_stt[tt], R_gb], writes=[nr])

        def p0_back(tt):
            na, nr = xn[tt % 2]
            for half in range(2):
                b = nb(0, 4)
                for j in range(4):
                    dc = half * 4 + j
                    S.pe(lambda e, b=b, j=j, dc=dc: e.transpose(out=bank(b)[:, ts(j, 128)], in_=na[:, ts(dc, 128)],
                                                               identity=ident),
                         reads=[nr, R_cf], writes=[PB[b]])
                S.dve(lambda e, b=b, half=half: e.tensor_copy(
                    out=hT[:, half * 4:half * 4 + 4, ts(tt, 128)],
                    in_=bank(b).rearrange("p (j t) -> p j t", j=4)), reads=[PB[b]], writes=[R_hT[tt // 4]])

        for tt in range(17 if do_p0 else 0):
            if tt < 16:
                p0_front(tt)
            if tt >= 1:
                p0_back(tt - 1)
        if li == 0:
            for _ in range(NSLOT - 2):
                issue_group(after=[xt[1][1]] if do_p0 else ())
        dump("hT", hT[:], [r for r in R_hT])
        if stop_after == "p0":
            return True

        fr_pre = S.frontier()
        AR.reset()
        spb = [AR.B(1024, f"sp{i}") for i in range(3)]
        racc, R_racc = AR.B(1024, "racc")
        Ab = [AR.B(1024, f"A{i}") for i in range(3)]
        eb = [AR.F(1024, f"e{i}") for i in range(2)]
        mgflat = mg[:].rearrange("p a b -> p (a b)")
        R_mgall = [r for rr in R_mg for r in rr]
        bsets = []
        for si in range(2):
            d = {}
            for k_, nm in enumerate(("qT", "kT", "vS", "zs")):
                if si == 0:
                    d[nm], d["R_" + nm] = AR.B(2048, f"{nm}0")
                else:
                    d[nm] = mgflat[:, k_ * 2048:(k_ + 1) * 2048]
                    d["R_" + nm] = Res(f"{nm}1", strict=True)
            d["vS3"] = d["vS"].rearrange("p (t c) -> p t c", t=16)
            bsets.append(d)
        IPB = 7
        OB = 6

        def inproj_units(hp, bs, banks=(IPB,)):
            slot = take_group()
            wv = w8(slot)
            units = []
            bsel = {"i": 0}

            def nbk():
                bsel["i"] += 1
                return banks[bsel["i"] % len(banks)]

            def uq(tg, c0, dst, rdst):
                def f():
                    bk = nbk()
                    proj_fm(slot, c0, tg, bk)
                    S.dve(lambda e: e.tensor_copy(out=dst[:, ts(tg, 512)], in_=bank(bk)), reads=[PB[bk]], writes=[rdst])
                return f

            def uv(t4):
                def f():
                    bk = nbk()
                    for j in range(4):
                        tt = t4 * 4 + j
                        for kc in range(8):
                            S.pe(lambda e, j=j, tt=tt, kc=kc: e.matmul(
                                bank(bk)[:, ts(j, 128)], lhsT=hT[:, kc, ts(tt, 128)], rhs=wv[:, kc, 256:384],
                                start=(kc == 0), stop=(kc == 7)), reads=[R_ws[slot], R_hT[t4]], writes=[PB[bk]])
                    S.dve(lambda e: e.tensor_copy(out=bs["vS3"][:, t4 * 4:t4 * 4 + 4, :],
                                                  in_=bank(bk).rearrange("p (j c) -> p j c", j=4)),
                          reads=[PB[bk]], writes=[bs["R_vS"]])
                return f

            def uz():
                for tg in range(4):
                    bk = nbk()
                    proj_fm(slot, 384, tg, bk)
                    S.act(lambda e, tg=tg, bk=bk: e.activation(out=bs["zs"][:, ts(tg, 512)], in_=bank(bk), func=AF.Silu),
                          reads=[PB[bk]], writes=[bs["R_zs"]])
                release_group()

            for tg in range(4):
                units.append(uq(tg, 0, bs["qT"], bs["R_qT"]))
            for tg in range(4):
                units.append(uq(tg, 128, bs["kT"], bs["R_kT"]))
            for t4 in range(4):
                units.append(uv(t4))
            units.append(uz)
            return units

        def h3(a_):
            return a_.rearrange("p (h c) -> p h c", h=2)

        def pair(sl):
            return ps[:, (2 * sl) * 512:(2 * sl + 2) * 512].rearrange("p (h c) -> p h c", h=2)

        mask2 = cb[:, MASKS:MASKS + 1, :].broadcast_to([128, 2, 128])
        racc3 = h3(racc)
        items = [(QS, kb) for QS in range(4) for kb in range(4 * (QS + 1) - 1, -1, -1)]
        n_it = len(items)

        def attn_pair(hp, bs, filler):
            qT, kT, vS3, zs = bs["qT"], bs["kT"], bs["vS3"], bs["zs"]
            R_q, R_k, R_v, R_z = bs["R_qT"], bs["R_kT"], bs["R_vS"], bs["R_zs"]

            def meta(i):
                QS, kb = items[i]
                nkb = 4 * (QS + 1)
                j = kb - 4 * QS
                c0 = 128 * j if j >= 0 else 0
                return QS, kb, nkb, c0, (j >= 0), i % 3

            def stA(i):
                QS, kb, nkb, c0, diag, sl = meta(i)
                q0 = QS * 512
                for h in range(2):
                    hs = slice(64 * h, 64 * h + 64)
                    S.pe(lambda e, h=h, hs=hs: e.matmul(
                        bank(2 * sl + h)[:, c0:512], lhsT=kT[hs, ts(kb, 128)], rhs=qT[hs, q0 + c0:q0 + 512],
                        start=True, stop=True), reads=[R_k, R_q], writes=[PB[2 * sl + h]])

            def stB(i):
                QS, kb, nkb, c0, diag, sl = meta(i)
                ea, er = eb[i % 2]
                spa, spr = spb[i % 3]
                S.act(lambda e: e.activation(out=h3(ea)[:, :, c0:512], in_=pair(sl)[:, :, c0:512], func=AF.Exp, scale=0.125),
                      reads=[PB[2 * sl], PB[2 * sl + 1]], writes=[er])
                S.act(lambda e: e.activation(out=h3(spa)[:, :, c0:512], in_=h3(ea)[:, :, c0:512], func=AF.Ln, bias=1.0, scale=1.0),
                      reads=[er], writes=[spr])
                if diag:
                    S.dve(lambda e: e.tensor_tensor(out=h3(spa)[:, :, c0:c0 + 128], in0=h3(spa)[:, :, c0:c0 + 128],
                                                    in1=mask2, op=ALU.mult), reads=[spr, R_cb], writes=[spr])

            def stC(i):
                QS, kb, nkb, c0, diag, sl = meta(i)
                spa, spr = spb[i % 3]
                if kb == nkb - 1:
                    S.pool(lambda e: e.memset(racc, 0.0), writes=[R_racc])
                for h in range(2):
                    S.pe(lambda e, h=h: e.matmul(
                        bank(2 * sl + h)[:, c0:512], lhsT=cb[:, TRI, :], rhs=h3(spa)[:, h, c0:512], start=False, stop=True,
                        skip_group_check=True), reads=[R_cb, spr], writes=[PB[2 * sl + h]])
                    if kb < nkb - 1:
                        S.pe(lambda e, h=h: e.matmul(
                            bank(2 * sl + h)[:, c0:512], lhsT=cb[:, ONESN8, :], rhs=racc3[:, h, c0:512], start=False, stop=True,
                            skip_group_check=True), reads=[R_cb, R_racc], writes=[PB[2 * sl + h]])
                if kb > 0:
                    S.dve(lambda e: e.tensor_tensor(out=racc3[:, :, c0:512], in0=racc3[:, :, c0:512],
                                                    in1=h3(spa)[:, :, c0:512], op=ALU.add),
                          reads=[R_racc, spr], writes=[R_racc])

            def stD(i):
                QS, kb, nkb, c0, diag, sl = meta(i)
                Aa, Ar = Ab[i % 3]
                S.act(lambda e: e.activation(out=h3(Aa)[:, :, c0:512], in_=pair(sl)[:, :, c0:512], func=AF.Exp, scale=0.125),
                      reads=[PB[2 * sl], PB[2 * sl + 1]], writes=[Ar])
                if diag:
                    S.dve(lambda e: e.tensor_tensor(out=h3(Aa)[:, :, c0:c0 + 128], in0=h3(Aa)[:, :, c0:c0 + 128],
                                                    in1=mask2, op=ALU.mult), reads=[Ar, R_cb], writes=[Ar])

            def stE(i):
                QS, kb, nkb, c0, diag, sl = meta(i)
                Aa, Ar = Ab[i % 3]
                for h in range(2):
                    hs = slice(64 * h, 64 * h + 64)
                    S.pe(lambda e, h=h, hs=hs: e.matmul(
                        bank(OB)[hs, c0:512], lhsT=vS3[:, kb, hs], rhs=h3(Aa)[:, h, c0:512],
                        start=(kb == nkb - 1), stop=(kb == 0), skip_group_check=True),
                         reads=[R_v, Ar], writes=[PB[OB]])
                if kb == 0:
                    S.dve(lambda e: e.tensor_tensor(out=obT[:, hp, ts(QS, 512)], in0=bank(OB), in1=zs[:, ts(QS, 512)],
                                                    op=ALU.mult), reads=[PB[OB], R_z], writes=[R_ob[hp][QS]])

            fill = list(filler)
            for t in range(n_it + 3):
                if t < n_it:
                    stA(t)
                if 0 <= t - 1 < n_it:
                    stB(t - 1)
                if 0 <= t - 2 < n_it:
                    stC(t - 2)
                    stD(t - 2)
                if 0 <= t - 3 < n_it:
                    stE(t - 3)
                if len(fill) > 1 and t % 2 == 1:
                    fill.pop(0)()
            while fill:
                fill.pop(0)()

        first_units = inproj_units(0, bsets[0], banks=tuple(range(8)))
        for u in first_units:
            u()
        if dbg:
            b0 = bsets[0]
            dump("qT0", b0["qT"], [b0["R_qT"]]); dump("kT0", b0["kT"], [b0["R_kT"]])
            dump("vS0", b0["vS"], [b0["R_vS"]]); dump("zs0", b0["zs"], [b0["R_zs"]])
        S.barrier(fr_pre)
        for hp in range(4):
            filler = inproj_units(hp + 1, bsets[(hp + 1) % 2]) if hp < 3 else []
            attn_pair(hp, bsets[hp % 2], filler)
        fr_A = S.frontier()
        dump("obA", obT[:], [r for rr in R_ob for r in rr])
        if stop_after == "A":
            return True

        def merge(br, barrier=True):
            if barrier:
                S.barrier()
            th, tm = mtmp["th"], mtmp["tm"]
            extra_w = [bsets[1][k_] for k_ in ("R_qT", "R_kT", "R_vS", "R_zs")] if br == 0 else []
            gslots = [take_group(), take_group()]
            wslot = take_group()
            wbv = w4(wslot)
            k = 0
            for dc in range(8):
                gs = gslots[dc // 4]
                for tg in range(4):
                    gbk = nb(0, 4)
                    proj_fm(gs, (dc % 4) * 128, tg, gbk)
                    pbk = 4 + nb(0, 4)
                    for kc in range(4):
                        S.pe(lambda e, pbk=pbk, kc=kc, dc=dc, tg=tg: e.matmul(
                            bank(pbk), lhsT=wbv[:, kc, ts(dc, 128)], rhs=obT[:, kc, ts(tg, 512)],
                            start=(kc == 0), stop=(kc == 3)), reads=[R_ws[wslot], R_ob[kc][tg]], writes=[PB[pbk]])
                    (tha, thr) = th[k % 2]
                    (tma, tmr) = tm[k % 2]
                    k += 1
                    S.act(lambda e, gbk=gbk, tha=tha: e.activation(out=tha, in_=bank(gbk), func=AF.Tanh, scale=0.5),
                          reads=[PB[gbk]], writes=[thr])
                    if br == 0:
                        S.dve(lambda e, pbk=pbk, tha=tha, dc=dc, tg=tg: e.scalar_tensor_tensor(
                            out=mg[:, dc, ts(tg, 512)], in0=tha, scalar=1.0, in1=bank(pbk), op0=ALU.add, op1=ALU.mult),
                              reads=[thr, PB[pbk]], writes=[R_mg[dc][tg]] + extra_w)
                    else:
                        S.dve(lambda e, pbk=pbk, tha=tha, tma=tma: e.scalar_tensor_tensor(
                            out=tma, in0=tha, scalar=1.0, in1=bank(pbk), op0=ALU.add, op1=ALU.mult),
                              reads=[thr, PB[pbk]], writes=[tmr])
                        S.pool(lambda e, tma=tma, dc=dc, tg=tg: e.tensor_tensor(
                            out=mg[:, dc, ts(tg, 512)], in0=mg[:, dc, ts(tg, 512)], in1=tma, op=ALU.add),
                               reads=[tmr, R_mg[dc][tg]], writes=[R_mg[dc][tg]])
                if dc == 3:
                    release_group()
            release_group()
            release_group()

        merge(0, barrier=False)
        dump("mgA", mg[:], [r for rr in R_mg for r in rr])
        if stop_after == "MA":
            return True

        S.barrier(fr_A)
        AR.reset()
        ubs = [AR.F(2064, f"ub{i}") for i in range(2)]
        pa, R_pa = AR.F(2064, "pa")
        pb_, R_pbb = AR.F(2064, "pb")
        t16, R_t16 = AR.F(16, "t16")
        dTs = [AR.B(2048, f"dT{i}") for i in range(2)]
        pzss = [AR.B(2048, f"pzs{i}") for i in range(2)]
        su = take_group()
        sz = take_group()
        for ub_, rub_ in ubs:
            S.dve(lambda e, ub_=ub_: e.memset(ub_[:, 0:16], 0.0), writes=[rub_])
        S.dve(lambda e: e.memset(pa[:, 0:16], 0.0), writes=[R_pa])
        S.dve(lambda e: e.memset(pb_[:, 0:16], 0.0), writes=[R_pbb])

        def b_inproj(g):
            ub, R_ub = ubs[g % 2]
            pzs, R_pzs = pzss[g % 2]
            for tg in range(4):
                b = nb(0, 4)
                proj_fm(su, g * 128, tg, b)
                S.act(lambda e, b=b, tg=tg: e.copy(out=ub[:, 16 + tg * 512:16 + (tg + 1) * 512], in_=bank(b)),
                      reads=[PB[b]], writes=[R_ub])
                b = nb(0, 4)
                proj_fm(sz, g * 128, tg, b)
                S.act(lambda e, b=b, tg=tg: e.activation(out=pzs[:, ts(tg, 512)], in_=bank(b), func=AF.Silu),
                      reads=[PB[b]], writes=[R_pzs])

        def b_chain(g):
            w = 2 << g
            ub, R_ub = ubs[g % 2]
            dT, R_dT = dTs[g % 2]
            cur, rcur = ub, R_ub
            bufs = [(pa, R_pa), (pb_, R_pbb)]
            sh = 1
            bi = 0
            while sh < w:
                dst, rdst = bufs[bi % 2]
                bi += 1
                S.dve(lambda e, cur=cur, dst=dst, sh=sh: e.tensor_tensor(out=dst[:, 16:2064], in0=cur[:, 16:2064],
                                                                        in1=cur[:, 16 - sh:2064 - sh], op=ALU.add),
                      reads=[rcur], writes=[rdst])
                cur, rcur = dst, rdst
                sh *= 2
            S.dve(lambda e, cur=cur: e.scalar_tensor_tensor(out=dT, in0=cur[:, 16:2064], scalar=1.0 / w,
                                                           in1=ub[:, 16:2064], op0=ALU.mult, op1=ALU.subtract),
                  reads=[rcur, R_ub], writes=[R_dT])
            wn = min(w, 16)
            S.dve(lambda e, cur=cur: e.tensor_tensor(out=t16[:, 0:wn], in0=cur[:, 16:16 + wn], in1=invcnt[:, 0:wn],
                                                    op=ALU.mult), reads=[rcur, R_cf], writes=[R_t16])
            S.dve(lambda e: e.tensor_tensor(out=dT[:, 0:wn], in0=t16[:, 0:wn], in1=ub[:, 16:16 + wn],
                                            op=ALU.subtract), reads=[R_t16, R_ub], writes=[R_dT])

        def b_out(g):
            dT, R_dT = dTs[g % 2]
            pzs, R_pzs = pzss[g % 2]
            for tg in range(4):
                b = 4 + nb(0, 4)
                S.pe(lambda e, b=b, tg=tg: e.matmul(bank(b), lhsT=poolw[:, ts(g, 128)], rhs=dT[:, ts(tg, 512)],
                                                    start=True, stop=True), reads=[R_pw, R_dT], writes=[PB[b]])
                S.dve(lambda e, b=b, tg=tg: e.scalar_tensor_tensor(
                    out=obT[:, g, ts(tg, 512)], in0=bank(b), scalar=vecs[:, VO + V_PS + g:VO + V_PS + g + 1],
                    in1=pzs[:, ts(tg, 512)], op0=ALU.mult, op1=ALU.mult),
                      reads=[PB[b], R_vecs, R_pzs], writes=[R_ob[g][tg]])

        b_inproj(0)
        for g in range(4):
            if g < 3:
                b_inproj(g + 1)
            b_chain(g)
            b_out(g)
        release_group()
        release_group()
        fr_B = S.frontier()
        dump("obB", obT[:], [r for rr in R_ob for r in rr])
        if stop_after == "B":
            return True
        merge(1, barrier=False)
        if stop_after == "MB":
            return True

        S.barrier(fr_B)
        AR.reset()
        Tsets = [[AR.F(512, f"T{k}_{i}") for i in range(6)] for k in range(2)]
        oTg, R_oTg = AR.F(2048, "oTg")
        rs = [AR.F(512, f"rs{i}") for i in range(2)]
        Sf, R_Sf = AR.F(512, "Sf")
        ebl, R_ebl = AR.F(16, "ebl")
        qt, R_qt = AR.B(2048, "qt")
        kt, R_kt = AR.B(2048, "kt")
        qp, R_qp = AR.B(2048, "qp")
        kpTs = [AR.B(512, f"kpT{i}") for i in range(2)]
        kp, R_kp = AR.B(2048, "kp")
        Vb, R_Vb = AR.B(2048, "Vb")
        q2, R_q2 = AR.B(1024, "q2")
        q2v = q2.rearrange("p (h c j) -> p h c j", h=4, c=4)
        v8 = lambda a: a.rearrange("p (c j) -> p c j", c=8)
        zsg, R_zsg = AR.B(2048, "zsg")
        sqbs = [AR.B(512, f"sqb{i}") for i in range(2)]
        Sb, R_Sb = AR.B(512, "Sb")
        AMs = [AR.B(512, f"AM{i}") for i in range(2)]
        maskI_u = mi4[:].bitcast(mybir.dt.uint16)
        si_, szc, sf_, sq_ = take_group(), take_group(), take_group(), take_group()
        Vb3 = Vb.rearrange("p (t c) -> p t c", t=4)
        kp4 = kp.rearrange("p (h c d) -> p h c d", h=4, c=4)
        S.dve(lambda e: e.memset(Sf, 0.0), writes=[R_Sf])
        S.dve(lambda e: e.memset(Sb, 0.0), writes=[R_Sb])
        v3 = lambda a: a.rearrange("p (c j) -> p c j", c=4)
        def vz_units(tg, Vb3, RVb, zsg, RZs):
            units = []

            def uv(j):
                def f():
                    tt = tg * 4 + j
                    b = 4 + nb(0, 4)
                    wv = w8(si_)
                    for kc in range(8):
                        S.pe(lambda e, kc=kc: e.matmul(bank(b), lhsT=hT[:, kc, ts(tt, 128)], rhs=wv[:, kc, :],
                                                       start=(kc == 0), stop=(kc == 7)),
                             reads=[R_ws[si_], R_hT[tg]], writes=[PB[b]])
                    S.dve(lambda e: e.tensor_copy(out=Vb3[:, j, :], in_=bank(b)), reads=[PB[b]], writes=list(RVb))
                return f

            def uz():
                for h in range(4):
                    bz = 4 + nb(0, 4)
                    proj_fm(szc, h * 128, tg, bz)
                    S.act(lambda e, bz=bz, h=h: e.activation(out=zsg[:, ts(h, 512)], in_=bank(bz), func=AF.Silu),
                          reads=[PB[bz]], writes=list(RZs))

            for j in range(4):
                units.append(uv(j))
            units.append(uz)
            return units

        def c_block(tg, Vb3, RVb, zsg, RZs, fill, prev_rms):
                def prep_head(h, tg):
                    hsl = ts(h, 512)
                    lb_ap = lbv[:, L * 4 + h:L * 4 + h + 1]
                    l1_ap = l1m[:, L * 4 + h:L * 4 + h + 1]
                    bx = nb(0, 4)
                    proj_fm(sf_, h * 128, tg, bx)
                    yield
                    bq = nb(0, 4)
                    proj_fm(sq_, h * 128, tg, bq)
                    yield
                    (t0, r0), (t1, r1), (t2, r2), (t3, r3), (t4, r4), (t5, r5) = Tsets[h % 2]
                    kpT, R_kpT = kpTs[h % 2]
                    S.act(lambda e, bx=bx: e.activation(out=t0, in_=bank(bx), func=AF.Exp, scale=-1.0), reads=[PB[bx]], writes=[r0])
                    yield
                    S.act(lambda e: e.activation(out=t1, in_=t0, func=AF.Ln, bias=1.0, scale=1.0), reads=[r0], writes=[r1])
                    yield
                    S.act(lambda e, lb_ap=lb_ap: e.activation(out=t2, in_=t0, func=AF.Ln, bias=1.0, scale=lb_ap),
                          reads=[r0, R_lb], writes=[r2])
                    yield
                    S.dve(lambda e, bx=bx: e.tensor_tensor(out=t3, in0=bank(bx), in1=t1, op=ALU.add), reads=[PB[bx], r1], writes=[r3])
                    yield
                    S.dve(lambda e: e.tensor_tensor(out=t2, in0=t2, in1=t1, op=ALU.subtract), reads=[r2, r1], writes=[r2])
                    yield
                    S.dve(lambda e: e.tensor_tensor_scan(out=t4, data0=rmask, data1=t2, initial=0.0, op0=ALU.mult, op1=ALU.add),
                          reads=[R_cf, r2], writes=[r4])
                    yield
                    S.dve(lambda e: e.tensor_tensor(out=v3(t5), in0=v3(t4), in1=v3(t4)[:, :, 63:64].broadcast_to([128, 4, 128]),
                                                     op=ALU.subtract), reads=[r4], writes=[r5])
                    yield
                    S.act(lambda e: e.activation(out=t0, in_=t5, func=AF.Exp), reads=[r5], writes=[r0])
                    yield
                    S.dve(lambda e, bq=bq, hsl=hsl: e.tensor_tensor(out=qt[:, hsl], in0=bank(bq), in1=t0, op=ALU.mult),
                          reads=[PB[bq], r0], writes=[R_qt])
                    yield
                    S.pool(lambda e: e.tensor_tensor(out=v3(t5)[:, :, 64:128], in0=v3(t4)[:, :, 64:128],
                                                     in1=v3(t4)[:, :, 127:128].broadcast_to([128, 4, 64]), op=ALU.subtract),
                           reads=[r4, r5], writes=[r5])
                    yield
                    S.act(lambda e: e.activation(out=v3(t0)[:, :, 64:128], in_=v3(t5)[:, :, 64:128], func=AF.Exp),
                          reads=[r5, r0], writes=[r0])
                    yield
                    S.dve(lambda e, bq=bq, h=h: e.tensor_tensor(out=q2v[:, h, :, :], in0=bank(bq).rearrange("p (c j) -> p c j", c=4)[:, :, 64:128],
                                                                in1=v3(t0)[:, :, 64:128], op=ALU.mult),
                          reads=[PB[bq], r0], writes=[R_q2])
                    yield
                    S.pool(lambda e: e.tensor_tensor(out=t3, in0=t4, in1=t3, op=ALU.add), reads=[r4, r3], writes=[r3])
                    yield
                    S.pool(lambda e: e.tensor_tensor(out=v8(t1), in0=v8(t4)[:, :, 63:64].broadcast_to([128, 8, 64]), in1=v8(t3),
                                                     op=ALU.subtract), reads=[r4, r3], writes=[r1])
                    yield
                    S.act(lambda e, hsl=hsl, l1_ap=l1_ap: e.activation(out=kt[:, hsl], in_=t1, func=AF.Exp, bias=l1_ap, scale=1.0),
                          reads=[r1, R_lb], writes=[R_kt])
                    yield
                    S.act(lambda e: e.activation(out=t0, in_=t4, func=AF.Exp), reads=[r4], writes=[r0])
                    yield
                    S.dve(lambda e, bq=bq, hsl=hsl: e.tensor_tensor(out=qp[:, hsl], in0=bank(bq), in1=t0, op=ALU.mult),
                          reads=[PB[bq], r0], writes=[R_qp])
                    yield
                    S.pool(lambda e: e.tensor_tensor(out=v3(t2), in0=v3(t4)[:, :, 127:128].broadcast_to([128, 4, 128]), in1=v3(t3),
                                                     op=ALU.subtract), reads=[r4, r3], writes=[r2])
                    yield
                    S.act(lambda e, l1_ap=l1_ap: e.activation(out=kpT, in_=t2, func=AF.Exp, bias=l1_ap, scale=1.0),
                          reads=[r2, R_lb], writes=[R_kpT])
                    yield
                    S.act(lambda e, h=h: e.activation(out=ebl[:, h * 4:h * 4 + 4], in_=v3(t4)[:, :, 127], func=AF.Exp),
                          reads=[r4], writes=[R_ebl])
                    yield
                    bt = 4 + nb(0, 4)
                    btv = bank(bt).bitcast(BF16)
                    for c in range(4):
                        S.pe(lambda e, c=c, h=h, btv=btv: e.transpose(out=btv[:, ts(c, 128)], in_=kpT[:, ts(c, 128)],
                                                                      identity=cbi), reads=[R_kpT, R_cb], writes=[PB[bt]])
                    S.dve(lambda e, h=h, btv=btv: e.tensor_copy(out=kp[:, ts(h, 512)], in_=btv[:, 0:512]), reads=[PB[bt]], writes=[R_kp])
                    yield

                fill = list(fill)
                for hpair in ((0, 1), (2, 3)):
                    gens = [prep_head(h, tg) for h in hpair]
                    for _ in range(5):
                        next(gens[0])
                    rg = prev_rms
                    if prev_rms is not None:
                        gens.append(prev_rms)
                        prev_rms = None
                    step = 0
                    while gens:
                        for g_ in list(gens):
                            try:
                                next(g_)
                            except StopIteration:
                                gens.remove(g_)
                        step += 1
                        if fill and step % 4 == 0 and (rg is None or rg not in gens):
                            fill.pop(0)()
                while fill:
                    fill.pop(0)()
                for c in range(4):
                    ba = nb(0, 4)
                    for h in range(4):
                        o_ = h * 512 + c * 128
                        S.pe(lambda e, ba=ba, h=h, o_=o_: e.matmul(bank(ba)[0:64, ts(h, 128)], lhsT=kt[:, o_:o_ + 64],
                                                                   rhs=qt[:, o_:o_ + 128], start=True, stop=True),
                             reads=[R_kt, R_qt], writes=[PB[ba]])
                        S.pe(lambda e, ba=ba, h=h, c=c, o_=o_: e.matmul(bank(ba)[64:128, h * 128 + 64:h * 128 + 128], lhsT=kt[:, o_ + 64:o_ + 128],
                                                                        rhs=q2v[:, h, c, :], start=True, stop=True),
                             reads=[R_kt, R_q2], writes=[PB[ba]])
                    bs = nb(0, 4)
                    for h in range(4):
                        S.pe(lambda e, bs=bs, h=h, c=c: e.matmul(bank(bs)[:, ts(h, 128)], lhsT=kp4[:, h, c, :], rhs=Vb3[:, c, ts(h, 128)],
                                                                 start=True, stop=True), reads=[R_kp, *RVb], writes=[PB[bs]])
                    AM, R_AM = AMs[c % 2]
                    S.pool(lambda e, AM=AM: e.memset(AM, 0.0), writes=[R_AM])
                    S.dve(lambda e, ba=ba, AM=AM: e.copy_predicated(out=AM, mask=maskI_u, data=bank(ba)),
                          reads=[PB[ba], R_cb, R_AM], writes=[R_AM])
                    bo = 4 + nb(0, 4)
                    for h in range(4):
                        S.pe(lambda e, bo=bo, h=h, c=c, AM=AM: e.matmul(bank(bo)[:, ts(h, 128)], lhsT=Vb3[:, c, ts(h, 128)], rhs=AM[:, ts(h, 128)],
                                                                 start=True, stop=False), reads=[*RVb, R_AM], writes=[PB[bo]])
                        S.pe(lambda e, bo=bo, h=h, c=c: e.matmul(bank(bo)[:, ts(h, 128)], lhsT=Sb[:, ts(h, 128)],
                                                                 rhs=qp[:, h * 512 + c * 128:h * 512 + (c + 1) * 128],
                                                                 start=False, stop=True), reads=[R_Sb, R_qp], writes=[PB[bo]])
                    S.act(lambda e, bo=bo, c=c: e.copy(out=oTg.rearrange("p (h r) -> p h r", h=4)[:, :, ts(c, 128)],
                                                       in_=bank(bo).rearrange("p (h r) -> p h r", h=4)),
                          reads=[PB[bo]], writes=[R_oTg])
                    Sf3 = Sf.rearrange("p (h v) -> p h v", h=4)
                    S.dve(lambda e, c=c, Sf3=Sf3: e.tensor_tensor(
                        out=Sf3, in0=Sf3, in1=ebl.rearrange("p (h c) -> p h c", h=4)[:, :, c:c + 1].broadcast_to([128, 4, 128]),
                        op=ALU.mult), reads=[R_Sf, R_ebl], writes=[R_Sf])
                    S.dve(lambda e, bs=bs: e.tensor_tensor(out=Sf, in0=Sf, in1=bank(bs), op=ALU.add),
                          reads=[R_Sf, PB[bs]], writes=[R_Sf])
                    S.dve(lambda e: e.tensor_copy(out=Sb, in_=Sf), reads=[R_Sf], writes=[R_Sb])
                if dbg and tg == 0:
                    dump("oTg0", oTg, [R_oTg])
                def rms_gen():
                  for h in range(4):
                    sqb, R_sqb = sqbs[h % 2]
                    S.act(lambda e, h=h, sqb=sqb: e.activation(out=sqb, in_=oTg[:, ts(h, 512)], func=AF.Square), reads=[R_oTg], writes=[R_sqb])
                    yield
                    bss = 4 + nb(0, 4)
                    S.pe(lambda e, bss=bss, h=h, sqb=sqb: e.matmul(bank(bss), lhsT=cb[:, ONES, :], rhs=sqb, start=True, stop=True),
                         reads=[R_cb, R_sqb], writes=[PB[bss]])
                    yield
                    (ra, rr_) = rs[h % 2]
                    S.act(lambda e, bss=bss, ra=ra: e.activation(out=ra, in_=bank(bss), func=AF.Ln, scale=1.0 / 128, bias=EPS),
                          reads=[PB[bss]], writes=[rr_])
                    yield
                    S.act(lambda e, ra=ra: e.activation(out=ra, in_=ra, func=AF.Exp, scale=-0.5), reads=[rr_], writes=[rr_])
                    yield
                    S.dve(lambda e, ra=ra, h=h: e.tensor_tensor(out=ra, in0=ra, in1=oTg[:, ts(h, 512)], op=ALU.mult),
                          reads=[rr_, R_oTg], writes=[rr_])
                    yield
                    S.dve(lambda e, ra=ra, h=h, tg=tg, VO=VO: e.scalar_tensor_tensor(
                        out=obT[:, h, ts(tg, 512)], in0=ra, scalar=vecs[:, VO + V_HG + h:VO + V_HG + h + 1], in1=zsg[:, ts(h, 512)],
                        op0=ALU.mult, op1=ALU.mult), reads=[rr_, R_vecs, *RZs], writes=[R_ob[h][tg]])
                    yield
                return rms_gen()

        mtb = mt[:].bitcast(BF16)
        vsets = [(Vb3, [R_Vb], zsg, [R_zsg]),
                 (mtb[:, 0:2048].rearrange("p (t c) -> p t c", t=4), [R_mt[0], R_mt[1]], mtb[:, 2048:4096], [R_mt[2], R_mt[3]])]
        for u_ in vz_units(0, *vsets[0]):
            u_()
        rms_prev = None
        for tg in range(4):
            nxt = vz_units(tg + 1, *vsets[(tg + 1) % 2]) if tg < 3 else []
            rms_prev = c_block(tg, *vsets[tg % 2], nxt, rms_prev)
        for _ in rms_prev:
            pass
        for _ in range(4):
            release_group()
        fr_C = S.frontier()
        dump("obC", obT[:], [r for rr in R_ob for r in rr])
        if stop_after == "C":
            return True
        merge(2, barrier=False)
        dump("mg", mg[:], [r for rr in R_mg for r in rr])
        if stop_after == "MC":
            return True

        S.barrier(fr_C)
        AR.reset()
        xo = [AR.F(1024, f"xo{i}") for i in range(4)]
        yo = [AR.F(1024, f"yo{i}") for i in range(2)]
        sq2, R_sq2 = AR.F(1024, "sq2")
        gbuf2, R_gb2 = AR.F(1024, "gbuf2")
        st2, _ = AR.F(64, "st2")
        R_st2t = [Res(f"st2{i}") for i in range(16)]
        so0, so1 = take_group(), take_group()
        wo = [w8(so0), w8(so1)]
        rwo = [R_ws[so0], R_ws[so1]]
        final = is_last_layer and last
        fuse_next = not is_last_layer
        if final:
            S.dma("sp", ch_misc, lambda e: e.dma_start(out=gbuf2, in_=gb_d[DEPTH]), writes=[R_gb2])
        elif fuse_next:
            Lnext = layers[li + 1]
            S.dma("sp", ch_misc, lambda e: e.dma_start(out=gbuf2, in_=gb_d[Lnext]), writes=[R_gb2])
            S.dma("pool", ch_pw, lambda e: e.dma_start(out=poolw[:], in_=poolw_d[Lnext]), writes=[R_pw])

        def o_load(tt):
            xa, xr = xo[tt % 4]
            S.dma("sp", ch_x[tt % 4], lambda e: e.dma_start(out=xa, in_=src[ts(tt, 128), :]), writes=[xr])

        def o_front(tt):
            k = tt % 2
            xa, xr = xo[tt % 4]
            ya, yr = yo[k]
            if tt + 2 < 16:
                o_load(tt + 2)
            for half in range(2):
                b = nb(0, 6)
                for dc in range(8):
                    S.pe(lambda e, b=b, dc=dc, half=half: e.matmul(bank(b), lhsT=mg[:, dc, ts(tt, 128)], rhs=wo[half][:, dc, :],
                                                                  start=(dc == 0), stop=(dc == 7)),
                         reads=[R_mg[dc][tt // 4], rwo[half]], writes=[PB[b]])
                S.dve(lambda e, b=b, half=half: e.scalar_tensor_tensor(
                    out=xa[:, ts(half, 512)], in0=bank(b), scalar=0.5, in1=xa[:, ts(half, 512)], op0=ALU.mult, op1=ALU.add),
                      reads=[PB[b], xr], writes=[xr])
            if not final:
                dst = hscr if fuse_next else out_d
                S.dma("sp", ch_out, lambda e: e.dma_start(out=dst[ts(tt, 128), :], in_=xa), reads=[xr])
            if final or fuse_next:
                S.act(lambda e: e.activation(out=sq2, in_=xa, func=AF.Square, accum_out=st2[:, tt:tt + 1]),
                      reads=[xr], writes=[R_sq2, R_st2t[tt]])
                S.act(lambda e: e.activation(out=st2[:, 16 + tt:17 + tt], in_=st2[:, tt:tt + 1], func=AF.Ln,
                                             scale=1.0 / DM, bias=EPS), reads=[R_st2t[tt]], writes=[R_st2t[tt]])
                S.act(lambda e: e.activation(out=st2[:, 32 + tt:33 + tt], in_=st2[:, 16 + tt:17 + tt], func=AF.Exp,
                                             scale=-0.5), reads=[R_st2t[tt]], writes=[R_st2t[tt]])

        def o_mid(tt):
            k = tt % 2
            xa, xr = xo[tt % 4]
            ya, yr = yo[k]
            if final or fuse_next:
                S.dve(lambda e: e.scalar_tensor_tensor(out=ya, in0=xa, scalar=st2[:, 32 + tt:33 + tt],
                                                       in1=gbuf2, op0=ALU.mult, op1=ALU.mult),
                      reads=[xr, R_st2t[tt], R_gb2], writes=[yr])
            if final:
                S.dma("sp", ch_out, lambda e: e.dma_start(out=out_d[ts(tt, 128), :], in_=ya), reads=[yr])

        def o_back(tt):
            ya, yr = yo[tt % 2]
            for half in range(2):
                b = 6 + half
                for j in range(4):
                    dc = half * 4 + j
                    S.pe(lambda e, b=b, j=j, dc=dc: e.transpose(out=bank(b)[:, ts(j, 128)], in_=ya[:, ts(dc, 128)],
                                                               identity=ident),
                         reads=[yr, R_cf], writes=[PB[b]])
                S.dve(lambda e, b=b, half=half: e.tensor_copy(
                    out=hT[:, half * 4:half * 4 + 4, ts(tt, 128)],
                    in_=bank(b).rearrange("p (j t) -> p j t", j=4)), reads=[PB[b]], writes=[R_hT[tt // 4]])

        o_load(0)
        o_load(1)
        for tt in range(18):
            if tt < 16:
                o_front(tt)
            if 0 <= tt - 1 < 16:
                o_mid(tt - 1)
            if fuse_next and 0 <= tt - 2 < 16:
                o_back(tt - 2)
        release_group()
        release_group()

        return False

    for li, L in enumerate(layers):
        if layer_body(li, L):
            break

    S.emit(final_chans=[ch_out, ch_dbg])
    es.close()
    return nc


def _consts():
    c = np.zeros((128, CW), np.float32)
    j = np.arange(128)[:, None]
    q = np.arange(128)[None, :]
    c[:, 0:128] = np.eye(128, dtype=np.float32)
    c[:, 128:256] = np.where(j >= q, -8.0, 0.0)
    c[:, 256:384] = np.where(j < q, 1.0, 0.0)
    c[:, 384:512] = np.where(j <= q, 1.0, 0.0)
    rm = np.ones((512,), np.float32)
    rm[0::128] = 0.0
    c[:, 512:1024] = rm[None, :]
    c[:, 1024:1040] = (1.0 / np.arange(1, 17, dtype=np.float32))[None, :]
    return c


def _fm8(w):
    C = w.shape[1]
    return np.ascontiguousarray(w.reshape(8, 128, C).transpose(1, 0, 2)).reshape(128, 8 * C)


def _fm4(w):
    C = w.shape[1]
    return np.ascontiguousarray(w.reshape(4, 128, C).transpose(1, 0, 2)).reshape(128, 4 * C)


def _pack(norm_g, w_in, pool_w, pool_scale, hgrn_lb, hgrn_norm_g, w_branch, w_out, final_g):
    wg = np.empty((DEPTH * NGRP, 128, 4096), np.float32)
    for l in range(DEPTH):
        W = w_in[l]
        groups = []
        for hp in range(4):
            cols = np.concatenate([np.arange(hp * 128, hp * 128 + 128) + off for off in (0, 512, 1024, 1536)])
            groups.append(_fm8(W[:, cols]))
        groups.append(_fm8(W[:, 5120:5632])); groups.append(_fm8(W[:, 5632:6144]))
        groups.append(_fm4(w_branch[l, 0]))
        groups.append(_fm8(W[:, 2048:2560])); groups.append(_fm8(W[:, 2560:3072]))
        groups.append(_fm8(W[:, 6144:6656])); groups.append(_fm8(W[:, 6656:7168]))
        groups.append(_fm4(w_branch[l, 1]))
        groups.append(_fm8(W[:, 4096:4608])); groups.append(_fm8(W[:, 4608:5120]))
        groups.append(_fm8(W[:, 3584:4096])); groups.append(_fm8(W[:, 3072:3584]))
        groups.append(_fm8(W[:, 7168:7680])); groups.append(_fm8(W[:, 7680:8192]))
        groups.append(_fm4(w_branch[l, 2]))
        groups.append(_fm8(w_out[l][:, 0:512])); groups.append(_fm8(w_out[l][:, 512:1024]))
        assert len(groups) == NGRP
        for g, a in enumerate(groups):
            wg[l * NGRP + g] = a
    pw = np.ascontiguousarray(pool_w.transpose(0, 2, 1, 3)).reshape(DEPTH, 128, 512)
    vec = np.empty((128, DEPTH * 12), np.float32)
    for l in range(DEPTH):
        vec[:, l * 12 + 0:l * 12 + 4] = pool_scale[l].reshape(4, 128).T
        vec[:, l * 12 + 4:l * 12 + 8] = hgrn_lb[l].reshape(4, 128).T
        vec[:, l * 12 + 8:l * 12 + 12] = hgrn_norm_g[l].reshape(4, 128).T
    gb = np.empty((DEPTH + 1, 128, DM), np.float32)
    for l in range(DEPTH):
        gb[l] = np.broadcast_to(norm_g[l][None, :], (128, DM))
    gb[DEPTH] = np.broadcast_to(final_g[None, :], (128, DM))
    return {"wgrp": wg, "poolw": pw, "vecs": vec, "gb": gb, "cst": _consts()}


_NC_CACHE = {}


def kernel(x, norm_g, w_in, pool_w, pool_scale, hgrn_lb, hgrn_norm_g, w_branch, w_out, final_g):
    f = lambda a: np.ascontiguousarray(np.asarray(a, dtype=np.float32))
    x = f(x)
    shared = _pack(f(norm_g), f(w_in), f(pool_w), f(pool_scale), f(hgrn_lb), f(hgrn_norm_g), f(w_branch), f(w_out), f(final_g))
    if "nc" not in _NC_CACHE:
        _NC_CACHE["nc"] = build_program()
    nc = _NC_CACHE["nc"]
    in_maps = [dict(shared, x=x[b]) for b in range(NCORES)]
    res = run_bass_kernel_spmd(nc, in_maps, core_ids=list(range(NCORES)))
    return np.stack([np.asarray(r["out"], dtype=np.float32) for r in res.results], axis=0)
```

```python
import numpy as np
from contextlib import ExitStack
import concourse.bass as bass
import concourse.mybir as mybir
from concourse.bass_utils import run_bass_kernel_spmd

F32 = mybir.dt.float32
BF16 = mybir.dt.bfloat16
AF = mybir.ActivationFunctionType
ALU = mybir.AluOpType

SEQ = 2048
DM = 1024
DEPTH = 2
NCORES = 8
EPS = 1e-6
NGRP = 21
NSLOT = 5
CW = 4 * 128 + 512 + 16

ENGS = ("pe", "act", "dve", "pool", "sp")
SEM_CHUNK = 4000
DMA_CHUNK = 1000


class Res:
    __slots__ = ("name", "psum", "last_w", "readers", "strict")

    def __init__(self, name, psum=False, strict=False):
        self.name = name
        self.psum = psum
        self.last_w = None
        self.readers = []
        self.strict = strict


class Op:
    __slots__ = ("eng", "fn", "deps", "signal", "sig_idx", "chan", "chan_idx", "name")

    def __init__(self, eng, fn, chan=None, name=""):
        self.eng = eng
        self.fn = fn
        self.deps = []
        self.signal = False
        self.sig_idx = -1
        self.chan = chan
        self.chan_idx = -1
        self.name = name


class Chan:
    __slots__ = ("name", "n", "sems", "last", "serial")

    def __init__(self, name, serial=True):
        self.name = name
        self.n = 0
        self.sems = []
        self.last = None
        self.serial = serial


class Sched:
    def __init__(self, nc):
        self.nc = nc
        self.ops = {e: [] for e in ENGS}
        self.chans = []
        self.last = {e: None for e in ENGS}
        self.pending_barrier = {e: None for e in ENGS}

    def chan(self, name, serial=True):
        c = Chan(name, serial)
        self.chans.append(c)
        return c

    def add(self, eng, fn, reads=(), writes=(), chan=None, name=""):
        op = Op(eng, fn, chan, name)
        raw = set()
        other = set()
        strict = set()
        for r in reads:
            if r.last_w is not None:
                raw.add(r.last_w)
            if r.psum:
                for o in r.readers:
                    if o.eng != eng:
                        other.add(o)
            r.readers.append(op)
        for w in writes:
            if w.last_w is not None:
                other.add(w.last_w)
                if w.strict:
                    strict.add(w.last_w)
            for o in w.readers:
                if o is not op:
                    other.add(o)
                    if w.strict:
                        strict.add(o)
            w.last_w = op
            w.readers = []
        pb = self.pending_barrier[eng]
        if pb is not None:
            for o in pb:
                raw.add(o)
            self.pending_barrier[eng] = None
        if chan is not None and chan.serial and chan.last is not None:
            raw.add(chan.last)
        deps = []
        seen = set()
        for d in list(raw | other):
            if d is op:
                continue
            israw = d in raw
            if d.chan is not None and not d.chan.serial:
                d = d.chan.last
            if id(d) in seen:
                continue
            seen.add(id(d))
            if d.chan is None and d.eng == eng:
                if eng == "pe" or eng == "sp":
                    continue
                if not israw and d not in strict:
                    continue
            deps.append(d)
        op.deps = deps
        for d in deps:
            d.signal = True
        if chan is not None:
            op.chan_idx = chan.n
            chan.n += 1
            chan.last = op
        self.ops[eng].append(op)
        self.last[eng] = op
        return op

    def pe(self, fn, reads=(), writes=()):
        return self.add("pe", fn, reads, writes)

    def act(self, fn, reads=(), writes=()):
        return self.add("act", fn, reads, writes)

    def dve(self, fn, reads=(), writes=()):
        return self.add("dve", fn, reads, writes)

    def pool(self, fn, reads=(), writes=()):
        return self.add("pool", fn, reads, writes)

    def dma(self, eng, chan, fn, reads=(), writes=()):
        return self.add(eng, fn, reads, writes, chan=chan)

    def frontier(self):
        fr = [o for o in self.last.values() if o is not None]
        fr += [c.last for c in self.chans if c.last is not None]
        return fr

    def barrier(self, fr=None):
        if fr is None:
            fr = self.frontier()
        for e in ENGS:
            cur = self.pending_barrier[e]
            self.pending_barrier[e] = list(fr) + (cur if cur else [])

    def emit(self, final_chans=()):
        nc = self.nc
        with ExitStack() as es:
            eng_sems = {}
            for e in ENGS:
                j = 0
                for op in self.ops[e]:
                    if op.chan is None and op.signal:
                        op.sig_idx = j
                        j += 1
                nsem = (j + SEM_CHUNK - 1) // SEM_CHUNK
                eng_sems[e] = [es.enter_context(nc.semaphore(f"s_{e}_{i}")) for i in range(nsem)]
            for ci, c in enumerate(self.chans):
                nsem = (c.n + DMA_CHUNK - 1) // DMA_CHUNK
                c.sems = [es.enter_context(nc.semaphore(f"c_{ci}_{i}")) for i in range(nsem)]

            def sem_of(op):
                if op.chan is not None:
                    return (op.chan.sems[op.chan_idx // DMA_CHUNK],
                            16 * (op.chan_idx % DMA_CHUNK + 1))
                return (eng_sems[op.eng][op.sig_idx // SEM_CHUNK],
                        op.sig_idx % SEM_CHUNK + 1)

            block = es.enter_context(nc.Block())
            handles = {"pe": block.tensor, "act": block.scalar, "dve": block.vector,
                       "pool": block.gpsimd, "sp": block.sync}

            def make(e):
                def body(eng):
                    waited = {}
                    for op in self.ops[e]:
                        need = {}
                        for d in op.deps:
                            s, v = sem_of(d)
                            k = id(s)
                            if waited.get(k, 0) >= v:
                                continue
                            if k not in need or need[k][1] < v:
                                need[k] = (s, v)
                        for k, (s, v) in need.items():
                            eng.wait_ge(s, v)
                            waited[k] = v
                        ins = op.fn(eng)
                        if op.chan is not None:
                            s, _ = sem_of(op)
                            ins.then_inc(s, 16)
                        elif op.signal:
                            s, _ = sem_of(op)
                            ins.then_inc(s, 1)
                    if e == "sp":
                        for c in final_chans:
                            if c.last is not None:
                                s, v = sem_of(c.last)
                                eng.wait_ge(s, v)
                return body

            for e in ENGS:
                handles[e](make(e))


def ts(i, n):
    return slice(i * n, (i + 1) * n)


def build_program(layers=(0, 1), first=True, last=True, dbg=None, stop_after=None):
    nc = bass.Bass("TRN2", target_bir_lowering=False)
    x_in = nc.dram_tensor("x", [SEQ, DM], F32, kind="ExternalInput").ap()
    wgrp = nc.dram_tensor("wgrp", [DEPTH * NGRP, 128, 4096], F32, kind="ExternalInput").ap()
    poolw_d = nc.dram_tensor("poolw", [DEPTH, 128, 512], F32, kind="ExternalInput").ap()
    vecs_d = nc.dram_tensor("vecs", [128, DEPTH * 12], F32, kind="ExternalInput").ap()
    gb_d = nc.dram_tensor("gb", [DEPTH + 1, 128, DM], F32, kind="ExternalInput").ap()
    cst_d = nc.dram_tensor("cst", [128, CW], F32, kind="ExternalInput").ap()
    out_d = nc.dram_tensor("out", [SEQ, DM], F32, kind="ExternalOutput").ap()
    hscr = nc.dram_tensor("hscr", [SEQ, DM], F32, kind="Internal").ap()
    dbg_out = {}
    if dbg:
        for name, shape in dbg.items():
            dbg_out[name] = nc.dram_tensor("dbg_" + name, list(shape), F32, kind="ExternalOutput").ap()

    S = Sched(nc)
    es = ExitStack()

    def sb(name, shape, dt):
        return es.enter_context(nc.sbuf_tensor(name, shape, dt))

    cf = sb("cf", [128, CW], F32)
    cb = sb("cb", [128, 6, 128], BF16)
    mi4 = sb("mi4", [128, 512], BF16)
    vecs = sb("vecs_s", [128, DEPTH * 12], F32)
    lbv = sb("lbv", [128, DEPTH * 4], F32)
    l1m = sb("l1m", [128, DEPTH * 4], F32)
    lbt = sb("lbt", [128, 16], F32)
    hT = sb("hT", [128, 8, SEQ], BF16)
    mg = sb("mg", [128, 8, SEQ], BF16)
    obT = sb("obT", [128, 4, SEQ], BF16)
    wsl = [sb(f"wsl{i}", [128, 4096], BF16) for i in range(NSLOT)]
    poolw = sb("poolw_s", [128, 512], BF16)
    NF = 9760
    NB = 16896
    arf = sb("arf", [128, NF], F32)
    mt = sb("mt", [128, 2048], F32)
    arb = sb("arb", [128, NB], BF16)
    ps = es.enter_context(nc.psum_tensor("ps", [128, 8 * 512], F32))

    def bank(i):
        return ps[:, i * 512:(i + 1) * 512]

    PB = [Res(f"pb{i}", psum=True) for i in range(8)]
    R_cf = Res("cf"); R_cb = Res("cb"); R_vecs = Res("vecs"); R_lb = Res("lb")
    R_hT = [Res(f"hT{tg}") for tg in range(4)]
    R_mg = [[Res(f"mg{dc}_{tg}") for tg in range(4)] for dc in range(8)]
    R_ob = [[Res(f"ob{c}_{tg}") for tg in range(4)] for c in range(4)]
    R_ws = [Res(f"ws{i}") for i in range(NSLOT)]
    R_pw = Res("poolw")
    R_mt = [Res(f"mt{i}", strict=True) for i in range(4)]
    ch_ws = [S.chan(f"ws{i}") for i in range(NSLOT)]
    ch_misc = S.chan("misc")
    ch_pw = S.chan("pw")
    ch_out = S.chan("out", serial=False)
    ch_dbg = S.chan("dbg", serial=False)

    ident = cf[:, 0:128]
    TRI, ONESN8, MASKS, ONES, MASKI = 0, 1, 2, 3, 4
    rmask = cf[:, 512:1024]
    invcnt = cf[:, 1024:1040]

    class Arena:
        def __init__(self):
            self.f = 0
            self.b = 0

        def reset(self):
            self.f = 0
            self.b = 0

        def F(self, n, name):
            a = arf[:, self.f:self.f + n]
            self.f += n
            assert self.f <= NF, (name, self.f)
            return a, Res(name)

        def B(self, n, name):
            a = arb[:, self.b:self.b + n]
            self.b += n
            assert self.b <= NB, (name, self.b)
            return a, Res(name)

    AR = Arena()

    glist = [(l, g) for l in layers for g in range(NGRP)]
    gstate = {"next": 0}

    def issue_group(after=()):
        n = gstate["next"]
        if n >= len(glist):
            return
        l, g = glist[n]
        slot = n % NSLOT
        S.dma("pool", ch_ws[slot],
              lambda e, l=l, g=g, slot=slot: e.dma_start(out=wsl[slot][:], in_=wgrp[l * NGRP + g]),
              reads=list(after), writes=[R_ws[slot]])
        gstate["next"] = n + 1

    gpos = {"cur": 0}

    def take_group():
        n = gpos["cur"]
        gpos["cur"] = n + 1
        return n % NSLOT

    def release_group():
        issue_group()

    def w8(slot):
        return wsl[slot][:].rearrange("p (k c) -> p k c", k=8)

    def w4(slot):
        return wsl[slot][:].rearrange("p (k c) -> p k c", k=4)

    bank_rr = {"i": 0}

    def nb(lo=0, hi=8):
        i = bank_rr["i"]
        bank_rr["i"] = (i + 1) % (hi - lo)
        return lo + i % (hi - lo)

    def proj_fm(slot, c0, tg, b, nk=8):
        wv = w8(slot)
        for kc in range(nk):
            S.pe(lambda e, kc=kc: e.matmul(bank(b), lhsT=wv[:, kc, c0:c0 + 128], rhs=hT[:, kc, ts(tg, 512)],
                                           start=(kc == 0), stop=(kc == nk - 1)),
                 reads=[R_ws[slot], R_hT[tg]], writes=[PB[b]])

    def dump(name, ap, res):
        if dbg and name in dbg:
            S.dma("pool", ch_dbg, lambda e: e.dma_start(out=dbg_out[name], in_=ap), reads=res)

    S.dma("sp", ch_misc, lambda e: e.dma_start(out=cf[:], in_=cst_d), writes=[R_cf])
    S.dma("sp", ch_misc, lambda e: e.dma_start(out=vecs[:], in_=vecs_d), writes=[R_vecs])
    for i, c0_ in ((0, 128), (2, 256), (4, 384)):
        S.dve(lambda e, i=i, c0_=c0_: e.tensor_copy(out=cb[:, i, :], in_=cf[:, c0_:c0_ + 128]), reads=[R_cf], writes=[R_cb])
    S.dve(lambda e: e.memset(cb[:, 1, :], -8.0), writes=[R_cb])
    S.dve(lambda e: e.memset(cb[:, 3, :], 1.0), writes=[R_cb])
    for i in range(4):
        S.dve(lambda e, i=i: e.tensor_copy(out=mi4[:, ts(i, 128)], in_=cf[:, 384:512]), reads=[R_cf], writes=[R_cb])
    S.dve(lambda e: e.tensor_copy(out=cb[:, 5, :], in_=cf[:, 0:128]), reads=[R_cf], writes=[R_cb])
    cbi = cb[:, 5, :]
    V_PS, V_LB, V_HG = 0, 4, 8
    S.dve(lambda e: e.memset(lbv[:], 0.0), writes=[R_lb])
    S.dve(lambda e: e.memset(l1m[:], 0.0), writes=[R_lb])
    S.dve(lambda e: e.tensor_tensor(out=lbt[:, 0:4], in0=vecs[:, V_LB:V_LB + 4], in1=vecs[:, 12 + V_LB:12 + V_LB + 4],
                                    op=ALU.subtract), reads=[R_vecs], writes=[R_lb])
    S.act(lambda e: e.activation(out=lbt[:, 4:8], in_=lbt[:, 0:4], func=AF.Exp), reads=[R_lb], writes=[R_lb])
    S.dve(lambda e: e.tensor_scalar(out=lbt[:, 8:12], in0=lbt[:, 4:8], scalar1=1.0, scalar2=None, op0=ALU.add), reads=[R_lb], writes=[R_lb])
    S.dve(lambda e: e.reciprocal(out=lbv[:, 4:8], in_=lbt[:, 8:12]), reads=[R_lb], writes=[R_lb])
    S.dve(lambda e: e.tensor_tensor(out=lbt[:, 12:16], in0=lbt[:, 4:8], in1=lbv[:, 4:8], op=ALU.mult),
          reads=[R_lb], writes=[R_lb])
    S.act(lambda e: e.activation(out=l1m[:, 4:8], in_=lbt[:, 12:16], func=AF.Ln), reads=[R_lb], writes=[R_lb])

    n_layers = len(layers)
    chx_box = {}

    def layer_body(li, L):
        is_first_layer = (li == 0)
        is_last_layer = (li == n_layers - 1)
        src = x_in if (is_first_layer and first) else hscr
        if is_first_layer and not first:
            src = x_in
        VO = L * 12

        mtmp = {"th": [(mt[:, 512 * i:512 * (i + 1)], R_mt[i]) for i in range(2)],
                "tm": [(mt[:, 1024 + 512 * i:1024 + 512 * (i + 1)], R_mt[2 + i]) for i in range(2)]}
        AR.reset()
        do_p0 = (li == 0)
        xt = [AR.F(1024, f"xt{i}") for i in range(2)]
        xn = [AR.F(1024, f"xn{i}") for i in range(2)]
        sq, R_sq = AR.F(1024, "sq")
        gbuf, R_gb = AR.F(1024, "gbuf")
        st, _ = AR.F(64, "st")
        R_stt = [Res(f"st{i}") for i in range(16)]
        if li == 0:
            chx_box["c"] = [S.chan("x0"), S.chan("x1"), S.chan("x2"), S.chan("x3")]
        ch_x = chx_box["c"]
        if do_p0:
            S.dma("sp", ch_misc, lambda e, L=L: e.dma_start(out=gbuf, in_=gb_d[L]), writes=[R_gb])
            S.dma("pool", ch_pw, lambda e, L=L: e.dma_start(out=poolw[:], in_=poolw_d[L]), writes=[R_pw])
        def p0_front(tt):
            k = tt % 2
            xa, xr = xt[k]
            na, nr = xn[k]
            S.dma("sp", ch_x[k], lambda e: e.dma_start(out=xa, in_=src[ts(tt, 128), :]), writes=[xr])
            S.act(lambda e: e.activation(out=sq, in_=xa, func=AF.Square, accum_out=st[:, tt:tt + 1]),
                  reads=[xr], writes=[R_sq, R_stt[tt]])
            S.act(lambda e: e.activation(out=st[:, 16 + tt:17 + tt], in_=st[:, tt:tt + 1], func=AF.Ln,
                                         scale=1.0 / DM, bias=EPS), reads=[R_stt[tt]], writes=[R_stt[tt]])
            S.act(lambda e: e.activation(out=st[:, 32 + tt:33 + tt], in_=st[:, 16 + tt:17 + tt], func=AF.Exp,
                                         scale=-0.5), reads=[R_stt[tt]], writes=[R_stt[tt]])
            S.dve(lambda e: e.scalar_tensor_tensor(out=na, in0=xa, scalar=st[:, 32 + tt:33 + tt],
                                                   in1=gbuf, op0=ALU.mult, op1=ALU.mult),
                  reads=[xr, R_stt[tt], R_gb], writes=[nr])

        def p0_back(tt):
            na, nr = xn[tt % 2]
            for half in range(2):
                b = nb(0, 4)
                for j in range(4):
                    dc = half * 4 + j
                    S.pe(lambda e, b=b, j=j, dc=dc: e.transpose(out=bank(b)[:, ts(j, 128)], in_=na[:, ts(dc, 128)],
                                                               identity=ident),
                         reads=[nr, R_cf], writes=[PB[b]])
                S.dve(lambda e, b=b, half=half: e.tensor_copy(
                    out=hT[:, half * 4:half * 4 + 4, ts(tt, 128)],
                    in_=bank(b).rearrange("p (j t) -> p j t", j=4)), reads=[PB[b]], writes=[R_hT[tt // 4]])

        for tt in range(17 if do_p0 else 0):
            if tt < 16:
                p0_front(tt)
            if tt >= 1:
                p0_back(tt - 1)
            if tt == 1 and li == 0:
                for _ in range(2):
                    issue_group(after=[xt[1][1]])
        if li == 0:
            for _ in range(NSLOT - 2):
                issue_group(after=[xt[1][1]] if do_p0 else ())
        dump("hT", hT[:], [r for r in R_hT])
        if stop_after == "p0":
            return True

        fr_pre = S.frontier()
        AR.reset()
        spb = [AR.B(1024, f"sp{i}") for i in range(3)]
        racc, R_racc = AR.B(1024, "racc")
        Ab = [AR.B(1024, f"A{i}") for i in range(3)]
        eb = [AR.F(1024, f"e{i}") for i in range(2)]
        mgflat = mg[:].rearrange("p a b -> p (a b)")
        R_mgall = [r for rr in R_mg for r in rr]
        bsets = []
        for si in range(2):
            d = {}
            for k_, nm in enumerate(("qT", "kT", "vS", "zs")):
                if si == 0:
                    d[nm], d["R_" + nm] = AR.B(2048, f"{nm}0")
                else:
                    d[nm] = mgflat[:, k_ * 2048:(k_ + 1) * 2048]
                    d["R_" + nm] = Res(f"{nm}1", strict=True)
            d["vS3"] = d["vS"].rearrange("p (t c) -> p t c", t=16)
            bsets.append(d)
        IPB = 7
        OB = 6

        def inproj_units(hp, bs, banks=(IPB,)):
            slot = take_group()
            wv = w8(slot)
            units = []
            bsel = {"i": 0}

            def nbk():
                bsel["i"] += 1
                return banks[bsel["i"] % len(banks)]

            def uq(tg, c0, dst, rdst):
                def f():
                    bk = nbk()
                    proj_fm(slot, c0, tg, bk)
                    S.dve(lambda e: e.tensor_copy(out=dst[:, ts(tg, 512)], in_=bank(bk)), reads=[PB[bk]], writes=[rdst])
                return f

            def uv(t4):
                def f():
                    bk = nbk()
                    for j in range(4):
                        tt = t4 * 4 + j
                        for kc in range(8):
                            S.pe(lambda e, j=j, tt=tt, kc=kc: e.matmul(
                                bank(bk)[:, ts(j, 128)], lhsT=hT[:, kc, ts(tt, 128)], rhs=wv[:, kc, 256:384],
                                start=(kc == 0), stop=(kc == 7)), reads=[R_ws[slot], R_hT[t4]], writes=[PB[bk]])
                    S.dve(lambda e: e.tensor_copy(out=bs["vS3"][:, t4 * 4:t4 * 4 + 4, :],
                                                  in_=bank(bk).rearrange("p (j c) -> p j c", j=4)),
                          reads=[PB[bk]], writes=[bs["R_vS"]])
                return f

            def uz():
                for tg in range(4):
                    bk = nbk()
                    proj_fm(slot, 384, tg, bk)
                    S.act(lambda e, tg=tg, bk=bk: e.activation(out=bs["zs"][:, ts(tg, 512)], in_=bank(bk), func=AF.Silu),
                          reads=[PB[bk]], writes=[bs["R_zs"]])
                release_group()

            for tg in range(4):
                units.append(uq(tg, 0, bs["qT"], bs["R_qT"]))
            for tg in range(4):
                units.append(uq(tg, 128, bs["kT"], bs["R_kT"]))
            for t4 in range(4):
                units.append(uv(t4))
            units.append(uz)
            return units

        def h3(a_):
            return a_.rearrange("p (h c) -> p h c", h=2)

        def pair(sl):
            return ps[:, (2 * sl) * 512:(2 * sl + 2) * 512].rearrange("p (h c) -> p h c", h=2)

        mask2 = cb[:, MASKS:MASKS + 1, :].broadcast_to([128, 2, 128])
        racc3 = h3(racc)
        items = [(QS, kb) for QS in range(4) for kb in range(4 * (QS + 1) - 1, -1, -1)]
        n_it = len(items)

        def attn_pair(hp, bs, filler):
            qT, kT, vS3, zs = bs["qT"], bs["kT"], bs["vS3"], bs["zs"]
            R_q, R_k, R_v, R_z = bs["R_qT"], bs["R_kT"], bs["R_vS"], bs["R_zs"]

            def meta(i):
                QS, kb = items[i]
                nkb = 4 * (QS + 1)
                j = kb - 4 * QS
                c0 = 128 * j if j >= 0 else 0
                return QS, kb, nkb, c0, (j >= 0), i % 3

            def stA(i):
                QS, kb, nkb, c0, diag, sl = meta(i)
                q0 = QS * 512
                for h in range(2):
                    hs = slice(64 * h, 64 * h + 64)
                    S.pe(lambda e, h=h, hs=hs: e.matmul(
                        bank(2 * sl + h)[:, c0:512], lhsT=kT[hs, ts(kb, 128)], rhs=qT[hs, q0 + c0:q0 + 512],
                        start=True, stop=True), reads=[R_k, R_q], writes=[PB[2 * sl + h]])

            def stB(i):
                QS, kb, nkb, c0, diag, sl = meta(i)
                ea, er = eb[i % 2]
                spa, spr = spb[i % 3]
                S.act(lambda e: e.activation(out=h3(ea)[:, :, c0:512], in_=pair(sl)[:, :, c0:512], func=AF.Exp, scale=0.125),
                      reads=[PB[2 * sl], PB[2 * sl + 1]], writes=[er])
                S.act(lambda e: e.activation(out=h3(spa)[:, :, c0:512], in_=h3(ea)[:, :, c0:512], func=AF.Ln, bias=1.0, scale=1.0),
                      reads=[er], writes=[spr])
                if diag:
                    S.dve(lambda e: e.tensor_tensor(out=h3(spa)[:, :, c0:c0 + 128], in0=h3(spa)[:, :, c0:c0 + 128],
                                                    in1=mask2, op=ALU.mult), reads=[spr, R_cb], writes=[spr])

            def stC(i):
                QS, kb, nkb, c0, diag, sl = meta(i)
                spa, spr = spb[i % 3]
                if kb == nkb - 1:
                    S.pool(lambda e: e.memset(racc, 0.0), writes=[R_racc])
                for h in range(2):
                    S.pe(lambda e, h=h: e.matmul(
                        bank(2 * sl + h)[:, c0:512], lhsT=cb[:, TRI, :], rhs=h3(spa)[:, h, c0:512], start=False, stop=True,
                        skip_group_check=True), reads=[R_cb, spr], writes=[PB[2 * sl + h]])
                    if kb < nkb - 1:
                        S.pe(lambda e, h=h: e.matmul(
                            bank(2 * sl + h)[:, c0:512], lhsT=cb[:, ONESN8, :], rhs=racc3[:, h, c0:512], start=False, stop=True,
                            skip_group_check=True), reads=[R_cb, R_racc], writes=[PB[2 * sl + h]])
                if kb > 0:
                    S.dve(lambda e: e.tensor_tensor(out=racc3[:, :, c0:512], in0=racc3[:, :, c0:512],
                                                    in1=h3(spa)[:, :, c0:512], op=ALU.add),
                          reads=[R_racc, spr], writes=[R_racc])

            def stD(i):
                QS, kb, nkb, c0, diag, sl = meta(i)
                Aa, Ar = Ab[i % 3]
                S.act(lambda e: e.activation(out=h3(Aa)[:, :, c0:512], in_=pair(sl)[:, :, c0:512], func=AF.Exp, scale=0.125),
                      reads=[PB[2 * sl], PB[2 * sl + 1]], writes=[Ar])
                if diag:
                    S.dve(lambda e: e.tensor_tensor(out=h3(Aa)[:, :, c0:c0 + 128], in0=h3(Aa)[:, :, c0:c0 + 128],
                                                    in1=mask2, op=ALU.mult), reads=[Ar, R_cb], writes=[Ar])

            def stE(i):
                QS, kb, nkb, c0, diag, sl = meta(i)
                Aa, Ar = Ab[i % 3]
                for h in range(2):
                    hs = slice(64 * h, 64 * h + 64)
                    S.pe(lambda e, h=h, hs=hs: e.matmul(
                        bank(OB)[hs, c0:512], lhsT=vS3[:, kb, hs], rhs=h3(Aa)[:, h, c0:512],
                        start=(kb == nkb - 1), stop=(kb == 0), skip_group_check=True),
                         reads=[R_v, Ar], writes=[PB[OB]])
                if kb == 0:
                    S.dve(lambda e: e.tensor_tensor(out=obT[:, hp, ts(QS, 512)], in0=bank(OB), in1=zs[:, ts(QS, 512)],
                                                    op=ALU.mult), reads=[PB[OB], R_z], writes=[R_ob[hp][QS]])

            fill = list(filler)
            for t in range(n_it + 3):
                if t < n_it:
                    stA(t)
                if 0 <= t - 1 < n_it:
                    stB(t - 1)
                if 0 <= t - 2 < n_it:
                    stC(t - 2)
                    stD(t - 2)
                if 0 <= t - 3 < n_it:
                    stE(t - 3)
                if fill and t % 2 == 1:
                    fill.pop(0)()
            while fill:
                fill.pop(0)()

        first_units = inproj_units(0, bsets[0], banks=tuple(range(8)))
        for u in first_units:
            u()
        if dbg:
            b0 = bsets[0]
            dump("qT0", b0["qT"], [b0["R_qT"]]); dump("kT0", b0["kT"], [b0["R_kT"]])
            dump("vS0", b0["vS"], [b0["R_vS"]]); dump("zs0", b0["zs"], [b0["R_zs"]])
        S.barrier(fr_pre)
        for hp in range(4):
            filler = inproj_units(hp + 1, bsets[(hp + 1) % 2]) if hp < 3 else []
            attn_pair(hp, bsets[hp % 2], filler)
        fr_A = S.frontier()
        dump("obA", obT[:], [r for rr in R_ob for r in rr])
        if stop_after == "A":
            return True

        def merge(br, barrier=True):
            if barrier:
                S.barrier()
            th, tm = mtmp["th"], mtmp["tm"]
            extra_w = [bsets[1][k_] for k_ in ("R_qT", "R_kT", "R_vS", "R_zs")] if br == 0 else []
            gslots = [take_group(), take_group()]
            wslot = take_group()
            wbv = w4(wslot)
            k = 0
            for dc in range(8):
                gs = gslots[dc // 4]
                for tg in range(4):
                    gbk = nb(0, 4)
                    proj_fm(gs, (dc % 4) * 128, tg, gbk)
                    pbk = 4 + nb(0, 4)
                    for kc in range(4):
                        S.pe(lambda e, pbk=pbk, kc=kc, dc=dc, tg=tg: e.matmul(
                            bank(pbk), lhsT=wbv[:, kc, ts(dc, 128)], rhs=obT[:, kc, ts(tg, 512)],
                            start=(kc == 0), stop=(kc == 3)), reads=[R_ws[wslot], R_ob[kc][tg]], writes=[PB[pbk]])
                    (tha, thr) = th[k % 2]
                    (tma, tmr) = tm[k % 2]
                    k += 1
                    S.act(lambda e, gbk=gbk, tha=tha: e.activation(out=tha, in_=bank(gbk), func=AF.Tanh, scale=0.5),
                          reads=[PB[gbk]], writes=[thr])
                    if br == 0:
                        S.dve(lambda e, pbk=pbk, tha=tha, dc=dc, tg=tg: e.scalar_tensor_tensor(
                            out=mg[:, dc, ts(tg, 512)], in0=tha, scalar=1.0, in1=bank(pbk), op0=ALU.add, op1=ALU.mult),
                              reads=[thr, PB[pbk]], writes=[R_mg[dc][tg]] + extra_w)
                    else:
                        S.dve(lambda e, pbk=pbk, tha=tha, tma=tma: e.scalar_tensor_tensor(
                            out=tma, in0=tha, scalar=1.0, in1=bank(pbk), op0=ALU.add, op1=ALU.mult),
                              reads=[thr, PB[pbk]], writes=[tmr])
                        S.pool(lambda e, tma=tma, dc=dc, tg=tg: e.tensor_tensor(
                            out=mg[:, dc, ts(tg, 512)], in0=mg[:, dc, ts(tg, 512)], in1=tma, op=ALU.add),
                               reads=[tmr, R_mg[dc][tg]], writes=[R_mg[dc][tg]])
                if dc == 3:
                    release_group()
            release_group()
            release_group()

        merge(0, barrier=False)
        dump("mgA", mg[:], [r for rr in R_mg for r in rr])
        if stop_after == "MA":
            return True

        S.barrier(fr_A)
        AR.reset()
        ubs = [AR.F(2064, f"ub{i}") for i in range(2)]
        pa, R_pa = AR.F(2064, "pa")
        pb_, R_pbb = AR.F(2064, "pb")
        t16, R_t16 = AR.F(16, "t16")
        dTs = [AR.B(2048, f"dT{i}") for i in range(2)]
        pzss = [AR.B(2048, f"pzs{i}") for i in range(2)]
        su = take_group()
        sz = take_group()
        for ub_, rub_ in ubs:
            S.dve(lambda e, ub_=ub_: e.memset(ub_[:, 0:16], 0.0), writes=[rub_])
        S.dve(lambda e: e.memset(pa[:, 0:16], 0.0), writes=[R_pa])
        S.dve(lambda e: e.memset(pb_[:, 0:16], 0.0), writes=[R_pbb])

        def b_inproj(g):
            ub, R_ub = ubs[g % 2]
            pzs, R_pzs = pzss[g % 2]
            for tg in range(4):
                b = nb(0, 4)
                proj_fm(su, g * 128, tg, b)
                S.act(lambda e, b=b, tg=tg: e.copy(out=ub[:, 16 + tg * 512:16 + (tg + 1) * 512], in_=bank(b)),
                      reads=[PB[b]], writes=[R_ub])
                b = nb(0, 4)
                proj_fm(sz, g * 128, tg, b)
                S.act(lambda e, b=b, tg=tg: e.activation(out=pzs[:, ts(tg, 512)], in_=bank(b), func=AF.Silu),
                      reads=[PB[b]], writes=[R_pzs])

        def b_chain(g):
            w = 2 << g
            ub, R_ub = ubs[g % 2]
            dT, R_dT = dTs[g % 2]
            cur, rcur = ub, R_ub
            bufs = [(pa, R_pa), (pb_, R_pbb)]
            sh = 1
            bi = 0
            while sh < w:
                dst, rdst = bufs[bi % 2]
                bi += 1
                S.dve(lambda e, cur=cur, dst=dst, sh=sh: e.tensor_tensor(out=dst[:, 16:2064], in0=cur[:, 16:2064],
                                                                        in1=cur[:, 16 - sh:2064 - sh], op=ALU.add),
                      reads=[rcur], writes=[rdst])
                cur, rcur = dst, rdst
                sh *= 2
            S.dve(lambda e, cur=cur: e.scalar_tensor_tensor(out=dT, in0=cur[:, 16:2064], scalar=1.0 / w,
                                                           in1=ub[:, 16:2064], op0=ALU.mult, op1=ALU.subtract),
                  reads=[rcur, R_ub], writes=[R_dT])
            wn = min(w, 16)
            S.dve(lambda e, cur=cur: e.tensor_tensor(out=t16[:, 0:wn], in0=cur[:, 16:16 + wn], in1=invcnt[:, 0:wn],
                                                    op=ALU.mult), reads=[rcur, R_cf], writes=[R_t16])
            S.dve(lambda e: e.tensor_tensor(out=dT[:, 0:wn], in0=t16[:, 0:wn], in1=ub[:, 16:16 + wn],
                                            op=ALU.subtract), reads=[R_t16, R_ub], writes=[R_dT])

        def b_out(g):
            dT, R_dT = dTs[g % 2]
            pzs, R_pzs = pzss[g % 2]
            for tg in range(4):
                b = 4 + nb(0, 4)
                S.pe(lambda e, b=b, tg=tg: e.matmul(bank(b), lhsT=poolw[:, ts(g, 128)], rhs=dT[:, ts(tg, 512)],
                                                    start=True, stop=True), reads=[R_pw, R_dT], writes=[PB[b]])
                S.dve(lambda e, b=b, tg=tg: e.scalar_tensor_tensor(
                    out=obT[:, g, ts(tg, 512)], in0=bank(b), scalar=vecs[:, VO + V_PS + g:VO + V_PS + g + 1],
                    in1=pzs[:, ts(tg, 512)], op0=ALU.mult, op1=ALU.mult),
                      reads=[PB[b], R_vecs, R_pzs], writes=[R_ob[g][tg]])

        b_inproj(0)
        for g in range(4):
            if g < 3:
                b_inproj(g + 1)
            b_chain(g)
            b_out(g)
        release_group()
        release_group()
        fr_B = S.frontier()
        dump("obB", obT[:], [r for rr in R_ob for r in rr])
        if stop_after == "B":
            return True
        merge(1, barrier=False)
        if stop_after == "MB":
            return True

        S.barrier(fr_B)
        AR.reset()
        Tsets = [[AR.F(512, f"T{k}_{i}") for i in range(6)] for k in range(2)]
        oTg, R_oTg = AR.F(2048, "oTg")
        rs = [AR.F(512, f"rs{i}") for i in range(2)]
        Sf, R_Sf = AR.F(512, "Sf")
        ebl, R_ebl = AR.F(16, "ebl")
        qt, R_qt = AR.B(2048, "qt")
        kt, R_kt = AR.B(2048, "kt")
        qp, R_qp = AR.B(2048, "qp")
        kpTs = [AR.B(512, f"kpT{i}") for i in range(2)]
        kp, R_kp = AR.B(2048, "kp")
        Vb, R_Vb = AR.B(2048, "Vb")
        q2, R_q2 = AR.B(1024, "q2")
        q2v = q2.rearrange("p (h c j) -> p h c j", h=4, c=4)
        v8 = lambda a: a.rearrange("p (c j) -> p c j", c=8)
        zsg, R_zsg = AR.B(2048, "zsg")
        sqbs = [AR.B(512, f"sqb{i}") for i in range(2)]
        Sb, R_Sb = AR.B(512, "Sb")
        AMs = [AR.B(512, f"AM{i}") for i in range(2)]
        maskI_u = mi4[:].bitcast(mybir.dt.uint16)
        si_, szc, sf_, sq_ = take_group(), take_group(), take_group(), take_group()
        Vb3 = Vb.rearrange("p (t c) -> p t c", t=4)
        kp4 = kp.rearrange("p (h c d) -> p h c d", h=4, c=4)
        S.dve(lambda e: e.memset(Sf, 0.0), writes=[R_Sf])
        S.dve(lambda e: e.memset(Sb, 0.0), writes=[R_Sb])
        v3 = lambda a: a.rearrange("p (c j) -> p c j", c=4)
        def vz_units(tg, Vb3, RVb, zsg, RZs):
            units = []

            def uv(j):
                def f():
                    tt = tg * 4 + j
                    b = 4 + nb(0, 4)
                    wv = w8(si_)
                    for kc in range(8):
                        S.pe(lambda e, kc=kc: e.matmul(bank(b), lhsT=hT[:, kc, ts(tt, 128)], rhs=wv[:, kc, :],
                                                       start=(kc == 0), stop=(kc == 7)),
                             reads=[R_ws[si_], R_hT[tg]], writes=[PB[b]])
                    S.dve(lambda e: e.tensor_copy(out=Vb3[:, j, :], in_=bank(b)), reads=[PB[b]], writes=list(RVb))
                return f

            def uz():
                for h in range(4):
                    bz = 4 + nb(0, 4)
                    proj_fm(szc, h * 128, tg, bz)
                    S.act(lambda e, bz=bz, h=h: e.activation(out=zsg[:, ts(h, 512)], in_=bank(bz), func=AF.Silu),
                          reads=[PB[bz]], writes=list(RZs))

            for j in range(4):
                units.append(uv(j))
            units.append(uz)
            return units

        def c_block(tg, Vb3, RVb, zsg, RZs, fill, prev_rms):
                def prep_head(h, tg):
                    hsl = ts(h, 512)
                    lb_ap = lbv[:, L * 4 + h:L * 4 + h + 1]
                    l1_ap = l1m[:, L * 4 + h:L * 4 + h + 1]
                    bx = nb(0, 4)
                    proj_fm(sf_, h * 128, tg, bx)
                    yield
                    bq = nb(0, 4)
                    proj_fm(sq_, h * 128, tg, bq)
                    yield
                    (t0, r0), (t1, r1), (t2, r2), (t3, r3), (t4, r4), (t5, r5) = Tsets[h % 2]
                    kpT, R_kpT = kpTs[h % 2]
                    S.act(lambda e, bx=bx: e.activation(out=t0, in_=bank(bx), func=AF.Exp, scale=-1.0), reads=[PB[bx]], writes=[r0])
                    yield
                    S.act(lambda e: e.activation(out=t1, in_=t0, func=AF.Ln, bias=1.0, scale=1.0), reads=[r0], writes=[r1])
                    yield
                    S.act(lambda e, lb_ap=lb_ap: e.activation(out=t2, in_=t0, func=AF.Ln, bias=1.0, scale=lb_ap),
                          reads=[r0, R_lb], writes=[r2])
                    yield
                    S.dve(lambda e, bx=bx: e.tensor_tensor(out=t3, in0=bank(bx), in1=t1, op=ALU.add), reads=[PB[bx], r1], writes=[r3])
                    yield
                    S.dve(lambda e: e.tensor_tensor(out=t2, in0=t2, in1=t1, op=ALU.subtract), reads=[r2, r1], writes=[r2])
                    yield
                    S.dve(lambda e: e.tensor_tensor_scan(out=t4, data0=rmask, data1=t2, initial=0.0, op0=ALU.mult, op1=ALU.add),
                          reads=[R_cf, r2], writes=[r4])
                    yield
                    S.dve(lambda e: e.tensor_tensor(out=v3(t5), in0=v3(t4), in1=v3(t4)[:, :, 63:64].broadcast_to([128, 4, 128]),
                                                     op=ALU.subtract), reads=[r4], writes=[r5])
                    yield
                    S.act(lambda e: e.activation(out=t0, in_=t5, func=AF.Exp), reads=[r5], writes=[r0])
                    yield
                    S.dve(lambda e, bq=bq, hsl=hsl: e.tensor_tensor(out=qt[:, hsl], in0=bank(bq), in1=t0, op=ALU.mult),
                          reads=[PB[bq], r0], writes=[R_qt])
                    yield
                    S.pool(lambda e: e.tensor_tensor(out=v3(t5)[:, :, 64:128], in0=v3(t4)[:, :, 64:128],
                                                     in1=v3(t4)[:, :, 127:128].broadcast_to([128, 4, 64]), op=ALU.subtract),
                           reads=[r4, r5], writes=[r5])
                    yield
                    S.act(lambda e: e.activation(out=v3(t0)[:, :, 64:128], in_=v3(t5)[:, :, 64:128], func=AF.Exp),
                          reads=[r5, r0], writes=[r0])
                    yield
                    S.dve(lambda e, bq=bq, h=h: e.tensor_tensor(out=q2v[:, h, :, :], in0=bank(bq).rearrange("p (c j) -> p c j", c=4)[:, :, 64:128],
                                                                in1=v3(t0)[:, :, 64:128], op=ALU.mult),
                          reads=[PB[bq], r0], writes=[R_q2])
                    yield
                    S.pool(lambda e: e.tensor_tensor(out=t3, in0=t4, in1=t3, op=ALU.add), reads=[r4, r3], writes=[r3])
                    yield
                    S.pool(lambda e: e.tensor_tensor(out=v8(t1), in0=v8(t4)[:, :, 63:64].broadcast_to([128, 8, 64]), in1=v8(t3),
                                                     op=ALU.subtract), reads=[r4, r3], writes=[r1])
                    yield
                    S.act(lambda e, hsl=hsl, l1_ap=l1_ap: e.activation(out=kt[:, hsl], in_=t1, func=AF.Exp, bias=l1_ap, scale=1.0),
                          reads=[r1, R_lb], writes=[R_kt])
                    yield
                    S.act(lambda e: e.activation(out=t0, in_=t4, func=AF.Exp), reads=[r4], writes=[r0])
                    yield
                    S.dve(lambda e, bq=bq, hsl=hsl: e.tensor_tensor(out=qp[:, hsl], in0=bank(bq), in1=t0, op=ALU.mult),
                          reads=[PB[bq], r0], writes=[R_qp])
                    yield
                    S.pool(lambda e: e.tensor_tensor(out=v3(t2), in0=v3(t4)[:, :, 127:128].broadcast_to([128, 4, 128]), in1=v3(t3),
                                                     op=ALU.subtract), reads=[r4, r3], writes=[r2])
                    yield
                    S.act(lambda e, l1_ap=l1_ap: e.activation(out=kpT, in_=t2, func=AF.Exp, bias=l1_ap, scale=1.0),
                          reads=[r2, R_lb], writes=[R_kpT])
                    yield
                    S.act(lambda e, h=h: e.activation(out=ebl[:, h * 4:h * 4 + 4], in_=v3(t4)[:, :, 127], func=AF.Exp),
                          reads=[r4], writes=[R_ebl])
                    yield
                    bt = 4 + nb(0, 4)
                    btv = bank(bt).bitcast(BF16)
                    for c in range(4):
                        S.pe(lambda e, c=c, h=h, btv=btv: e.transpose(out=btv[:, ts(c, 128)], in_=kpT[:, ts(c, 128)],
                                                                      identity=cbi), reads=[R_kpT, R_cb], writes=[PB[bt]])
                    S.dve(lambda e, h=h, btv=btv: e.tensor_copy(out=kp[:, ts(h, 512)], in_=btv[:, 0:512]), reads=[PB[bt]], writes=[R_kp])
                    yield

                fill = list(fill)
                for hpair in ((0, 1), (2, 3)):
                    gens = [prep_head(h, tg) for h in hpair]
                    for _ in range(5):
                        next(gens[0])
                    rg = prev_rms
                    if prev_rms is not None:
                        gens.append(prev_rms)
                        prev_rms = None
                    step = 0
                    while gens:
                        for g_ in list(gens):
                            try:
                                next(g_)
                            except StopIteration:
                                gens.remove(g_)
                        step += 1
                        if fill and step % 4 == 0 and (rg is None or rg not in gens):
                            fill.pop(0)()
                while fill:
                    fill.pop(0)()
                for c in range(4):
                    ba = nb(0, 4)
                    for h in range(4):
                        o_ = h * 512 + c * 128
                        S.pe(lambda e, ba=ba, h=h, o_=o_: e.matmul(bank(ba)[0:64, ts(h, 128)], lhsT=kt[:, o_:o_ + 64],
                                                                   rhs=qt[:, o_:o_ + 128], start=True, stop=True),
                             reads=[R_kt, R_qt], writes=[PB[ba]])
                        S.pe(lambda e, ba=ba, h=h, c=c, o_=o_: e.matmul(bank(ba)[64:128, h * 128 + 64:h * 128 + 128], lhsT=kt[:, o_ + 64:o_ + 128],
                                                                        rhs=q2v[:, h, c, :], start=True, stop=True),
                             reads=[R_kt, R_q2], writes=[PB[ba]])
                    bs = nb(0, 4)
                    for h in range(4):
                        S.pe(lambda e, bs=bs, h=h, c=c: e.matmul(bank(bs)[:, ts(h, 128)], lhsT=kp4[:, h, c, :], rhs=Vb3[:, c, ts(h, 128)],
                                                                 start=True, stop=True), reads=[R_kp, *RVb], writes=[PB[bs]])
                    AM, R_AM = AMs[c % 2]
                    S.pool(lambda e, AM=AM: e.memset(AM, 0.0), writes=[R_AM])
                    S.dve(lambda e, ba=ba, AM=AM: e.copy_predicated(out=AM, mask=maskI_u, data=bank(ba)),
                          reads=[PB[ba], R_cb, R_AM], writes=[R_AM])
                    bo = 4 + nb(0, 4)
                    for h in range(4):
                        S.pe(lambda e, bo=bo, h=h, c=c, AM=AM: e.matmul(bank(bo)[:, ts(h, 128)], lhsT=Vb3[:, c, ts(h, 128)], rhs=AM[:, ts(h, 128)],
                                                                 start=True, stop=False), reads=[*RVb, R_AM], writes=[PB[bo]])
                        S.pe(lambda e, bo=bo, h=h, c=c: e.matmul(bank(bo)[:, ts(h, 128)], lhsT=Sb[:, ts(h, 128)],
                                                                 rhs=qp[:, h * 512 + c * 128:h * 512 + (c + 1) * 128],
                                                                 start=False, stop=True), reads=[R_Sb, R_qp], writes=[PB[bo]])
                    S.act(lambda e, bo=bo, c=c: e.copy(out=oTg.rearrange("p (h r) -> p h r", h=4)[:, :, ts(c, 128)],
                                                       in_=bank(bo).rearrange("p (h r) -> p h r", h=4)),
                          reads=[PB[bo]], writes=[R_oTg])
                    Sf3 = Sf.rearrange("p (h v) -> p h v", h=4)
                    S.dve(lambda e, c=c, Sf3=Sf3: e.tensor_tensor(
                        out=Sf3, in0=Sf3, in1=ebl.rearrange("p (h c) -> p h c", h=4)[:, :, c:c + 1].broadcast_to([128, 4, 128]),
                        op=ALU.mult), reads=[R_Sf, R_ebl], writes=[R_Sf])
                    S.dve(lambda e, bs=bs: e.tensor_tensor(out=Sf, in0=Sf, in1=bank(bs), op=ALU.add),
                          reads=[R_Sf, PB[bs]], writes=[R_Sf])
                    S.dve(lambda e: e.tensor_copy(out=Sb, in_=Sf), reads=[R_Sf], writes=[R_Sb])
                if dbg and tg == 0:
                    dump("oTg0", oTg, [R_oTg])
                def rms_gen():
                  for h in range(4):
                    sqb, R_sqb = sqbs[h % 2]
                    S.act(lambda e, h=h, sqb=sqb: e.activation(out=sqb, in_=oTg[:, ts(h, 512)], func=AF.Square), reads=[R_oTg], writes=[R_sqb])
                    yield
                    bss = 4 + nb(0, 4)
                    S.pe(lambda e, bss=bss, h=h, sqb=sqb: e.matmul(bank(bss), lhsT=cb[:, ONES, :], rhs=sqb, start=True, stop=True),
                         reads=[R_cb, R_sqb], writes=[PB[bss]])
                    yield
                    (ra, rr_) = rs[h % 2]
                    S.act(lambda e, bss=bss, ra=ra: e.activation(out=ra, in_=bank(bss), func=AF.Ln, scale=1.0 / 128, bias=EPS),
                          reads=[PB[bss]], writes=[rr_])
                    yield
                    S.act(lambda e, ra=ra: e.activation(out=ra, in_=ra, func=AF.Exp, scale=-0.5), reads=[rr_], writes=[rr_])
                    yield
                    S.dve(lambda e, ra=ra, h=h: e.tensor_tensor(out=ra, in0=ra, in1=oTg[:, ts(h, 512)], op=ALU.mult),
                          reads=[rr_, R_oTg], writes=[rr_])
                    yield
                    S.dve(lambda e, ra=ra, h=h, tg=tg, VO=VO: e.scalar_tensor_tensor(
                        out=obT[:, h, ts(tg, 512)], in0=ra, scalar=vecs[:, VO + V_HG + h:VO + V_HG + h + 1], in1=zsg[:, ts(h, 512)],
                        op0=ALU.mult, op1=ALU.mult), reads=[rr_, R_vecs, *RZs], writes=[R_ob[h][tg]])
                    yield
                return rms_gen()

        mtb = mt[:].bitcast(BF16)
        vsets = [(Vb3, [R_Vb], zsg, [R_zsg]),
                 (mtb[:, 0:2048].rearrange("p (t c) -> p t c", t=4), [R_mt[0], R_mt[1]], mtb[:, 2048:4096], [R_mt[2], R_mt[3]])]
        for u_ in vz_units(0, *vsets[0]):
            u_()
        rms_prev = None
        for tg in range(4):
            nxt = vz_units(tg + 1, *vsets[(tg + 1) % 2]) if tg < 3 else []
            rms_prev = c_block(tg, *vsets[tg % 2], nxt, rms_prev)
        for _ in rms_prev:
            pass
        for _ in range(4):
            release_group()
        fr_C = S.frontier()
        dump("obC", obT[:], [r for rr in R_ob for r in rr])
        if stop_after == "C":
            return True
        merge(2, barrier=False)
        dump("mg", mg[:], [r for rr in R_mg for r in rr])
        if stop_after == "MC":
            return True

        S.barrier(fr_C)
        AR.reset()
        xo = [AR.F(1024, f"xo{i}") for i in range(4)]
        yo = [AR.F(1024, f"yo{i}") for i in range(2)]
        sq2, R_sq2 = AR.F(1024, "sq2")
        gbuf2, R_gb2 = AR.F(1024, "gbuf2")
        st2, _ = AR.F(64, "st2")
        R_st2t = [Res(f"st2{i}") for i in range(16)]
        so0, so1 = take_group(), take_group()
        wo = [w8(so0), w8(so1)]
        rwo = [R_ws[so0], R_ws[so1]]
        final = is_last_layer and last
        fuse_next = not is_last_layer
        if final:
            S.dma("sp", ch_misc, lambda e: e.dma_start(out=gbuf2, in_=gb_d[DEPTH]), writes=[R_gb2])
        elif fuse_next:
            Lnext = layers[li + 1]
            S.dma("sp", ch_misc, lambda e: e.dma_start(out=gbuf2, in_=gb_d[Lnext]), writes=[R_gb2])
            S.dma("pool", ch_pw, lambda e: e.dma_start(out=poolw[:], in_=poolw_d[Lnext]), writes=[R_pw])

        def o_load(tt):
            xa, xr = xo[tt % 4]
            S.dma("sp", ch_x[tt % 4], lambda e: e.dma_start(out=xa, in_=src[ts(tt, 128), :]), writes=[xr])

        def o_front(tt):
            k = tt % 2
            xa, xr = xo[tt % 4]
            ya, yr = yo[k]
            if tt + 2 < 16:
                o_load(tt + 2)
            for half in range(2):
                b = nb(0, 6)
                for dc in range(8):
                    S.pe(lambda e, b=b, dc=dc, half=half: e.matmul(bank(b), lhsT=mg[:, dc, ts(tt, 128)], rhs=wo[half][:, dc, :],
                                                                  start=(dc == 0), stop=(dc == 7)),
                         reads=[R_mg[dc][tt // 4], rwo[half]], writes=[PB[b]])
                S.dve(lambda e, b=b, half=half: e.scalar_tensor_tensor(
                    out=xa[:, ts(half, 512)], in0=bank(b), scalar=0.5, in1=xa[:, ts(half, 512)], op0=ALU.mult, op1=ALU.add),
                      reads=[PB[b], xr], writes=[xr])
            if not final:
                dst = hscr if fuse_next else out_d
                S.dma("sp", ch_out, lambda e: e.dma_start(out=dst[ts(tt, 128), :], in_=xa), reads=[xr])
            if final or fuse_next:
                S.act(lambda e: e.activation(out=sq2, in_=xa, func=AF.Square, accum_out=st2[:, tt:tt + 1]),
                      reads=[xr], writes=[R_sq2, R_st2t[tt]])
                S.act(lambda e: e.activation(out=st2[:, 16 + tt:17 + tt], in_=st2[:, tt:tt + 1], func=AF.Ln,
                                             scale=1.0 / DM, bias=EPS), reads=[R_st2t[tt]], writes=[R_st2t[tt]])
                S.act(lambda e: e.activation(out=st2[:, 32 + tt:33 + tt], in_=st2[:, 16 + tt:17 + tt], func=AF.Exp,
                                             scale=-0.5), reads=[R_st2t[tt]], writes=[R_st2t[tt]])

        def o_mid(tt):
            k = tt % 2
            xa, xr = xo[tt % 4]
            ya, yr = yo[k]
            if final or fuse_next:
                S.dve(lambda e: e.scalar_tensor_tensor(out=ya, in0=xa, scalar=st2[:, 32 + tt:33 + tt],
                                                       in1=gbuf2, op0=ALU.mult, op1=ALU.mult),
                      reads=[xr, R_st2t[tt], R_gb2], writes=[yr])
            if final:
                S.dma("sp", ch_out, lambda e: e.dma_start(out=out_d[ts(tt, 128), :], in_=ya), reads=[yr])

        def o_back(tt):
            ya, yr = yo[tt % 2]
            for half in range(2):
                b = 6 + half
                for j in range(4):
                    dc = half * 4 + j
                    S.pe(lambda e, b=b, j=j, dc=dc: e.transpose(out=bank(b)[:, ts(j, 128)], in_=ya[:, ts(dc, 128)],
                                                               identity=ident),
                         reads=[yr, R_cf], writes=[PB[b]])
                S.dve(lambda e, b=b, half=half: e.tensor_copy(
                    out=hT[:, half * 4:half * 4 + 4, ts(tt, 128)],
                    in_=bank(b).rearrange("p (j t) -> p j t", j=4)), reads=[PB[b]], writes=[R_hT[tt // 4]])

        o_load(0)
        o_load(1)
        for tt in range(18):
            if tt < 16:
                o_front(tt)
            if 0 <= tt - 1 < 16:
                o_mid(tt - 1)
            if fuse_next and 0 <= tt - 2 < 16:
                o_back(tt - 2)
        release_group()
        release_group()

        return False

    for li, L in enumerate(layers):
        if layer_body(li, L):
            break

    S.emit(final_chans=[ch_out, ch_dbg])
    es.close()
    return nc


def _consts():
    c = np.zeros((128, CW), np.float32)
    j = np.arange(128)[:, None]
    q = np.arange(128)[None, :]
    c[:, 0:128] = np.eye(128, dtype=np.float32)
    c[:, 128:256] = np.where(j >= q, -8.0, 0.0)
    c[:, 256:384] = np.where(j < q, 1.0, 0.0)
    c[:, 384:512] = np.where(j <= q, 1.0, 0.0)
    rm = np.ones((512,), np.float32)
    rm[0::128] = 0.0
    c[:, 512:1024] = rm[None, :]
    c[:, 1024:1040] = (1.0 / np.arange(1, 17, dtype=np.float32))[None, :]
    return c


def _fm8(w):
    C = w.shape[1]
    return np.ascontiguousarray(w.reshape(8, 128, C).transpose(1, 0, 2)).reshape(128, 8 * C)


def _fm4(w):
    C = w.shape[1]
    return np.ascontiguousarray(w.reshape(4, 128, C).transpose(1, 0, 2)).reshape(128, 4 * C)


def _pack(norm_g, w_in, pool_w, pool_scale, hgrn_lb, hgrn_norm_g, w_branch, w_out, final_g):
    wg = np.empty((DEPTH * NGRP, 128, 4096), np.float32)
    for l in range(DEPTH):
        W = w_in[l]
        groups = []
        for hp in range(4):
            cols = np.concatenate([np.arange(hp * 128, hp * 128 + 128) + off for off in (0, 512, 1024, 1536)])
            groups.append(_fm8(W[:, cols]))
        groups.append(_fm8(W[:, 5120:5632])); groups.append(_fm8(W[:, 5632:6144]))
        groups.append(_fm4(w_branch[l, 0]))
        groups.append(_fm8(W[:, 2048:2560])); groups.append(_fm8(W[:, 2560:3072]))
        groups.append(_fm8(W[:, 6144:6656])); groups.append(_fm8(W[:, 6656:7168]))
        groups.append(_fm4(w_branch[l, 1]))
        groups.append(_fm8(W[:, 4096:4608])); groups.append(_fm8(W[:, 4608:5120]))
        groups.append(_fm8(W[:, 3584:4096])); groups.append(_fm8(W[:, 3072:3584]))
        groups.append(_fm8(W[:, 7168:7680])); groups.append(_fm8(W[:, 7680:8192]))
        groups.append(_fm4(w_branch[l, 2]))
        groups.append(_fm8(w_out[l][:, 0:512])); groups.append(_fm8(w_out[l][:, 512:1024]))
        assert len(groups) == NGRP
        for g, a in enumerate(groups):
            wg[l * NGRP + g] = a
    pw = np.ascontiguousarray(pool_w.transpose(0, 2, 1, 3)).reshape(DEPTH, 128, 512)
    vec = np.empty((128, DEPTH * 12), np.float32)
    for l in range(DEPTH):
        vec[:, l * 12 + 0:l * 12 + 4] = pool_scale[l].reshape(4, 128).T
        vec[:, l * 12 + 4:l * 12 + 8] = hgrn_lb[l].reshape(4, 128).T
        vec[:, l * 12 + 8:l * 12 + 12] = hgrn_norm_g[l].reshape(4, 128).T
    gb = np.empty((DEPTH + 1, 128, DM), np.float32)
    for l in range(DEPTH):
        gb[l] = np.broadcast_to(norm_g[l][None, :], (128, DM))
    gb[DEPTH] = np.broadcast_to(final_g[None, :], (128, DM))
    return {"wgrp": wg, "poolw": pw, "vecs": vec, "gb": gb, "cst": _consts()}


_NC_CACHE = {}


def kernel(x, norm_g, w_in, pool_w, pool_scale, hgrn_lb, hgrn_norm_g, w_branch, w_out, final_g):
    f = lambda a: np.ascontiguousarray(np.asarray(a, dtype=np.float32))
    x = f(x)
    shared = _pack(f(norm_g), f(w_in), f(pool_w), f(pool_scale), f(hgrn_lb), f(hgrn_norm_g), f(w_branch), f(w_out), f(final_g))
    if "nc" not in _NC_CACHE:
        _NC_CACHE["nc"] = build_program()
    nc = _NC_CACHE["nc"]
    in_maps = [dict(shared, x=x[b]) for b in range(NCORES)]
    res = run_bass_kernel_spmd(nc, in_maps, core_ids=list(range(NCORES)))
    return np.stack([np.asarray(r["out"], dtype=np.float32) for r in res.results], axis=0)
```

```python
import numpy as np
from contextlib import ExitStack
import concourse.bass as bass
import concourse.mybir as mybir
from concourse.bass_utils import run_bass_kernel_spmd

F32 = mybir.dt.float32
BF16 = mybir.dt.bfloat16
AF = mybir.ActivationFunctionType
ALU = mybir.AluOpType

SEQ = 2048
DM = 1024
DEPTH = 2
NCORES = 8
EPS = 1e-6
NGRP = 21
NSLOT = 5
CW = 4 * 128 + 512 + 16

ENGS = ("pe", "act", "dve", "pool", "sp")
SEM_CHUNK = 4000
DMA_CHUNK = 1000


class Res:
    __slots__ = ("name", "psum", "last_w", "readers", "strict")

    def __init__(self, name, psum=False, strict=False):
        self.name = name
        self.psum = psum
        self.last_w = None
        self.readers = []
        self.strict = strict


class Op:
    __slots__ = ("eng", "fn", "deps", "signal", "sig_idx", "chan", "chan_idx", "name")

    def __init__(self, eng, fn, chan=None, name=""):
        self.eng = eng
        self.fn = fn
        self.deps = []
        self.signal = False
        self.sig_idx = -1
        self.chan = chan
        self.chan_idx = -1
        self.name = name


class Chan:
    __slots__ = ("name", "n", "sems", "last", "serial")

    def __init__(self, name, serial=True):
        self.name = name
        self.n = 0
        self.sems = []
        self.last = None
        self.serial = serial


class Sched:
    def __init__(self, nc):
        self.nc = nc
        self.ops = {e: [] for e in ENGS}
        self.chans = []
        self.last = {e: None for e in ENGS}
        self.pending_barrier = {e: None for e in ENGS}

    def chan(self, name, serial=True):
        c = Chan(name, serial)
        self.chans.append(c)
        return c

    def add(self, eng, fn, reads=(), writes=(), chan=None, name=""):
        op = Op(eng, fn, chan, name)
        raw = set()
        other = set()
        strict = set()
        for r in reads:
            if r.last_w is not None:
                raw.add(r.last_w)
            if r.psum:
                for o in r.readers:
                    if o.eng != eng:
                        other.add(o)
            r.readers.append(op)
        for w in writes:
            if w.last_w is not None:
                other.add(w.last_w)
                if w.strict:
                    strict.add(w.last_w)
            for o in w.readers:
                if o is not op:
                    other.add(o)
                    if w.strict:
                        strict.add(o)
            w.last_w = op
            w.readers = []
        pb = self.pending_barrier[eng]
        if pb is not None:
            for o in pb:
                raw.add(o)
            self.pending_barrier[eng] = None
        if chan is not None and chan.serial and chan.last is not None:
            raw.add(chan.last)
        deps = []
        seen = set()
        for d in list(raw | other):
            if d is op:
                continue
            israw = d in raw
            if d.chan is not None and not d.chan.serial:
                d = d.chan.last
            if id(d) in seen:
                continue
            seen.add(id(d))
            if d.chan is None and d.eng == eng:
                if eng == "pe" or eng == "sp":
                    continue
                if not israw and d not in strict:
                    continue
            deps.append(d)
        op.deps = deps
        for d in deps:
            d.signal = True
        if chan is not None:
            op.chan_idx = chan.n
            chan.n += 1
            chan.last = op
        self.ops[eng].append(op)
        self.last[eng] = op
        return op

    def pe(self, fn, reads=(), writes=()):
        return self.add("pe", fn, reads, writes)

    def act(self, fn, reads=(), writes=()):
        return self.add("act", fn, reads, writes)

    def dve(self, fn, reads=(), writes=()):
        return self.add("dve", fn, reads, writes)

    def pool(self, fn, reads=(), writes=()):
        return self.add("pool", fn, reads, writes)

    def dma(self, eng, chan, fn, reads=(), writes=()):
        return self.add(eng, fn, reads, writes, chan=chan)

    def frontier(self):
        fr = [o for o in self.last.values() if o is not None]
        fr += [c.last for c in self.chans if c.last is not None]
        return fr

    def barrier(self, fr=None):
        if fr is None:
            fr = self.frontier()
        for e in ENGS:
            cur = self.pending_barrier[e]
            self.pending_barrier[e] = list(fr) + (cur if cur else [])

    def emit(self, final_chans=()):
        nc = self.nc
        with ExitStack() as es:
            eng_sems = {}
            for e in ENGS:
                j = 0
                for op in self.ops[e]:
                    if op.chan is None and op.signal:
                        op.sig_idx = j
                        j += 1
                nsem = (j + SEM_CHUNK - 1) // SEM_CHUNK
                eng_sems[e] = [es.enter_context(nc.semaphore(f"s_{e}_{i}")) for i in range(nsem)]
            for ci, c in enumerate(self.chans):
                nsem = (c.n + DMA_CHUNK - 1) // DMA_CHUNK
                c.sems = [es.enter_context(nc.semaphore(f"c_{ci}_{i}")) for i in range(nsem)]

            def sem_of(op):
                if op.chan is not None:
                    return (op.chan.sems[op.chan_idx // DMA_CHUNK],
                            16 * (op.chan_idx % DMA_CHUNK + 1))
                return (eng_sems[op.eng][op.sig_idx // SEM_CHUNK],
                        op.sig_idx % SEM_CHUNK + 1)

            block = es.enter_context(nc.Block())
            handles = {"pe": block.tensor, "act": block.scalar, "dve": block.vector,
                       "pool": block.gpsimd, "sp": block.sync}

            def make(e):
                def body(eng):
                    waited = {}
                    for op in self.ops[e]:
                        need = {}
                        for d in op.deps:
                            s, v = sem_of(d)
                            k = id(s)
                            if waited.get(k, 0) >= v:
                                continue
                            if k not in need or need[k][1] < v:
                                need[k] = (s, v)
                        for k, (s, v) in need.items():
                            eng.wait_ge(s, v)
                            waited[k] = v
                        ins = op.fn(eng)
                        if op.chan is not None:
                            s, _ = sem_of(op)
                            ins.then_inc(s, 16)
                        elif op.signal:
                            s, _ = sem_of(op)
                            ins.then_inc(s, 1)
                    if e == "sp":
                        for c in final_chans:
                            if c.last is not None:
                                s, v = sem_of(c.last)
                                eng.wait_ge(s, v)
                return body

            for e in ENGS:
                handles[e](make(e))


def ts(i, n):
    return slice(i * n, (i + 1) * n)


def build_program(layers=(0, 1), first=True, last=True, dbg=None, stop_after=None):
    nc = bass.Bass("TRN2", target_bir_lowering=False)
    x_in = nc.dram_tensor("x", [SEQ, DM], F32, kind="ExternalInput").ap()
    wgrp = nc.dram_tensor("wgrp", [DEPTH * NGRP, 128, 4096], F32, kind="ExternalInput").ap()
    poolw_d = nc.dram_tensor("poolw", [DEPTH, 128, 512], F32, kind="ExternalInput").ap()
    vecs_d = nc.dram_tensor("vecs", [128, DEPTH * 12], F32, kind="ExternalInput").ap()
    gb_d = nc.dram_tensor("gb", [DEPTH + 1, 128, DM], F32, kind="ExternalInput").ap()
    cst_d = nc.dram_tensor("cst", [128, CW], F32, kind="ExternalInput").ap()
    out_d = nc.dram_tensor("out", [SEQ, DM], F32, kind="ExternalOutput").ap()
    hscr = nc.dram_tensor("hscr", [SEQ, DM], F32, kind="Internal").ap()
    dbg_out = {}
    if dbg:
        for name, shape in dbg.items():
            dbg_out[name] = nc.dram_tensor("dbg_" + name, list(shape), F32, kind="ExternalOutput").ap()

    S = Sched(nc)
    es = ExitStack()

    def sb(name, shape, dt):
        return es.enter_context(nc.sbuf_tensor(name, shape, dt))

    cf = sb("cf", [128, CW], F32)
    cb = sb("cb", [128, 6, 128], BF16)
    mi4 = sb("mi4", [128, 512], BF16)
    vecs = sb("vecs_s", [128, DEPTH * 12], F32)
    lbv = sb("lbv", [128, DEPTH * 4], F32)
    l1m = sb("l1m", [128, DEPTH * 4], F32)
    lbt = sb("lbt", [128, 16], F32)
    hT = sb("hT", [128, 8, SEQ], BF16)
    mg = sb("mg", [128, 8, SEQ], BF16)
    obT = sb("obT", [128, 4, SEQ], BF16)
    wsl = [sb(f"wsl{i}", [128, 4096], BF16) for i in range(NSLOT)]
    poolw = sb("poolw_s", [128, 512], BF16)
    NF = 9760
    NB = 16896
    arf = sb("arf", [128, NF], F32)
    mt = sb("mt", [128, 2048], F32)
    arb = sb("arb", [128, NB], BF16)
    ps = es.enter_context(nc.psum_tensor("ps", [128, 8 * 512], F32))

    def bank(i):
        return ps[:, i * 512:(i + 1) * 512]

    PB = [Res(f"pb{i}", psum=True) for i in range(8)]
    R_cf = Res("cf"); R_cb = Res("cb"); R_vecs = Res("vecs"); R_lb = Res("lb")
    R_hT = [Res(f"hT{tg}") for tg in range(4)]
    R_mg = [[Res(f"mg{dc}_{tg}") for tg in range(4)] for dc in range(8)]
    R_ob = [[Res(f"ob{c}_{tg}") for tg in range(4)] for c in range(4)]
    R_ws = [Res(f"ws{i}") for i in range(NSLOT)]
    R_pw = Res("poolw")
    R_mt = [Res(f"mt{i}", strict=True) for i in range(4)]
    ch_ws = [S.chan(f"ws{i}") for i in range(NSLOT)]
    ch_misc = S.chan("misc")
    ch_pw = S.chan("pw")
    ch_out = S.chan("out", serial=False)
    ch_dbg = S.chan("dbg", serial=False)

    ident = cf[:, 0:128]
    TRI, ONESN8, MASKS, ONES, MASKI = 0, 1, 2, 3, 4
    rmask = cf[:, 512:1024]
    invcnt = cf[:, 1024:1040]

    class Arena:
        def __init__(self):
            self.f = 0
            self.b = 0

        def reset(self):
            self.f = 0
            self.b = 0

        def F(self, n, name):
            a = arf[:, self.f:self.f + n]
            self.f += n
            assert self.f <= NF, (name, self.f)
            return a, Res(name)

        def B(self, n, name):
            a = arb[:, self.b:self.b + n]
            self.b += n
            assert self.b <= NB, (name, self.b)
            return a, Res(name)

    AR = Arena()

    glist = [(l, g) for l in layers for g in range(NGRP)]
    gstate = {"next": 0}

    def issue_group(after=()):
        n = gstate["next"]
        if n >= len(glist):
            return
        l, g = glist[n]
        slot = n % NSLOT
        S.dma("pool", ch_ws[slot],
              lambda e, l=l, g=g, slot=slot: e.dma_start(out=wsl[slot][:], in_=wgrp[l * NGRP + g]),
              reads=list(after), writes=[R_ws[slot]])
        gstate["next"] = n + 1

    gpos = {"cur": 0}

    def take_group():
        n = gpos["cur"]
        gpos["cur"] = n + 1
        return n % NSLOT

    def release_group():
        issue_group()

    def w8(slot):
        return wsl[slot][:].rearrange("p (k c) -> p k c", k=8)

    def w4(slot):
        return wsl[slot][:].rearrange("p (k c) -> p k c", k=4)

    bank_rr = {"i": 0}

    def nb(lo=0, hi=8):
        i = bank_rr["i"]
        bank_rr["i"] = (i + 1) % (hi - lo)
        return lo + i % (hi - lo)

    def proj_fm(slot, c0, tg, b, nk=8):
        wv = w8(slot)
        for kc in range(nk):
            S.pe(lambda e, kc=kc: e.matmul(bank(b), lhsT=wv[:, kc, c0:c0 + 128], rhs=hT[:, kc, ts(tg, 512)],
                                           start=(kc == 0), stop=(kc == nk - 1)),
                 reads=[R_ws[slot], R_hT[tg]], writes=[PB[b]])

    def dump(name, ap, res):
        if dbg and name in dbg:
            S.dma("pool", ch_dbg, lambda e: e.dma_start(out=dbg_out[name], in_=ap), reads=res)

    S.dma("sp", ch_misc, lambda e: e.dma_start(out=cf[:], in_=cst_d), writes=[R_cf])
    S.dma("sp", ch_misc, lambda e: e.dma_start(out=vecs[:], in_=vecs_d), writes=[R_vecs])
    for _ in range(2):
        issue_group()
    for i, c0_ in ((0, 128), (2, 256), (4, 384)):
        S.dve(lambda e, i=i, c0_=c0_: e.tensor_copy(out=cb[:, i, :], in_=cf[:, c0_:c0_ + 128]), reads=[R_cf], writes=[R_cb])
    S.dve(lambda e: e.memset(cb[:, 1, :], -8.0), writes=[R_cb])
    S.dve(lambda e: e.memset(cb[:, 3, :], 1.0), writes=[R_cb])
    for i in range(4):
        S.dve(lambda e, i=i: e.tensor_copy(out=mi4[:, ts(i, 128)], in_=cf[:, 384:512]), reads=[R_cf], writes=[R_cb])
    S.dve(lambda e: e.tensor_copy(out=cb[:, 5, :], in_=cf[:, 0:128]), reads=[R_cf], writes=[R_cb])
    cbi = cb[:, 5, :]
    V_PS, V_LB, V_HG = 0, 4, 8
    S.dve(lambda e: e.memset(lbv[:], 0.0), writes=[R_lb])
    S.dve(lambda e: e.memset(l1m[:], 0.0), writes=[R_lb])
    S.dve(lambda e: e.tensor_tensor(out=lbt[:, 0:4], in0=vecs[:, V_LB:V_LB + 4], in1=vecs[:, 12 + V_LB:12 + V_LB + 4],
                                    op=ALU.subtract), reads=[R_vecs], writes=[R_lb])
    S.act(lambda e: e.activation(out=lbt[:, 4:8], in_=lbt[:, 0:4], func=AF.Exp), reads=[R_lb], writes=[R_lb])
    S.dve(lambda e: e.tensor_scalar(out=lbt[:, 8:12], in0=lbt[:, 4:8], scalar1=1.0, scalar2=None, op0=ALU.add), reads=[R_lb], writes=[R_lb])
    S.dve(lambda e: e.reciprocal(out=lbv[:, 4:8], in_=lbt[:, 8:12]), reads=[R_lb], writes=[R_lb])
    S.dve(lambda e: e.tensor_tensor(out=lbt[:, 12:16], in0=lbt[:, 4:8], in1=lbv[:, 4:8], op=ALU.mult),
          reads=[R_lb], writes=[R_lb])
    S.act(lambda e: e.activation(out=l1m[:, 4:8], in_=lbt[:, 12:16], func=AF.Ln), reads=[R_lb], writes=[R_lb])

    n_layers = len(layers)
    chx_box = {}

    def layer_body(li, L):
        is_first_layer = (li == 0)
        is_last_layer = (li == n_layers - 1)
        src = x_in if (is_first_layer and first) else hscr
        if is_first_layer and not first:
            src = x_in
        VO = L * 12

        mtmp = {"th": [(mt[:, 512 * i:512 * (i + 1)], R_mt[i]) for i in range(2)],
                "tm": [(mt[:, 1024 + 512 * i:1024 + 512 * (i + 1)], R_mt[2 + i]) for i in range(2)]}
        AR.reset()
        do_p0 = (li == 0)
        xt = [AR.F(1024, f"xt{i}") for i in range(2)]
        xn = [AR.F(1024, f"xn{i}") for i in range(2)]
        sq, R_sq = AR.F(1024, "sq")
        gbuf, R_gb = AR.F(1024, "gbuf")
        st, _ = AR.F(64, "st")
        R_stt = [Res(f"st{i}") for i in range(16)]
        if li == 0:
            chx_box["c"] = [S.chan("x0"), S.chan("x1"), S.chan("x2"), S.chan("x3")]
        ch_x = chx_box["c"]
        if do_p0:
            S.dma("sp", ch_misc, lambda e, L=L: e.dma_start(out=gbuf, in_=gb_d[L]), writes=[R_gb])
            S.dma("pool", ch_pw, lambda e, L=L: e.dma_start(out=poolw[:], in_=poolw_d[L]), writes=[R_pw])
        def p0_front(tt):
            k = tt % 2
            xa, xr = xt[k]
            na, nr = xn[k]
            S.dma("sp", ch_x[k], lambda e: e.dma_start(out=xa, in_=src[ts(tt, 128), :]), writes=[xr])
            S.act(lambda e: e.activation(out=sq, in_=xa, func=AF.Square, accum_out=st[:, tt:tt + 1]),
                  reads=[xr], writes=[R_sq, R_stt[tt]])
            S.act(lambda e: e.activation(out=st[:, 16 + tt:17 + tt], in_=st[:, tt:tt + 1], func=AF.Ln,
                                         scale=1.0 / DM, bias=EPS), reads=[R_stt[tt]], writes=[R_stt[tt]])
            S.act(lambda e: e.activation(out=st[:, 32 + tt:33 + tt], in_=st[:, 16 + tt:17 + tt], func=AF.Exp,
                                         scale=-0.5), reads=[R_stt[tt]], writes=[R_stt[tt]])
            S.dve(lambda e: e.scalar_tensor_tensor(out=na, in0=xa, scalar=st[:, 32 + tt:33 + tt],
                                                   in1=gbuf, op0=ALU.mult, op1=ALU.mult),
                  reads=[xr, R_stt[tt], R_gb], writes=[nr])

        def p0_back(tt):
            na, nr = xn[tt % 2]
            for half in range(2):
                b = nb(0, 4)
                for j in range(4):
                    dc = half * 4 + j
                    S.pe(lambda e, b=b, j=j, dc=dc: e.transpose(out=bank(b)[:, ts(j, 128)], in_=na[:, ts(dc, 128)],
                                                               identity=ident),
                         reads=[nr, R_cf], writes=[PB[b]])
                S.dve(lambda e, b=b, half=half: e.tensor_copy(
                    out=hT[:, half * 4:half * 4 + 4, ts(tt, 128)],
                    in_=bank(b).rearrange("p (j t) -> p j t", j=4)), reads=[PB[b]], writes=[R_hT[tt // 4]])

        for tt in range(17 if do_p0 else 0):
            if tt < 16:
                p0_front(tt)
            if tt >= 1:
                p0_back(tt - 1)
        if li == 0:
            for _ in range(NSLOT - 2):
                issue_group(after=[xt[1][1]] if do_p0 else ())
        dump("hT", hT[:], [r for r in R_hT])
        if stop_after == "p0":
            return True

        fr_pre = S.frontier()
        AR.reset()
        spb = [AR.B(1024, f"sp{i}") for i in range(3)]
        racc, R_racc = AR.B(1024, "racc")
        Ab = [AR.B(1024, f"A{i}") for i in range(3)]
        eb = [AR.F(1024, f"e{i}") for i in range(2)]
        mgflat = mg[:].rearrange("p a b -> p (a b)")
        R_mgall = [r for rr in R_mg for r in rr]
        bsets = []
        for si in range(2):
            d = {}
            for k_, nm in enumerate(("qT", "kT", "vS", "zs")):
                if si == 0:
                    d[nm], d["R_" + nm] = AR.B(2048, f"{nm}0")
                else:
                    d[nm] = mgflat[:, k_ * 2048:(k_ + 1) * 2048]
                    d["R_" + nm] = Res(f"{nm}1", strict=True)
            d["vS3"] = d["vS"].rearrange("p (t c) -> p t c", t=16)
            bsets.append(d)
        IPB = 7
        OB = 6

        def inproj_units(hp, bs, banks=(IPB,)):
            slot = take_group()
            wv = w8(slot)
            units = []
            bsel = {"i": 0}

            def nbk():
                bsel["i"] += 1
                return banks[bsel["i"] % len(banks)]

            def uq(tg, c0, dst, rdst):
                def f():
                    bk = nbk()
                    proj_fm(slot, c0, tg, bk)
                    S.dve(lambda e: e.tensor_copy(out=dst[:, ts(tg, 512)], in_=bank(bk)), reads=[PB[bk]], writes=[rdst])
                return f

            def uv(t4):
                def f():
                    bk = nbk()
                    for j in range(4):
                        tt = t4 * 4 + j
                        for kc in range(8):
                            S.pe(lambda e, j=j, tt=tt, kc=kc: e.matmul(
                                bank(bk)[:, ts(j, 128)], lhsT=hT[:, kc, ts(tt, 128)], rhs=wv[:, kc, 256:384],
                                start=(kc == 0), stop=(kc == 7)), reads=[R_ws[slot], R_hT[t4]], writes=[PB[bk]])
                    S.dve(lambda e: e.tensor_copy(out=bs["vS3"][:, t4 * 4:t4 * 4 + 4, :],
                                                  in_=bank(bk).rearrange("p (j c) -> p j c", j=4)),
                          reads=[PB[bk]], writes=[bs["R_vS"]])
                return f

            def uz():
                for tg in range(4):
                    bk = nbk()
                    proj_fm(slot, 384, tg, bk)
                    S.act(lambda e, tg=tg, bk=bk: e.activation(out=bs["zs"][:, ts(tg, 512)], in_=bank(bk), func=AF.Silu),
                          reads=[PB[bk]], writes=[bs["R_zs"]])
                release_group()

            for tg in range(4):
                units.append(uq(tg, 0, bs["qT"], bs["R_qT"]))
            for tg in range(4):
                units.append(uq(tg, 128, bs["kT"], bs["R_kT"]))
            for t4 in range(4):
                units.append(uv(t4))
            units.append(uz)
            return units

        def h3(a_):
            return a_.rearrange("p (h c) -> p h c", h=2)

        def pair(sl):
            return ps[:, (2 * sl) * 512:(2 * sl + 2) * 512].rearrange("p (h c) -> p h c", h=2)

        mask2 = cb[:, MASKS:MASKS + 1, :].broadcast_to([128, 2, 128])
        racc3 = h3(racc)
        items = [(QS, kb) for QS in range(4) for kb in range(4 * (QS + 1) - 1, -1, -1)]
        n_it = len(items)

        def attn_pair(hp, bs, filler):
            qT, kT, vS3, zs = bs["qT"], bs["kT"], bs["vS3"], bs["zs"]
            R_q, R_k, R_v, R_z = bs["R_qT"], bs["R_kT"], bs["R_vS"], bs["R_zs"]

            def meta(i):
                QS, kb = items[i]
                nkb = 4 * (QS + 1)
                j = kb - 4 * QS
                c0 = 128 * j if j >= 0 else 0
                return QS, kb, nkb, c0, (j >= 0), i % 3

            def stA(i):
                QS, kb, nkb, c0, diag, sl = meta(i)
                q0 = QS * 512
                for h in range(2):
                    hs = slice(64 * h, 64 * h + 64)
                    S.pe(lambda e, h=h, hs=hs: e.matmul(
                        bank(2 * sl + h)[:, c0:512], lhsT=kT[hs, ts(kb, 128)], rhs=qT[hs, q0 + c0:q0 + 512],
                        start=True, stop=True), reads=[R_k, R_q], writes=[PB[2 * sl + h]])

            def stB(i):
                QS, kb, nkb, c0, diag, sl = meta(i)
                ea, er = eb[i % 2]
                spa, spr = spb[i % 3]
                S.act(lambda e: e.activation(out=h3(ea)[:, :, c0:512], in_=pair(sl)[:, :, c0:512], func=AF.Exp, scale=0.125),
                      reads=[PB[2 * sl], PB[2 * sl + 1]], writes=[er])
                S.act(lambda e: e.activation(out=h3(spa)[:, :, c0:512], in_=h3(ea)[:, :, c0:512], func=AF.Ln, bias=1.0, scale=1.0),
                      reads=[er], writes=[spr])
                if diag:
                    S.dve(lambda e: e.tensor_tensor(out=h3(spa)[:, :, c0:c0 + 128], in0=h3(spa)[:, :, c0:c0 + 128],
                                                    in1=mask2, op=ALU.mult), reads=[spr, R_cb], writes=[spr])

            def stC(i):
                QS, kb, nkb, c0, diag, sl = meta(i)
                spa, spr = spb[i % 3]
                if kb == nkb - 1:
                    S.pool(lambda e: e.memset(racc, 0.0), writes=[R_racc])
                for h in range(2):
                    S.pe(lambda e, h=h: e.matmul(
                        bank(2 * sl + h)[:, c0:512], lhsT=cb[:, TRI, :], rhs=h3(spa)[:, h, c0:512], start=False, stop=True,
                        skip_group_check=True), reads=[R_cb, spr], writes=[PB[2 * sl + h]])
                    if kb < nkb - 1:
                        S.pe(lambda e, h=h: e.matmul(
                            bank(2 * sl + h)[:, c0:512], lhsT=cb[:, ONESN8, :], rhs=racc3[:, h, c0:512], start=False, stop=True,
                            skip_group_check=True), reads=[R_cb, R_racc], writes=[PB[2 * sl + h]])
                if kb > 0:
                    S.dve(lambda e: e.tensor_tensor(out=racc3[:, :, c0:512], in0=racc3[:, :, c0:512],
                                                    in1=h3(spa)[:, :, c0:512], op=ALU.add),
                          reads=[R_racc, spr], writes=[R_racc])

            def stD(i):
                QS, kb, nkb, c0, diag, sl = meta(i)
                Aa, Ar = Ab[i % 3]
                S.act(lambda e: e.activation(out=h3(Aa)[:, :, c0:512], in_=pair(sl)[:, :, c0:512], func=AF.Exp, scale=0.125),
                      reads=[PB[2 * sl], PB[2 * sl + 1]], writes=[Ar])
                if diag:
                    S.dve(lambda e: e.tensor_tensor(out=h3(Aa)[:, :, c0:c0 + 128], in0=h3(Aa)[:, :, c0:c0 + 128],
                                                    in1=mask2, op=ALU.mult), reads=[Ar, R_cb], writes=[Ar])

            def stE(i):
                QS, kb, nkb, c0, diag, sl = meta(i)
                Aa, Ar = Ab[i % 3]
                for h in range(2):
                    hs = slice(64 * h, 64 * h + 64)
                    S.pe(lambda e, h=h, hs=hs: e.matmul(
                        bank(OB)[hs, c0:512], lhsT=vS3[:, kb, hs], rhs=h3(Aa)[:, h, c0:512],
                        start=(kb == nkb - 1), stop=(kb == 0), skip_group_check=True),
                         reads=[R_v, Ar], writes=[PB[OB]])
                if kb == 0:
                    S.dve(lambda e: e.tensor_tensor(out=obT[:, hp, ts(QS, 512)], in0=bank(OB), in1=zs[:, ts(QS, 512)],
                                                    op=ALU.mult), reads=[PB[OB], R_z], writes=[R_ob[hp][QS]])

            fill = list(filler)
            for t in range(n_it + 3):
                if t < n_it:
                    stA(t)
                if 0 <= t - 1 < n_it:
                    stB(t - 1)
                if 0 <= t - 2 < n_it:
                    stC(t - 2)
                    stD(t - 2)
                if 0 <= t - 3 < n_it:
                    stE(t - 3)
                if fill and t % 2 == 1:
                    fill.pop(0)()
            while fill:
                fill.pop(0)()

        first_units = inproj_units(0, bsets[0], banks=tuple(range(8)))
        for u in first_units:
            u()
        if dbg:
            b0 = bsets[0]
            dump("qT0", b0["qT"], [b0["R_qT"]]); dump("kT0", b0["kT"], [b0["R_kT"]])
            dump("vS0", b0["vS"], [b0["R_vS"]]); dump("zs0", b0["zs"], [b0["R_zs"]])
        S.barrier(fr_pre)
        for hp in range(4):
            filler = inproj_units(hp + 1, bsets[(hp + 1) % 2]) if hp < 3 else []
            attn_pair(hp, bsets[hp % 2], filler)
        fr_A = S.frontier()
        dump("obA", obT[:], [r for rr in R_ob for r in rr])
        if stop_after == "A":
            return True

        def merge(br, barrier=True):
            if barrier:
                S.barrier()
            th, tm = mtmp["th"], mtmp["tm"]
            extra_w = [bsets[1][k_] for k_ in ("R_qT", "R_kT", "R_vS", "R_zs")] if br == 0 else []
            gslots = [take_group(), take_group()]
            wslot = take_group()
            wbv = w4(wslot)
            k = 0
            for dc in range(8):
                gs = gslots[dc // 4]
                for tg in range(4):
                    gbk = nb(0, 4)
                    proj_fm(gs, (dc % 4) * 128, tg, gbk)
                    pbk = 4 + nb(0, 4)
                    for kc in range(4):
                        S.pe(lambda e, pbk=pbk, kc=kc, dc=dc, tg=tg: e.matmul(
                            bank(pbk), lhsT=wbv[:, kc, ts(dc, 128)], rhs=obT[:, kc, ts(tg, 512)],
                            start=(kc == 0), stop=(kc == 3)), reads=[R_ws[wslot], R_ob[kc][tg]], writes=[PB[pbk]])
                    (tha, thr) = th[k % 2]
                    (tma, tmr) = tm[k % 2]
                    k += 1
                    S.act(lambda e, gbk=gbk, tha=tha: e.activation(out=tha, in_=bank(gbk), func=AF.Tanh, scale=0.5),
                          reads=[PB[gbk]], writes=[thr])
                    if br == 0:
                        S.dve(lambda e, pbk=pbk, tha=tha, dc=dc, tg=tg: e.scalar_tensor_tensor(
                            out=mg[:, dc, ts(tg, 512)], in0=tha, scalar=1.0, in1=bank(pbk), op0=ALU.add, op1=ALU.mult),
                              reads=[thr, PB[pbk]], writes=[R_mg[dc][tg]] + extra_w)
                    else:
                        S.dve(lambda e, pbk=pbk, tha=tha, tma=tma: e.scalar_tensor_tensor(
                            out=tma, in0=tha, scalar=1.0, in1=bank(pbk), op0=ALU.add, op1=ALU.mult),
                              reads=[thr, PB[pbk]], writes=[tmr])
                        S.pool(lambda e, tma=tma, dc=dc, tg=tg: e.tensor_tensor(
                            out=mg[:, dc, ts(tg, 512)], in0=mg[:, dc, ts(tg, 512)], in1=tma, op=ALU.add),
                               reads=[tmr, R_mg[dc][tg]], writes=[R_mg[dc][tg]])
                if dc == 3:
                    release_group()
            release_group()
            release_group()

        merge(0, barrier=False)
        dump("mgA", mg[:], [r for rr in R_mg for r in rr])
        if stop_after == "MA":
            return True

        S.barrier(fr_A)
        AR.reset()
        ubs = [AR.F(2064, f"ub{i}") for i in range(2)]
        pa, R_pa = AR.F(2064, "pa")
        pb_, R_pbb = AR.F(2064, "pb")
        t16, R_t16 = AR.F(16, "t16")
        dTs = [AR.B(2048, f"dT{i}") for i in range(2)]
        pzss = [AR.B(2048, f"pzs{i}") for i in range(2)]
        su = take_group()
        sz = take_group()
        for ub_, rub_ in ubs:
            S.dve(lambda e, ub_=ub_: e.memset(ub_[:, 0:16], 0.0), writes=[rub_])
        S.dve(lambda e: e.memset(pa[:, 0:16], 0.0), writes=[R_pa])
        S.dve(lambda e: e.memset(pb_[:, 0:16], 0.0), writes=[R_pbb])

        def b_inproj(g):
            ub, R_ub = ubs[g % 2]
            pzs, R_pzs = pzss[g % 2]
            for tg in range(4):
                b = nb(0, 4)
                proj_fm(su, g * 128, tg, b)
                S.act(lambda e, b=b, tg=tg: e.copy(out=ub[:, 16 + tg * 512:16 + (tg + 1) * 512], in_=bank(b)),
                      reads=[PB[b]], writes=[R_ub])
                b = nb(0, 4)
                proj_fm(sz, g * 128, tg, b)
                S.act(lambda e, b=b, tg=tg: e.activation(out=pzs[:, ts(tg, 512)], in_=bank(b), func=AF.Silu),
                      reads=[PB[b]], writes=[R_pzs])

        def b_chain(g):
            w = 2 << g
            ub, R_ub = ubs[g % 2]
            dT, R_dT = dTs[g % 2]
            cur, rcur = ub, R_ub
            bufs = [(pa, R_pa), (pb_, R_pbb)]
            sh = 1
            bi = 0
            while sh < w:
                dst, rdst = bufs[bi % 2]
                bi += 1
                S.dve(lambda e, cur=cur, dst=dst, sh=sh: e.tensor_tensor(out=dst[:, 16:2064], in0=cur[:, 16:2064],
                                                                        in1=cur[:, 16 - sh:2064 - sh], op=ALU.add),
                      reads=[rcur], writes=[rdst])
                cur, rcur = dst, rdst
                sh *= 2
            S.dve(lambda e, cur=cur: e.scalar_tensor_tensor(out=dT, in0=cur[:, 16:2064], scalar=1.0 / w,
                                                           in1=ub[:, 16:2064], op0=ALU.mult, op1=ALU.subtract),
                  reads=[rcur, R_ub], writes=[R_dT])
            wn = min(w, 16)
            S.dve(lambda e, cur=cur: e.tensor_tensor(out=t16[:, 0:wn], in0=cur[:, 16:16 + wn], in1=invcnt[:, 0:wn],
                                                    op=ALU.mult), reads=[rcur, R_cf], writes=[R_t16])
            S.dve(lambda e: e.tensor_tensor(out=dT[:, 0:wn], in0=t16[:, 0:wn], in1=ub[:, 16:16 + wn],
                                            op=ALU.subtract), reads=[R_t16, R_ub], writes=[R_dT])

        def b_out(g):
            dT, R_dT = dTs[g % 2]
            pzs, R_pzs = pzss[g % 2]
            for tg in range(4):
                b = 4 + nb(0, 4)
                S.pe(lambda e, b=b, tg=tg: e.matmul(bank(b), lhsT=poolw[:, ts(g, 128)], rhs=dT[:, ts(tg, 512)],
                                                    start=True, stop=True), reads=[R_pw, R_dT], writes=[PB[b]])
                S.dve(lambda e, b=b, tg=tg: e.scalar_tensor_tensor(
                    out=obT[:, g, ts(tg, 512)], in0=bank(b), scalar=vecs[:, VO + V_PS + g:VO + V_PS + g + 1],
                    in1=pzs[:, ts(tg, 512)], op0=ALU.mult, op1=ALU.mult),
                      reads=[PB[b], R_vecs, R_pzs], writes=[R_ob[g][tg]])

        b_inproj(0)
        for g in range(4):
            if g < 3:
                b_inproj(g + 1)
            b_chain(g)
            b_out(g)
        release_group()
        release_group()
        fr_B = S.frontier()
        dump("obB", obT[:], [r for rr in R_ob for r in rr])
        if stop_after == "B":
            return True
        merge(1, barrier=False)
        if stop_after == "MB":
            return True

        S.barrier(fr_B)
        AR.reset()
        Tsets = [[AR.F(512, f"T{k}_{i}") for i in range(6)] for k in range(2)]
        oTg, R_oTg = AR.F(2048, "oTg")
        rs = [AR.F(512, f"rs{i}") for i in range(2)]
        Sf, R_Sf = AR.F(512, "Sf")
        ebl, R_ebl = AR.F(16, "ebl")
        qt, R_qt = AR.B(2048, "qt")
        kt, R_kt = AR.B(2048, "kt")
        qp, R_qp = AR.B(2048, "qp")
        kpTs = [AR.B(512, f"kpT{i}") for i in range(2)]
        kp, R_kp = AR.B(2048, "kp")
        Vb, R_Vb = AR.B(2048, "Vb")
        q2, R_q2 = AR.B(1024, "q2")
        q2v = q2.rearrange("p (h c j) -> p h c j", h=4, c=4)
        v8 = lambda a: a.rearrange("p (c j) -> p c j", c=8)
        zsg, R_zsg = AR.B(2048, "zsg")
        sqbs = [AR.B(512, f"sqb{i}") for i in range(2)]
        Sb, R_Sb = AR.B(512, "Sb")
        AMs = [AR.B(512, f"AM{i}") for i in range(2)]
        maskI_u = mi4[:].bitcast(mybir.dt.uint16)
        si_, szc, sf_, sq_ = take_group(), take_group(), take_group(), take_group()
        Vb3 = Vb.rearrange("p (t c) -> p t c", t=4)
        kp4 = kp.rearrange("p (h c d) -> p h c d", h=4, c=4)
        S.dve(lambda e: e.memset(Sf, 0.0), writes=[R_Sf])
        S.dve(lambda e: e.memset(Sb, 0.0), writes=[R_Sb])
        v3 = lambda a: a.rearrange("p (c j) -> p c j", c=4)
        def vz_units(tg, Vb3, RVb, zsg, RZs):
            units = []

            def uv(j):
                def f():
                    tt = tg * 4 + j
                    b = 4 + nb(0, 4)
                    wv = w8(si_)
                    for kc in range(8):
                        S.pe(lambda e, kc=kc: e.matmul(bank(b), lhsT=hT[:, kc, ts(tt, 128)], rhs=wv[:, kc, :],
                                                       start=(kc == 0), stop=(kc == 7)),
                             reads=[R_ws[si_], R_hT[tg]], writes=[PB[b]])
                    S.dve(lambda e: e.tensor_copy(out=Vb3[:, j, :], in_=bank(b)), reads=[PB[b]], writes=list(RVb))
                return f

            def uz():
                for h in range(4):
                    bz = 4 + nb(0, 4)
                    proj_fm(szc, h * 128, tg, bz)
                    S.act(lambda e, bz=bz, h=h: e.activation(out=zsg[:, ts(h, 512)], in_=bank(bz), func=AF.Silu),
                          reads=[PB[bz]], writes=list(RZs))

            for j in range(4):
                units.append(uv(j))
            units.append(uz)
            return units

        def c_block(tg, Vb3, RVb, zsg, RZs, fill, prev_rms):
                def prep_head(h, tg):
                    hsl = ts(h, 512)
                    lb_ap = lbv[:, L * 4 + h:L * 4 + h + 1]
                    l1_ap = l1m[:, L * 4 + h:L * 4 + h + 1]
                    bx = nb(0, 4)
                    proj_fm(sf_, h * 128, tg, bx)
                    yield
                    bq = nb(0, 4)
                    proj_fm(sq_, h * 128, tg, bq)
                    yield
                    (t0, r0), (t1, r1), (t2, r2), (t3, r3), (t4, r4), (t5, r5) = Tsets[h % 2]
                    kpT, R_kpT = kpTs[h % 2]
                    S.act(lambda e, bx=bx: e.activation(out=t0, in_=bank(bx), func=AF.Exp, scale=-1.0), reads=[PB[bx]], writes=[r0])
                    yield
                    S.act(lambda e: e.activation(out=t1, in_=t0, func=AF.Ln, bias=1.0, scale=1.0), reads=[r0], writes=[r1])
                    yield
                    S.act(lambda e, lb_ap=lb_ap: e.activation(out=t2, in_=t0, func=AF.Ln, bias=1.0, scale=lb_ap),
                          reads=[r0, R_lb], writes=[r2])
                    yield
                    S.dve(lambda e, bx=bx: e.tensor_tensor(out=t3, in0=bank(bx), in1=t1, op=ALU.add), reads=[PB[bx], r1], writes=[r3])
                    yield
                    S.dve(lambda e: e.tensor_tensor(out=t2, in0=t2, in1=t1, op=ALU.subtract), reads=[r2, r1], writes=[r2])
                    yield
                    S.dve(lambda e: e.tensor_tensor_scan(out=t4, data0=rmask, data1=t2, initial=0.0, op0=ALU.mult, op1=ALU.add),
                          reads=[R_cf, r2], writes=[r4])
                    yield
                    S.dve(lambda e: e.tensor_tensor(out=v3(t5), in0=v3(t4), in1=v3(t4)[:, :, 63:64].broadcast_to([128, 4, 128]),
                                                     op=ALU.subtract), reads=[r4], writes=[r5])
                    yield
                    S.act(lambda e: e.activation(out=t0, in_=t5, func=AF.Exp), reads=[r5], writes=[r0])
                    yield
                    S.dve(lambda e, bq=bq, hsl=hsl: e.tensor_tensor(out=qt[:, hsl], in0=bank(bq), in1=t0, op=ALU.mult),
                          reads=[PB[bq], r0], writes=[R_qt])
                    yield
                    S.pool(lambda e: e.tensor_tensor(out=v3(t5)[:, :, 64:128], in0=v3(t4)[:, :, 64:128],
                                                     in1=v3(t4)[:, :, 127:128].broadcast_to([128, 4, 64]), op=ALU.subtract),
                           reads=[r4, r5], writes=[r5])
                    yield
                    S.act(lambda e: e.activation(out=v3(t0)[:, :, 64:128], in_=v3(t5)[:, :, 64:128], func=AF.Exp),
                          reads=[r5, r0], writes=[r0])
                    yield
                    S.dve(lambda e, bq=bq, h=h: e.tensor_tensor(out=q2v[:, h, :, :], in0=bank(bq).rearrange("p (c j) -> p c j", c=4)[:, :, 64:128],
                                                                in1=v3(t0)[:, :, 64:128], op=ALU.mult),
                          reads=[PB[bq], r0], writes=[R_q2])
                    yield
                    S.pool(lambda e: e.tensor_tensor(out=t3, in0=t4, in1=t3, op=ALU.add), reads=[r4, r3], writes=[r3])
                    yield
                    S.pool(lambda e: e.tensor_tensor(out=v8(t1), in0=v8(t4)[:, :, 63:64].broadcast_to([128, 8, 64]), in1=v8(t3),
                                                     op=ALU.subtract), reads=[r4, r3], writes=[r1])
                    yield
                    S.act(lambda e, hsl=hsl, l1_ap=l1_ap: e.activation(out=kt[:, hsl], in_=t1, func=AF.Exp, bias=l1_ap, scale=1.0),
                          reads=[r1, R_lb], writes=[R_kt])
                    yield
                    S.act(lambda e: e.activation(out=t0, in_=t4, func=AF.Exp), reads=[r4], writes=[r0])
                    yield
                    S.dve(lambda e, bq=bq, hsl=hsl: e.tensor_tensor(out=qp[:, hsl], in0=bank(bq), in1=t0, op=ALU.mult),
                          reads=[PB[bq], r0], writes=[R_qp])
                    yield
                    S.pool(lambda e: e.tensor_tensor(out=v3(t2), in0=v3(t4)[:, :, 127:128].broadcast_to([128, 4, 128]), in1=v3(t3),
                                                     op=ALU.subtract), reads=[r4, r3], writes=[r2])
                    yield
                    S.act(lambda e, l1_ap=l1_ap: e.activation(out=kpT, in_=t2, func=AF.Exp, bias=l1_ap, scale=1.0),
                          reads=[r2, R_lb], writes=[R_kpT])
                    yield
                    S.act(lambda e, h=h: e.activation(out=ebl[:, h * 4:h * 4 + 4], in_=v3(t4)[:, :, 127], func=AF.Exp),
                          reads=[r4], writes=[R_ebl])
                    yield
                    bt = 4 + nb(0, 4)
                    btv = bank(bt).bitcast(BF16)
                    for c in range(4):
                        S.pe(lambda e, c=c, h=h, btv=btv: e.transpose(out=btv[:, ts(c, 128)], in_=kpT[:, ts(c, 128)],
                                                                      identity=cbi), reads=[R_kpT, R_cb], writes=[PB[bt]])
                    S.dve(lambda e, h=h, btv=btv: e.tensor_copy(out=kp[:, ts(h, 512)], in_=btv[:, 0:512]), reads=[PB[bt]], writes=[R_kp])
                    yield

                fill = list(fill)
                for hpair in ((0, 1), (2, 3)):
                    gens = [prep_head(h, tg) for h in hpair]
                    for _ in range(9):
                        next(gens[0])
                    rg = prev_rms
                    if prev_rms is not None:
                        gens.append(prev_rms)
                        prev_rms = None
                    step = 0
                    while gens:
                        for g_ in list(gens):
                            try:
                                next(g_)
                            except StopIteration:
                                gens.remove(g_)
                        step += 1
                        if fill and step % 4 == 0 and (rg is None or rg not in gens):
                            fill.pop(0)()
                while fill:
                    fill.pop(0)()
                for c in range(4):
                    ba = nb(0, 4)
                    for h in range(4):
                        o_ = h * 512 + c * 128
                        S.pe(lambda e, ba=ba, h=h, o_=o_: e.matmul(bank(ba)[0:64, ts(h, 128)], lhsT=kt[:, o_:o_ + 64],
                                                                   rhs=qt[:, o_:o_ + 128], start=True, stop=True),
                             reads=[R_kt, R_qt], writes=[PB[ba]])
                        S.pe(lambda e, ba=ba, h=h, c=c, o_=o_: e.matmul(bank(ba)[64:128, h * 128 + 64:h * 128 + 128], lhsT=kt[:, o_ + 64:o_ + 128],
                                                                        rhs=q2v[:, h, c, :], start=True, stop=True),
                             reads=[R_kt, R_q2], writes=[PB[ba]])
                    bs = nb(0, 4)
                    for h in range(4):
                        S.pe(lambda e, bs=bs, h=h, c=c: e.matmul(bank(bs)[:, ts(h, 128)], lhsT=kp4[:, h, c, :], rhs=Vb3[:, c, ts(h, 128)],
                                                                 start=True, stop=True), reads=[R_kp, *RVb], writes=[PB[bs]])
                    AM, R_AM = AMs[c % 2]
                    S.pool(lambda e, AM=AM: e.memset(AM, 0.0), writes=[R_AM])
                    S.dve(lambda e, ba=ba, AM=AM: e.copy_predicated(out=AM, mask=maskI_u, data=bank(ba)),
                          reads=[PB[ba], R_cb, R_AM], writes=[R_AM])
                    bo = 4 + nb(0, 4)
                    for h in range(4):
                        S.pe(lambda e, bo=bo, h=h, c=c, AM=AM: e.matmul(bank(bo)[:, ts(h, 128)], lhsT=Vb3[:, c, ts(h, 128)], rhs=AM[:, ts(h, 128)],
                                                                 start=True, stop=False), reads=[*RVb, R_AM], writes=[PB[bo]])
                        S.pe(lambda e, bo=bo, h=h, c=c: e.matmul(bank(bo)[:, ts(h, 128)], lhsT=Sb[:, ts(h, 128)],
                                                                 rhs=qp[:, h * 512 + c * 128:h * 512 + (c + 1) * 128],
                                                                 start=False, stop=True), reads=[R_Sb, R_qp], writes=[PB[bo]])
                    S.act(lambda e, bo=bo, c=c: e.copy(out=oTg.rearrange("p (h r) -> p h r", h=4)[:, :, ts(c, 128)],
                                                       in_=bank(bo).rearrange("p (h r) -> p h r", h=4)),
                          reads=[PB[bo]], writes=[R_oTg])
                    Sf3 = Sf.rearrange("p (h v) -> p h v", h=4)
                    S.dve(lambda e, c=c, Sf3=Sf3: e.tensor_tensor(
                        out=Sf3, in0=Sf3, in1=ebl.rearrange("p (h c) -> p h c", h=4)[:, :, c:c + 1].broadcast_to([128, 4, 128]),
                        op=ALU.mult), reads=[R_Sf, R_ebl], writes=[R_Sf])
                    S.dve(lambda e, bs=bs: e.tensor_tensor(out=Sf, in0=Sf, in1=bank(bs), op=ALU.add),
                          reads=[R_Sf, PB[bs]], writes=[R_Sf])
                    S.dve(lambda e: e.tensor_copy(out=Sb, in_=Sf), reads=[R_Sf], writes=[R_Sb])
                if dbg and tg == 0:
                    dump("oTg0", oTg, [R_oTg])
                def rms_gen():
                  for h in range(4):
                    sqb, R_sqb = sqbs[h % 2]
                    S.act(lambda e, h=h, sqb=sqb: e.activation(out=sqb, in_=oTg[:, ts(h, 512)], func=AF.Square), reads=[R_oTg], writes=[R_sqb])
                    yield
                    bss = 4 + nb(0, 4)
                    S.pe(lambda e, bss=bss, h=h, sqb=sqb: e.matmul(bank(bss), lhsT=cb[:, ONES, :], rhs=sqb, start=True, stop=True),
                         reads=[R_cb, R_sqb], writes=[PB[bss]])
                    yield
                    (ra, rr_) = rs[h % 2]
                    S.act(lambda e, bss=bss, ra=ra: e.activation(out=ra, in_=bank(bss), func=AF.Ln, scale=1.0 / 128, bias=EPS),
                          reads=[PB[bss]], writes=[rr_])
                    yield
                    S.act(lambda e, ra=ra: e.activation(out=ra, in_=ra, func=AF.Exp, scale=-0.5), reads=[rr_], writes=[rr_])
                    yield
                    S.dve(lambda e, ra=ra, h=h: e.tensor_tensor(out=ra, in0=ra, in1=oTg[:, ts(h, 512)], op=ALU.mult),
                          reads=[rr_, R_oTg], writes=[rr_])
                    yield
                    S.dve(lambda e, ra=ra, h=h, tg=tg, VO=VO: e.scalar_tensor_tensor(
                        out=obT[:, h, ts(tg, 512)], in0=ra, scalar=vecs[:, VO + V_HG + h:VO + V_HG + h + 1], in1=zsg[:, ts(h, 512)],
                        op0=ALU.mult, op1=ALU.mult), reads=[rr_, R_vecs, *RZs], writes=[R_ob[h][tg]])
                    yield
                return rms_gen()

        mtb = mt[:].bitcast(BF16)
        vsets = [(Vb3, [R_Vb], zsg, [R_zsg]),
                 (mtb[:, 0:2048].rearrange("p (t c) -> p t c", t=4), [R_mt[0], R_mt[1]], mtb[:, 2048:4096], [R_mt[2], R_mt[3]])]
        for u_ in vz_units(0, *vsets[0]):
            u_()
        rms_prev = None
        for tg in range(4):
            nxt = vz_units(tg + 1, *vsets[(tg + 1) % 2]) if tg < 3 else []
            rms_prev = c_block(tg, *vsets[tg % 2], nxt, rms_prev)
        for _ in rms_prev:
            pass
        for _ in range(4):
            release_group()
        fr_C = S.frontier()
        dump("obC", obT[:], [r for rr in R_ob for r in rr])
        if stop_after == "C":
            return True
        merge(2, barrier=False)
        dump("mg", mg[:], [r for rr in R_mg for r in rr])
        if stop_after == "MC":
            return True

        S.barrier(fr_C)
        AR.reset()
        xo = [AR.F(1024, f"xo{i}") for i in range(4)]
        yo = [AR.F(1024, f"yo{i}") for i in range(2)]
        sq2, R_sq2 = AR.F(1024, "sq2")
        gbuf2, R_gb2 = AR.F(1024, "gbuf2")
        st2, _ = AR.F(64, "st2")
        R_st2t = [Res(f"st2{i}") for i in range(16)]
        so0, so1 = take_group(), take_group()
        wo = [w8(so0), w8(so1)]
        rwo = [R_ws[so0], R_ws[so1]]
        final = is_last_layer and last
        fuse_next = not is_last_layer
        if final:
            S.dma("sp", ch_misc, lambda e: e.dma_start(out=gbuf2, in_=gb_d[DEPTH]), writes=[R_gb2])
        elif fuse_next:
            Lnext = layers[li + 1]
            S.dma("sp", ch_misc, lambda e: e.dma_start(out=gbuf2, in_=gb_d[Lnext]), writes=[R_gb2])
            S.dma("pool", ch_pw, lambda e: e.dma_start(out=poolw[:], in_=poolw_d[Lnext]), writes=[R_pw])

        def o_load(tt):
            xa, xr = xo[tt % 4]
            S.dma("sp", ch_x[tt % 4], lambda e: e.dma_start(out=xa, in_=src[ts(tt, 128), :]), writes=[xr])

        def o_front(tt):
            k = tt % 2
            xa, xr = xo[tt % 4]
            ya, yr = yo[k]
            if tt + 2 < 16:
                o_load(tt + 2)
            for half in range(2):
                b = nb(0, 6)
                for dc in range(8):
                    S.pe(lambda e, b=b, dc=dc, half=half: e.matmul(bank(b), lhsT=mg[:, dc, ts(tt, 128)], rhs=wo[half][:, dc, :],
                                                                  start=(dc == 0), stop=(dc == 7)),
                         reads=[R_mg[dc][tt // 4], rwo[half]], writes=[PB[b]])
                S.dve(lambda e, b=b, half=half: e.scalar_tensor_tensor(
                    out=xa[:, ts(half, 512)], in0=bank(b), scalar=0.5, in1=xa[:, ts(half, 512)], op0=ALU.mult, op1=ALU.add),
                      reads=[PB[b], xr], writes=[xr])
            if not final:
                dst = hscr if fuse_next else out_d
                S.dma("sp", ch_out, lambda e: e.dma_start(out=dst[ts(tt, 128), :], in_=xa), reads=[xr])
            if final or fuse_next:
                S.act(lambda e: e.activation(out=sq2, in_=xa, func=AF.Square, accum_out=st2[:, tt:tt + 1]),
                      reads=[xr], writes=[R_sq2, R_st2t[tt]])
                S.act(lambda e: e.activation(out=st2[:, 16 + tt:17 + tt], in_=st2[:, tt:tt + 1], func=AF.Ln,
                                             scale=1.0 / DM, bias=EPS), reads=[R_st2t[tt]], writes=[R_st2t[tt]])
                S.act(lambda e: e.activation(out=st2[:, 32 + tt:33 + tt], in_=st2[:, 16 + tt:17 + tt], func=AF.Exp,
                                             scale=-0.5), reads=[R_st2t[tt]], writes=[R_st2t[tt]])

        def o_mid(tt):
            k = tt % 2
            xa, xr = xo[tt % 4]
            ya, yr = yo[k]
            if final or fuse_next:
                S.dve(lambda e: e.scalar_tensor_tensor(out=ya, in0=xa, scalar=st2[:, 32 + tt:33 + tt],
                                                       in1=gbuf2, op0=ALU.mult, op1=ALU.mult),
                      reads=[xr, R_st2t[tt], R_gb2], writes=[yr])
            if final:
                S.dma("sp", ch_out, lambda e: e.dma_start(out=out_d[ts(tt, 128), :], in_=ya), reads=[yr])

        def o_back(tt):
            ya, yr = yo[tt % 2]
            for half in range(2):
                b = 6 + half
                for j in range(4):
                    dc = half * 4 + j
                    S.pe(lambda e, b=b, j=j, dc=dc: e.transpose(out=bank(b)[:, ts(j, 128)], in_=ya[:, ts(dc, 128)],
                                                               identity=ident),
                         reads=[yr, R_cf], writes=[PB[b]])
                S.dve(lambda e, b=b, half=half: e.tensor_copy(
                    out=hT[:, half * 4:half * 4 + 4, ts(tt, 128)],
                    in_=bank(b).rearrange("p (j t) -> p j t", j=4)), reads=[PB[b]], writes=[R_hT[tt // 4]])

        o_load(0)
        o_load(1)
        for tt in range(18):
            if tt < 16:
                o_front(tt)
            if 0 <= tt - 1 < 16:
                o_mid(tt - 1)
            if fuse_next and 0 <= tt - 2 < 16:
                o_back(tt - 2)
        release_group()
        release_group()

        return False

    for li, L in enumerate(layers):
        if layer_body(li, L):
            break

    S.emit(final_chans=[ch_out, ch_dbg])
    es.close()
    return nc


def _consts():
    c = np.zeros((128, CW), np.float32)
    j = np.arange(128)[:, None]
    q = np.arange(128)[None, :]
    c[:, 0:128] = np.eye(128, dtype=np.float32)
    c[:, 128:256] = np.where(j >= q, -8.0, 0.0)
    c[:, 256:384] = np.where(j < q, 1.0, 0.0)
    c[:, 384:512] = np.where(j <= q, 1.0, 0.0)
    rm = np.ones((512,), np.float32)
    rm[0::128] = 0.0
    c[:, 512:1024] = rm[None, :]
    c[:, 1024:1040] = (1.0 / np.arange(1, 17, dtype=np.float32))[None, :]
    return c


def _fm8(w):
    C = w.shape[1]
    return np.ascontiguousarray(w.reshape(8, 128, C).transpose(1, 0, 2)).reshape(128, 8 * C)


def _fm4(w):
    C = w.shape[1]
    return np.ascontiguousarray(w.reshape(4, 128, C).transpose(1, 0, 2)).reshape(128, 4 * C)


def _pack(norm_g, w_in, pool_w, pool_scale, hgrn_lb, hgrn_norm_g, w_branch, w_out, final_g):
    wg = np.empty((DEPTH * NGRP, 128, 4096), np.float32)
    for l in range(DEPTH):
        W = w_in[l]
        groups = []
        for hp in range(4):
            cols = np.concatenate([np.arange(hp * 128, hp * 128 + 128) + off for off in (0, 512, 1024, 1536)])
            groups.append(_fm8(W[:, cols]))
        groups.append(_fm8(W[:, 5120:5632])); groups.append(_fm8(W[:, 5632:6144]))
        groups.append(_fm4(w_branch[l, 0]))
        groups.append(_fm8(W[:, 2048:2560])); groups.append(_fm8(W[:, 2560:3072]))
        groups.append(_fm8(W[:, 6144:6656])); groups.append(_fm8(W[:, 6656:7168]))
        groups.append(_fm4(w_branch[l, 1]))
        groups.append(_fm8(W[:, 4096:4608])); groups.append(_fm8(W[:, 4608:5120]))
        groups.append(_fm8(W[:, 3584:4096])); groups.append(_fm8(W[:, 3072:3584]))
        groups.append(_fm8(W[:, 7168:7680])); groups.append(_fm8(W[:, 7680:8192]))
        groups.append(_fm4(w_branch[l, 2]))
        groups.append(_fm8(w_out[l][:, 0:512])); groups.append(_fm8(w_out[l][:, 512:1024]))
        assert len(groups) == NGRP
        for g, a in enumerate(groups):
            wg[l * NGRP + g] = a
    pw = np.ascontiguousarray(pool_w.transpose(0, 2, 1, 3)).reshape(DEPTH, 128, 512)
    vec = np.empty((128, DEPTH * 12), np.float32)
    for l in range(DEPTH):
        vec[:, l * 12 + 0:l * 12 + 4] = pool_scale[l].reshape(4, 128).T
        vec[:, l * 12 + 4:l * 12 + 8] = hgrn_lb[l].reshape(4, 128).T
        vec[:, l * 12 + 8:l * 12 + 12] = hgrn_norm_g[l].reshape(4, 128).T
    gb = np.empty((DEPTH + 1, 128, DM), np.float32)
    for l in range(DEPTH):
        gb[l] = np.broadcast_to(norm_g[l][None, :], (128, DM))
    gb[DEPTH] = np.broadcast_to(final_g[None, :], (128, DM))
    return {"wgrp": wg, "poolw": pw, "vecs": vec, "gb": gb, "cst": _consts()}


_NC_CACHE = {}


def kernel(x, norm_g, w_in, pool_w, pool_scale, hgrn_lb, hgrn_norm_g, w_branch, w_out, final_g):
    f = lambda a: np.ascontiguousarray(np.asarray(a, dtype=np.float32))
    x = f(x)
    shared = _pack(f(norm_g), f(w_in), f(pool_w), f(pool_scale), f(hgrn_lb), f(hgrn_norm_g), f(w_branch), f(w_out), f(final_g))
    if "nc" not in _NC_CACHE:
        _NC_CACHE["nc"] = build_program()
    nc = _NC_CACHE["nc"]
    in_maps = [dict(shared, x=x[b]) for b in range(NCORES)]
    res = run_bass_kernel_spmd(nc, in_maps, core_ids=list(range(NCORES)))
    return np.stack([np.asarray(r["out"], dtype=np.float32) for r in res.results], axis=0)
```
